# Optimizing a Trainium2 kernel written in Bass

```python
import math
import jax, jax.numpy as jnp
from jax import lax
import numpy as np

D_MODEL = 2048
BATCH = 4
SEQ = 2048
DEPTH = 4
DEC_BATCH = 8
DEC_SEQ = 1
PAST_LEN = 16384
PAGE_SIZE = 128

N_MIXERS = 4
N_HEADS = 16
HEAD_DIM = 128
N_KV_HEADS = 4
Q_PER_KV = N_HEADS // N_KV_HEADS
Q_DIM = N_HEADS * HEAD_DIM
KV_DIM = N_KV_HEADS * HEAD_DIM
Q_BLOCK = 128
MOBA_BLOCK = 256
MOBA_TOPK = 3
MOBA_Q_CHUNK = 16
SSM_D_INNER = 2 * D_MODEL
SSM_HEAD_DIM = 64
SSM_HEADS = SSM_D_INNER // SSM_HEAD_DIM
SSM_GROUPS = 8
SSM_HPG = SSM_HEADS // SSM_GROUPS
SSM_D_STATE = 128
SSM_CONV = 4
SSM_CHUNK = 128
SSM_CONV_DIM = SSM_D_INNER + 2 * SSM_GROUPS * SSM_D_STATE
SSM_IN_DIM = SSM_D_INNER + SSM_CONV_DIM + SSM_HEADS
CONF_WIDTH = 31
D_FF = 5632
PLE_DIM = 256
LN_EPS = 1e-5
DN_ALPHA = (2 * DEPTH) ** 0.25
DN_BETA = (8 * DEPTH) ** -0.25
N_SB = (DEPTH + 3) // 4
N_SSM = (DEPTH + 2) // 4
N_CONF = (DEPTH + 1) // 4
N_MOBA = DEPTH // 4

kernel_name = 'hybrid_sb_ssd_conformer_moba_step'


def layer_norm(x, g, b):
    xf = x.astype(jnp.float32)
    mu = jnp.mean(xf, axis=-1, keepdims=True)
    var = jnp.mean(jnp.square(xf - mu), axis=-1, keepdims=True)
    return ((xf - mu) * lax.rsqrt(var + LN_EPS) * g + b).astype(x.dtype)


def post_norm(x, y, g, b):
    return layer_norm(DN_ALPHA * x + y, g, b)


def ffn_half(x, w1, w3, w2, g, b):
    h = jax.nn.silu(x @ w1) * (x @ w3)
    return post_norm(x, 0.5 * (h @ w2), g, b)


def ple_add(x, p, w_proj, w_gate, g, b):
    return post_norm(x, jax.nn.sigmoid(x @ w_gate) * (p @ w_proj), g, b)


def causal_dwconv(u, prev, w, b):
    full = jnp.concatenate([prev.astype(u.dtype), u], axis=1)
    out = lax.conv_general_dilated(full, w[:, None, :].astype(u.dtype), window_strides=(1,), padding='VALID',
                                   dimension_numbers=('NWC', 'WIO', 'NWC'), feature_group_count=u.shape[-1])
    return out + b, full[:, full.shape[1] - (w.shape[0] - 1):]


def gather_pages(cache, page_table):
    rows = cache[page_table]
    return rows.reshape((page_table.shape[0], -1) + cache.shape[2:])


def qkv_proj(x, w_qkv):
    bsz, L, _ = x.shape
    q, k, v = jnp.split(x @ w_qkv, [Q_DIM, Q_DIM + KV_DIM], axis=-1)
    return (q.reshape(bsz, L, N_KV_HEADS, Q_PER_KV, HEAD_DIM),
            k.reshape(bsz, L, N_KV_HEADS, HEAD_DIM),
            v.reshape(bsz, L, N_KV_HEADS, HEAD_DIM))


def merge_heads(o):
    return o.reshape(o.shape[:2] + (-1,))


def stick_breaking_attend(q, k, v, q_pos, k_pos):
    z = jnp.einsum('btkgd,bskd->bkgts', q, k).astype(jnp.float32) * (HEAD_DIM ** -0.5)
    mask = k_pos[None, :] < q_pos[:, None]
    log_keep = jnp.where(mask, jax.nn.log_sigmoid(-z), 0.0)
    log_surv = lax.cumsum(log_keep, axis=log_keep.ndim - 1, reverse=True) - log_keep
    w = jnp.where(mask, jnp.exp(jax.nn.log_sigmoid(z) + log_surv), 0.0)
    return jnp.einsum('bkgts,bskd->btkgd', w.astype(v.dtype), v)


def sb_prompt(q, k, v):
    bsz, L = q.shape[:2]
    nb = L // Q_BLOCK
    qb = jnp.moveaxis(q.reshape((bsz, nb, Q_BLOCK) + q.shape[2:]), 1, 0)
    pos = jnp.arange(L, dtype=jnp.int32)
    o = lax.map(lambda a: stick_breaking_attend(a[0], k, v, a[1], pos), (qb, pos.reshape(nb, Q_BLOCK)))
    return jnp.moveaxis(o, 0, 1).reshape(q.shape)


def moba_blocks(k, v):
    bsz, L = k.shape[:2]
    pad = (-L) % MOBA_BLOCK
    nb = (L + pad) // MOBA_BLOCK

    def blk(t):
        t = jnp.pad(t, ((0, 0), (0, pad), (0, 0), (0, 0)))
        return jnp.transpose(t.reshape(bsz, nb, MOBA_BLOCK, N_KV_HEADS, HEAD_DIM), (0, 3, 1, 2, 4))
    kb, vb = blk(k), blk(v)
    k_mean = jnp.mean(kb.astype(jnp.float32), axis=3).astype(k.dtype)
    return kb, vb, k_mean


def moba_attend(q, q_pos, kb, vb, k_mean):
    bsz, T = q.shape[:2]
    nb = kb.shape[2]
    scale = HEAD_DIM ** -0.5
    n_past = q_pos // MOBA_BLOCK
    gate = jnp.einsum('btkgd,bknd->btkgn', q, k_mean).astype(jnp.float32)
    cand = (jnp.arange(nb)[None, :] < n_past[:, None])[None, :, None, None, :]
    gate = jnp.where(cand, gate, -jnp.inf)
    n_sel = min(MOBA_TOPK, nb)
    _, sel = lax.top_k(gate, n_sel)
    sel_ok = (jnp.arange(n_sel)[None, :] < n_past[:, None])[None, :, None, None, :, None]
    bi = jnp.arange(bsz)[:, None, None, None, None]
    hi = jnp.arange(N_KV_HEADS)[None, None, :, None, None]
    k_sel = kb[bi, hi, sel]
    v_sel = vb[bi, hi, sel]
    bo = jnp.arange(bsz)[:, None, None]
    ho = jnp.arange(N_KV_HEADS)[None, None, :]
    oo = n_past[None, :, None]
    k_own = kb[bo, ho, oo]
    v_own = vb[bo, ho, oo]
    own_pos = n_past[:, None] * MOBA_BLOCK + jnp.arange(MOBA_BLOCK)[None, :]
    own_ok = (own_pos <= q_pos[:, None])[None, :, None, None, :]
    s_sel = jnp.einsum('btkgd,btkgjsd->btkgjs', q, k_sel).astype(jnp.float32) * scale
    s_sel = jnp.where(sel_ok, s_sel, -jnp.inf).reshape(bsz, T, N_KV_HEADS, Q_PER_KV, n_sel * MOBA_BLOCK)
    s_own = jnp.einsum('btkgd,btksd->btkgs', q, k_own).astype(jnp.float32) * scale
    s_own = jnp.where(own_ok, s_own, -jnp.inf)
    p = jax.nn.softmax(jnp.concatenate([s_sel, s_own], axis=-1), axis=-1).astype(vb.dtype)
    p_sel = p[..., :n_sel * MOBA_BLOCK].reshape(bsz, T, N_KV_HEADS, Q_PER_KV, n_sel, MOBA_BLOCK)
    p_own = p[..., n_sel * MOBA_BLOCK:]
    return (jnp.einsum('btkgjs,btkgjsd->btkgd', p_sel, v_sel)
            + jnp.einsum('btkgs,btksd->btkgd', p_own, v_own))


def moba_prompt(q, k, v):
    bsz, L = q.shape[:2]
    kb, vb, km = moba_blocks(k, v)
    nc = L // MOBA_Q_CHUNK
    qc = jnp.moveaxis(q.reshape((bsz, nc, MOBA_Q_CHUNK) + q.shape[2:]), 1, 0)
    pos = jnp.arange(L, dtype=jnp.int32).reshape(nc, MOBA_Q_CHUNK)
    o = lax.map(lambda a: moba_attend(a[0], a[1], kb, vb, km), (qc, pos))
    return jnp.moveaxis(o, 0, 1).reshape(q.shape)


def ssd_chunked(x, dt, a, bm, cm, h0, chunk):
    bsz, L = x.shape[:2]
    nc = L // chunk

    def chunks(t):
        return t.reshape((bsz, nc, chunk) + t.shape[2:])
    xc, dtc, bc, cc = chunks(x), chunks(dt), chunks(bm), chunks(cm)
    cum = jnp.cumsum(dtc * a, axis=2)
    causal = jnp.tril(jnp.ones((chunk, chunk), dtype=bool))[:, :, None, None]
    seg = cum[:, :, :, None] - cum[:, :, None, :]
    decay = jnp.exp(jnp.where(causal, seg, -jnp.inf))
    cb = jnp.einsum('bclgn,bcsgn->bclsg', cc, bc)
    y_intra = jnp.einsum('bclsgr,bcsgrp->bclgrp', cb[..., None] * decay * dtc[:, :, None], xc)
    to_end = jnp.exp(cum[:, :, -1:] - cum) * dtc
    s_chunk = jnp.einsum('bclgr,bclgn,bclgrp->bcgrpn', to_end, bc, xc)
    chunk_decay = jnp.exp(cum[:, :, -1])

    def step(h, inp):
        s_c, d_c = inp
        return d_c[..., None, None] * h + s_c, h
    h_last, h_start = lax.scan(step, h0, (jnp.moveaxis(s_chunk, 1, 0), jnp.moveaxis(chunk_decay, 1, 0)))
    h_start = jnp.moveaxis(h_start, 0, 1)
    y_inter = jnp.einsum('bclgn,bcgrpn->bclgrp', cc, h_start) * jnp.exp(cum)[..., None]
    return (y_intra + y_inter).reshape(x.shape), h_last


def gated_rms_norm(y, z, g):
    bsz, L, _ = y.shape
    h = (y * jax.nn.silu(z.astype(jnp.float32))).reshape(bsz, L, SSM_GROUPS, -1)
    h = h * lax.rsqrt(jnp.mean(h * h, axis=-1, keepdims=True) + LN_EPS)
    return h.reshape(bsz, L, -1) * g


def mamba2_mixer(x, h0, conv_buf, w_in, conv_w, conv_b, dt_bias, a_log, d_skip, norm_g, w_out, chunk):
    bsz, L, _ = x.shape
    f32 = jnp.float32
    z, xbc, dt = jnp.split(x @ w_in, [SSM_D_INNER, SSM_D_INNER + SSM_CONV_DIM], axis=-1)
    xbc, new_buf = causal_dwconv(xbc, conv_buf, conv_w, conv_b)
    xbc = jax.nn.silu(xbc)
    xh, bm, cm = jnp.split(xbc, [SSM_D_INNER, SSM_D_INNER + SSM_GROUPS * SSM_D_STATE], axis=-1)
    xh = xh.astype(f32).reshape(bsz, L, SSM_GROUPS, SSM_HPG, SSM_HEAD_DIM)
    bm = bm.astype(f32).reshape(bsz, L, SSM_GROUPS, SSM_D_STATE)
    cm = cm.astype(f32).reshape(bsz, L, SSM_GROUPS, SSM_D_STATE)
    dt = jax.nn.softplus((dt + dt_bias).astype(f32)).reshape(bsz, L, SSM_GROUPS, SSM_HPG)
    a = -jnp.exp(a_log.astype(f32)).reshape(SSM_GROUPS, SSM_HPG)
    h0 = h0.astype(f32).reshape(bsz, SSM_GROUPS, SSM_HPG, SSM_HEAD_DIM, SSM_D_STATE)
    y, h = ssd_chunked(xh, dt, a, bm, cm, h0, chunk)
    y = y + d_skip.astype(f32).reshape(SSM_GROUPS, SSM_HPG)[:, :, None] * xh
    y = gated_rms_norm(y.reshape(bsz, L, SSM_D_INNER), z, norm_g).astype(x.dtype)
    return (y @ w_out, h.reshape(bsz, SSM_HEADS, SSM_HEAD_DIM, SSM_D_STATE).astype(x.dtype), new_buf)


def conformer_conv_mixer(x, buf, w_pw1, b_pw1, w_dw, b_dw, g, b, w_pw2, b_pw2):
    u_a, u_g = jnp.split(x @ w_pw1 + b_pw1, 2, axis=-1)
    u = u_a * jax.nn.sigmoid(u_g)
    u, new_buf = causal_dwconv(u, buf, w_dw, b_dw)
    u = jax.nn.silu(layer_norm(u, g, b))
    return u @ w_pw2 + b_pw2, new_buf


def setup_inputs(seed: int = 0) -> dict:
    key = jax.random.key(seed)
    keys = jax.random.split(key, 48)
    counter = [0]
    f32 = jnp.float32

    def nxt():
        counter[0] += 1
        return keys[counter[0] - 1]

    def nrm(shape, scale=1.0):
        return jax.random.normal(nxt(), shape, f32) * scale

    def dense(shape, out_scale=1.0):
        return nrm(shape, shape[-2] ** -0.5 * out_scale)

    n_pages = PAST_LEN // PAGE_SIZE
    n_used = DEC_BATCH * n_pages
    n_phys = n_used + (n_used + 3) // 4
    inp = {}
    inp['x_prompt'] = nrm((BATCH, SEQ, D_MODEL))
    inp['x_sample'] = nrm((DEC_BATCH, DEC_SEQ, D_MODEL))
    inp['p_prompt'] = nrm((DEPTH, BATCH, SEQ, PLE_DIM))
    inp['p_sample'] = nrm((DEPTH, DEC_BATCH, DEC_SEQ, PLE_DIM))
    inp['cache_sb_k'] = nrm((N_SB, n_phys, PAGE_SIZE, N_KV_HEADS, HEAD_DIM))
    inp['cache_sb_v'] = nrm((N_SB, n_phys, PAGE_SIZE, N_KV_HEADS, HEAD_DIM))
    inp['cache_moba_k'] = nrm((N_MOBA, n_phys, PAGE_SIZE, N_KV_HEADS, HEAD_DIM))
    inp['cache_moba_v'] = nrm((N_MOBA, n_phys, PAGE_SIZE, N_KV_HEADS, HEAD_DIM))
    inp['state_ssm'] = nrm((N_SSM, DEC_BATCH, SSM_HEADS, SSM_HEAD_DIM, SSM_D_STATE), 0.2)
    inp['state_ssm_conv'] = nrm((N_SSM, DEC_BATCH, SSM_CONV - 1, SSM_CONV_DIM))
    inp['state_conf_conv'] = nrm((N_CONF, DEC_BATCH, CONF_WIDTH - 1, D_MODEL))
    inp['page_table'] = jax.random.permutation(nxt(), n_phys)[:n_used].reshape(DEC_BATCH, n_pages).astype(jnp.int32)
    inp['ln_g'] = 1.0 + nrm((DEPTH, 4, D_MODEL), 0.02)
    inp['ln_b'] = nrm((DEPTH, 4, D_MODEL), 0.02)
    inp['ffn_w1'] = dense((DEPTH, 2, D_MODEL, D_FF))
    inp['ffn_w3'] = dense((DEPTH, 2, D_MODEL, D_FF))
    inp['ffn_w2'] = dense((DEPTH, 2, D_FF, D_MODEL), DN_BETA)
    inp['ple_w_proj'] = dense((DEPTH, PLE_DIM, D_MODEL), DN_BETA)
    inp['ple_w_gate'] = dense((DEPTH, D_MODEL, D_MODEL))
    inp['sb_w_qkv'] = dense((N_SB, D_MODEL, Q_DIM + 2 * KV_DIM))
    inp['sb_w_o'] = dense((N_SB, Q_DIM, D_MODEL), DN_BETA)
    inp['ssm_w_in'] = dense((N_SSM, D_MODEL, SSM_IN_DIM))
    inp['ssm_conv_w'] = dense((N_SSM, SSM_CONV, SSM_CONV_DIM))
    inp['ssm_conv_b'] = nrm((N_SSM, SSM_CONV_DIM), 0.02)
    dt0 = jnp.exp(jax.random.uniform(nxt(), (N_SSM, SSM_HEADS), f32, math.log(1e-3), math.log(1e-1)))
    inp['ssm_dt_bias'] = dt0 + jnp.log(-jnp.expm1(-dt0))
    inp['ssm_a_log'] = jnp.log(jax.random.uniform(nxt(), (N_SSM, SSM_HEADS), f32, 1.0, 16.0))
    inp['ssm_d'] = 1.0 + nrm((N_SSM, SSM_HEADS), 0.02)
    inp['ssm_norm_g'] = 1.0 + nrm((N_SSM, SSM_D_INNER), 0.02)
    inp['ssm_w_out'] = dense((N_SSM, SSM_D_INNER, D_MODEL), DN_BETA)
    inp['conf_w_pw1'] = dense((N_CONF, D_MODEL, 2 * D_MODEL))
    inp['conf_b_pw1'] = nrm((N_CONF, 2 * D_MODEL), 0.02)
    inp['conf_w_dw'] = dense((N_CONF, CONF_WIDTH, D_MODEL))
    inp['conf_b_dw'] = nrm((N_CONF, D_MODEL), 0.02)
    inp['conf_ln_g'] = 1.0 + nrm((N_CONF, D_MODEL), 0.02)
    inp['conf_ln_b'] = nrm((N_CONF, D_MODEL), 0.02)
    inp['conf_w_pw2'] = dense((N_CONF, D_MODEL, D_MODEL), DN_BETA)
    inp['conf_b_pw2'] = nrm((N_CONF, D_MODEL), 0.02)
    inp['moba_w_qkv'] = dense((N_MOBA, D_MODEL, Q_DIM + 2 * KV_DIM))
    inp['moba_w_o'] = dense((N_MOBA, Q_DIM, D_MODEL), DN_BETA)
    return inp


def reference(x_prompt, x_sample, p_prompt, p_sample, cache_sb_k, cache_sb_v, cache_moba_k, cache_moba_v,
              state_ssm, state_ssm_conv, state_conf_conv, page_table,
              ln_g, ln_b, ffn_w1, ffn_w3, ffn_w2, ple_w_proj, ple_w_gate, sb_w_qkv, sb_w_o,
              ssm_w_in, ssm_conv_w, ssm_conv_b, ssm_dt_bias, ssm_a_log, ssm_d, ssm_norm_g, ssm_w_out,
              conf_w_pw1, conf_b_pw1, conf_w_dw, conf_b_dw, conf_ln_g, conf_ln_b, conf_w_pw2, conf_b_pw2,
              moba_w_qkv, moba_w_o):
    xp, xs = x_prompt, x_sample
    n_p = xp.shape[0]
    t_new = xs.shape[1]
    past = page_table.shape[1] * PAGE_SIZE
    pos_s = past + jnp.arange(t_new, dtype=jnp.int32)
    kpos_s = jnp.arange(past + t_new, dtype=jnp.int32)
    sb_kp, sb_vp, sb_ks, sb_vs = [], [], [], []
    ssm_hp, ssm_hs, ssm_cp, ssm_cs = [], [], [], []
    conf_cp, conf_cs = [], []
    mo_kp, mo_vp, mo_ks, mo_vs = [], [], [], []
    for i in range(DEPTH):
        m, j = i % N_MIXERS, i // N_MIXERS
        f1 = (ffn_w1[i, 0], ffn_w3[i, 0], ffn_w2[i, 0], ln_g[i, 0], ln_b[i, 0])
        xp, xs = ffn_half(xp, *f1), ffn_half(xs, *f1)
        if m == 0:
            q, k, v = qkv_proj(xp, sb_w_qkv[j])
            yp = merge_heads(sb_prompt(q, k, v)) @ sb_w_o[j]
            sb_kp.append(k)
            sb_vp.append(v)
            q, k, v = qkv_proj(xs, sb_w_qkv[j])
            k_all = jnp.concatenate([gather_pages(cache_sb_k[j], page_table).astype(k.dtype), k], axis=1)
            v_all = jnp.concatenate([gather_pages(cache_sb_v[j], page_table).astype(v.dtype), v], axis=1)
            ys = merge_heads(stick_breaking_attend(q, k_all, v_all, pos_s, kpos_s)) @ sb_w_o[j]
            sb_ks.append(k)
            sb_vs.append(v)
        elif m == 1:
            ssm_w = (ssm_w_in[j], ssm_conv_w[j], ssm_conv_b[j], ssm_dt_bias[j], ssm_a_log[j], ssm_d[j],
                     ssm_norm_g[j], ssm_w_out[j])
            h0p = jnp.zeros((n_p, SSM_HEADS, SSM_HEAD_DIM, SSM_D_STATE), xp.dtype)
            bufp = jnp.zeros((n_p, SSM_CONV - 1, SSM_CONV_DIM), xp.dtype)
            yp, h, buf = mamba2_mixer(xp, h0p, bufp, *ssm_w, SSM_CHUNK)
            ssm_hp.append(h)
            ssm_cp.append(buf)
            ys, h, buf = mamba2_mixer(xs, state_ssm[j], state_ssm_conv[j], *ssm_w, t_new)
            ssm_hs.append(h)
            ssm_cs.append(buf)
        elif m == 2:
            conf_w = (conf_w_pw1[j], conf_b_pw1[j], conf_w_dw[j], conf_b_dw[j], conf_ln_g[j], conf_ln_b[j],
                      conf_w_pw2[j], conf_b_pw2[j])
            bufp = jnp.zeros((n_p, CONF_WIDTH - 1, D_MODEL), xp.dtype)
            yp, buf = conformer_conv_mixer(xp, bufp, *conf_w)
            conf_cp.append(buf)
            ys, buf = conformer_conv_mixer(xs, state_conf_conv[j], *conf_w)
            conf_cs.append(buf)
        else:
            q, k, v = qkv_proj(xp, moba_w_qkv[j])
            yp = merge_heads(moba_prompt(q, k, v)) @ moba_w_o[j]
            mo_kp.append(k)
            mo_vp.append(v)
            q, k, v = qkv_proj(xs, moba_w_qkv[j])
            k_all = jnp.concatenate([gather_pages(cache_moba_k[j], page_table).astype(k.dtype), k], axis=1)
            v_all = jnp.concatenate([gather_pages(cache_moba_v[j], page_table).astype(v.dtype), v], axis=1)
            kb, vb, km = moba_blocks(k_all, v_all)
            ys = merge_heads(moba_attend(q, pos_s, kb, vb, km)) @ moba_w_o[j]
            mo_ks.append(k)
            mo_vs.append(v)
        xp, xs = post_norm(xp, yp, ln_g[i, 1], ln_b[i, 1]), post_norm(xs, ys, ln_g[i, 1], ln_b[i, 1])
        f2 = (ffn_w1[i, 1], ffn_w3[i, 1], ffn_w2[i, 1], ln_g[i, 2], ln_b[i, 2])
        xp, xs = ffn_half(xp, *f2), ffn_half(xs, *f2)
        xp = ple_add(xp, p_prompt[i], ple_w_proj[i], ple_w_gate[i], ln_g[i, 3], ln_b[i, 3])
        xs = ple_add(xs, p_sample[i], ple_w_proj[i], ple_w_gate[i], ln_g[i, 3], ln_b[i, 3])
    return (xp, xs,
            jnp.stack(sb_kp), jnp.stack(sb_vp), jnp.stack(sb_ks), jnp.stack(sb_vs),
            jnp.stack(ssm_hp), jnp.stack(ssm_hs), jnp.stack(ssm_cp), jnp.stack(ssm_cs),
            jnp.stack(conf_cp), jnp.stack(conf_cs),
            jnp.stack(mo_kp), jnp.stack(mo_vp), jnp.stack(mo_ks), jnp.stack(mo_vs))
```

```python
import numpy as np
import concourse.bass as bass
import concourse.mybir as mybir

F32 = mybir.dt.float32
BF16 = mybir.dt.bfloat16
I32 = mybir.dt.int32
U32 = mybir.dt.uint32
AF = mybir.ActivationFunctionType
ALU = mybir.AluOpType
AX = mybir.AxisListType

ENGS = ['pe', 'act', 'dve', 'pool', 'sp']
N_DMA_SEMS = 12


class Buf:
    __slots__ = ('name', 'w', 'rs')

    def __init__(self, name=''):
        self.name = name
        self.w = None
        self.rs = []


class T:
    __slots__ = ('ap', 'buf')

    def __init__(self, ap, name=''):
        self.ap = ap
        self.buf = Buf(name)


class Op:
    __slots__ = ('eng', 'fn', 'deps', 'dma', 'sem', 'val', 'marked', 'idx')

    def __init__(self, eng, fn, dma):
        self.eng = eng
        self.fn = fn
        self.deps = set()
        self.dma = dma
        self.sem = None
        self.val = None
        self.marked = False
        self.idx = -1


class Prog:
    def __init__(self, nc):
        self.nc = nc
        self.streams = {e: [] for e in ENGS}
        self.dma_count = {e: 0 for e in ENGS}
        self.dma_last = {}
        self.last_real = {e: None for e in ENGS}
        self.pending_dmas = []
        self.out_dmas = []
        self.arena_off = 0

    def op(self, eng, fn, reads=(), writes=(), dma=False, out=False):
        o = Op(eng, fn, dma)
        for t in reads:
            b = t.buf if isinstance(t, T) else t
            if b.w is not None:
                o.deps.add(b.w)
        for t in writes:
            b = t.buf if isinstance(t, T) else t
            if b.w is not None:
                o.deps.add(b.w)
            for r in b.rs:
                o.deps.add(r)
        if dma:
            slot = self.dma_count[eng] % N_DMA_SEMS
            self.dma_count[eng] += 1
            prev = self.dma_last.get((eng, slot))
            if prev is not None:
                o.deps.add(prev)
            self.dma_last[(eng, slot)] = o
            o.sem = ('dma', eng, slot)
            self.pending_dmas.append(o)
            if out:
                self.out_dmas.append(o)
        else:
            if eng == 'pe':
                o.deps = {d for d in o.deps if not (d.eng == 'pe' and not d.dma)}
            if fn is not None:
                self.last_real[eng] = o
        o.deps.discard(o)
        for t in reads:
            b = t.buf if isinstance(t, T) else t
            b.rs.append(o)
        for t in writes:
            b = t.buf if isinstance(t, T) else t
            b.w = o
            b.rs = []
        o.idx = len(self.streams[eng])
        self.streams[eng].append(o)
        return o

    def barrier(self):
        lasts = [self.last_real[e] for e in ENGS if self.last_real[e] is not None]
        pend = list(self.pending_dmas)
        self.pending_dmas = []
        for e in ENGS:
            o = Op(e, None, False)
            o.deps = set(lasts) | set(pend)
            o.idx = len(self.streams[e])
            self.streams[e].append(o)

    def finish(self):
        o = Op('sp', None, False)
        o.deps = set(self.out_dmas) | set(self.pending_dmas)
        self.streams['sp'].append(o)

    def emit(self, block, sems):
        for e in ENGS:
            for o in self.streams[e]:
                for d in o.deps:
                    d.marked = True
        for e in ENGS:
            cnt = 0
            dcnt = {}
            for o in self.streams[e]:
                if o.dma:
                    dcnt[o.sem] = dcnt.get(o.sem, 0) + 16
                    o.val = dcnt[o.sem]
                elif o.fn is not None and o.marked:
                    cnt += 1
                    o.sem = ('e', e)
                    o.val = cnt
        prog = self

        def run(e, eng):
            seen = {}
            n_wait = 0
            for o in prog.streams[e]:
                need = {}
                for d in o.deps:
                    if d.val is None:
                        continue
                    if need.get(d.sem, 0) < d.val:
                        need[d.sem] = d.val
                for s, v in need.items():
                    if seen.get(s, 0) < v:
                        eng.wait_ge(sems[s], v)
                        seen[s] = v
                        n_wait += 1
                if o.fn is None:
                    continue
                ins = o.fn(eng)
                if o.dma:
                    ins.then_inc(sems[o.sem], 16)
                elif o.marked:
                    ins.then_inc(sems[o.sem], 1)

        @block.tensor
        def _(eng):
            run('pe', eng)

        @block.scalar
        def _(eng):
            run('act', eng)

        @block.vector
        def _(eng):
            run('dve', eng)

        @block.gpsimd
        def _(eng):
            run('pool', eng)

        @block.sync
        def _(eng):
            run('sp', eng)

    def sem_names(self):
        names = [('e', e) for e in ENGS]
        for e in ('sp', 'act', 'pool'):
            for s in range(N_DMA_SEMS):
                names.append(('dma', e, s))
        return names


import math
from contextlib import ExitStack
from concourse.bass_utils import run_bass_kernel_spmd

D = 2048
DC = 16
FF = 5632
FC = 44
T_ = 2048
TG = 512
NG = 4
NL = 4
ALPHA = (2 * NL) ** 0.25
EPS = 1e-5
SCALE = 128 ** -0.5
NEG = -30000.0
N_WSLOT = 8
ARENA_WORDS = 51500


def _prod(s):
    r = 1
    for v in s:
        r *= v
    return r


class Arena:
    def __init__(self, sb, nwords):
        self.sb = sb
        self.n = nwords
        self.off = 0

    def alloc(self, shape, dtype, name=''):
        nfree = _prod(shape[1:])
        words = nfree if dtype in (F32, I32, U32) else (nfree + 1) // 2
        a = self.sb[0:shape[0], self.off:self.off + words]
        self.off += words
        assert self.off <= self.n, f"arena overflow at {name}: {self.off}"
        if dtype == BF16:
            a = a.bitcast(BF16)[:, 0:nfree]
        elif dtype != F32:
            a = a.bitcast(dtype)
        if len(shape) == 3:
            a = a.rearrange("p (k w) -> p k w", k=shape[1])
        elif len(shape) == 4:
            a = a.rearrange("p (k j w) -> p k j w", k=shape[1], j=shape[2])
        return T(a, name)


class Seg:
    def __init__(self, A, W, name):
        self.W = W
        self.xf = A.alloc([128, DC, W], F32, name + '_xf')
        self.xb = A.alloc([128, DC, W], BF16, name + '_xb')
        off = A.off
        self.h = A.alloc([128, FC, W], BF16, name + '_h')
        if W == TG:
            self.hf = T(A.sb[:, off:off + DC * W].rearrange("p (k w) -> p k w", k=DC), name + '_hf')
            self.hf.buf = self.h.buf


class K:
    def __init__(self, nc, cfg):
        self.nc = nc
        self.cfg = cfg
        self.P = Prog(nc)
        self.dram = {}

    def din(self, name, shape, dtype=F32):
        t = self.nc.dram_tensor(name, list(shape), dtype, kind="ExternalInput").ap()
        self.dram[name] = t
        return t

    def dout(self, name, shape, dtype=F32):
        t = self.nc.dram_tensor(name, list(shape), dtype, kind="ExternalOutput").ap()
        self.dram[name] = t
        return t

    def dscr(self, name, shape, dtype=F32):
        t = self.nc.dram_tensor(name, list(shape), dtype, kind="Internal").ap()
        self.dram[name] = t
        return t

    def dma(self, q, out, in_, reads=(), writes=(), is_out=False):
        return self.P.op(q, lambda e: e.dma_start(out=out, in_=in_), reads=reads, writes=writes, dma=True, out=is_out)

    def mm(self, out_t, out_ap, l_t, l_ap, r_t, r_ap, start, stop):
        return self.P.op('pe', lambda e: e.matmul(out_ap, lhsT=l_ap, rhs=r_ap, start=start, stop=stop),
                         reads=[l_t, r_t], writes=[out_t])

    def tr(self, out_t, out_ap, in_t, in_ap, ident_ap):
        return self.P.op('pe', lambda e: e.transpose(out_ap, in_ap, ident_ap), reads=[in_t], writes=[out_t])

    def act(self, out_t, out_ap, in_t, in_ap, func, bias=None, scale=None, reads=(), accum=None, extra_w=()):
        kw = {}
        if bias is not None:
            kw['bias'] = bias
        if scale is not None:
            kw['scale'] = scale
        if accum is not None:
            kw['accum_out'] = accum
        return self.P.op('act', lambda e: e.activation(out_ap, in_ap, func, **kw),
                         reads=[in_t] + list(reads), writes=[out_t] + list(extra_w))

    def dve(self, fn, reads, writes):
        return self.P.op('dve', fn, reads=reads, writes=writes)

    def psum(self, kind='p'):
        if kind == 'p':
            i = self.ps_rr % self.n_ps_p
            self.ps_rr += 1
            return self.ps[i]
        i = self.n_ps_p + (self.pss_rr % (8 - self.n_ps_p))
        self.pss_rr += 1
        return self.ps[i]

    def tmp(self, kind='p'):
        if kind == 'p':
            i = self.tmp_rr % len(self.tmps)
            self.tmp_rr += 1
            return self.tmps[i]
        i = self.tmps_rr % len(self.tmps_s)
        self.tmps_rr += 1
        return self.tmps_s[i]

    def wslot(self):
        i = self.ws_rr % N_WSLOT
        self.ws_rr += 1
        return self.wslots[i]

    def linear(self, segs, srcs, wt, KC, o_list, epi):
        for o in o_list:
            pss = [None] * len(segs)
            for kg in range(0, KC, 16):
                kc = min(16, KC - kg)
                sl = self.wslot()
                sl3 = sl.ap[:, 0:kc * 128].rearrange("p (k c) -> p k c", k=kc)
                self.dma('pool', sl3, wt[o, :, kg:kg + kc, :], writes=[sl])
                for si, sg in enumerate(segs):
                    if kg == 0:
                        pss[si] = self.psum('p' if sg.W > 1 else 's')
                    src_t, src_ap = srcs[si]
                    for k in range(kc):
                        self.mm(pss[si], pss[si].ap[:, 0:sg.W], sl, sl3[:, k, :], src_t, src_ap[:, kg + k, 0:sg.W],
                                start=(kg + k == 0), stop=(kg + k == KC - 1))
            for si, sg in enumerate(segs):
                epi(o, si, pss[si], pss[si].ap[:, 0:sg.W])

    def layernorm(self, sg, gcol):
        self.ln_generic(sg.W, sg.xf, sg.xf.ap, lambda k: (self.lng.ap[:, gcol, k:k + 1], self.lnb.ap[:, gcol, k:k + 1]),
                        (sg.xf, sg.xf.ap), (sg.xb, sg.xb.ap), AF.Identity)

    def ln_generic(self, W, x_t, x_ap, gb, out_f, out_b, func):
        kd = 'p' if W > 1 else 's'
        ps1 = self.psum(kd)
        ps2 = self.psum(kd)
        ones = self.c_ones
        for k in range(DC):
            sq = self.tmp(kd)
            xk = x_ap[:, k, 0:W]
            sqa = sq.ap[:, 0:W]
            self.dve(lambda e, sqa=sqa, xk=xk: e.tensor_tensor(sqa, xk, xk, ALU.mult), [x_t], [sq])
            self.mm(ps1, ps1.ap[:, 0:W], ones, ones.ap, x_t, xk, k == 0, k == DC - 1)
            self.mm(ps2, ps2.ap[:, 0:W], ones, ones.ap, sq, sqa, k == 0, k == DC - 1)
        mean, msq, rstd = self.ln_tiles[kd]
        ma, qa, ra = mean.ap[:, 0:W], msq.ap[:, 0:W], rstd.ap[:, 0:W]
        p1, p2 = ps1.ap[:, 0:W], ps2.ap[:, 0:W]
        self.dve(lambda e: e.tensor_scalar(ma, p1, 1.0 / D, None, ALU.mult), [ps1], [mean])
        self.dve(lambda e: e.tensor_tensor(qa, ma, ma, ALU.mult), [mean], [msq])
        self.dve(lambda e: e.scalar_tensor_tensor(qa, p2, 1.0 / D, qa, ALU.mult, ALU.subtract), [ps2, msq], [msq])
        self.act(rstd, ra, msq, qa, AF.Sqrt, bias=self.c_eps.ap[:, 0:1])
        self.dve(lambda e: e.reciprocal(ra, ra), [rstd], [rstd])
        for k in range(DC):
            t1 = self.tmp(kd)
            ta = t1.ap[:, 0:W]
            xk = x_ap[:, k, 0:W]
            self.dve(lambda e, ta=ta, xk=xk: e.tensor_tensor(ta, xk, ma, ALU.subtract), [x_t, mean], [t1])
            self.dve(lambda e, ta=ta: e.tensor_tensor(ta, ta, ra, ALU.mult), [t1, rstd], [t1])
            g, b = gb(k)
            if out_f is not None:
                self.act(out_f[0], out_f[1][:, k, 0:W], t1, ta, func, bias=b, scale=g, reads=[self.cvec])
            if out_b is not None:
                self.act(out_b[0], out_b[1][:, k, 0:W], t1, ta, func, bias=b, scale=g, reads=[self.cvec])

    def resid_epi(self, segs):
        def epi(o, si, ps_t, ps_ap):
            sg = segs[si]
            xk = sg.xf.ap[:, o, 0:sg.W]
            self.dve(lambda e: e.scalar_tensor_tensor(xk, xk, ALPHA, ps_ap, ALU.mult, ALU.add), [sg.xf, ps_t], [sg.xf])
        return epi

    def ffn(self, segs, l, j):
        w1, w3, w2 = self.dram['w1'], self.dram['w3'], self.dram['w2']
        srcs = [(sg.xb, sg.xb.ap) for sg in segs]
        for o in range(FC):
            s1 = self.wslot()
            s3 = self.wslot()
            a1 = s1.ap.rearrange("p (k c) -> p k c", k=16)
            a3 = s3.ap.rearrange("p (k c) -> p k c", k=16)
            self.dma('pool', a1, w1[l, j, o], writes=[s1])
            self.dma('pool', a3, w3[l, j, o], writes=[s3])
            for si, sg in enumerate(segs):
                kd = 'p' if sg.W > 1 else 's'
                pa = self.psum(kd)
                pb = self.psum(kd)
                W = sg.W
                for k in range(DC):
                    self.mm(pa, pa.ap[:, 0:W], s1, a1[:, k, :], sg.xb, sg.xb.ap[:, k, 0:W], k == 0, k == DC - 1)
                for k in range(DC):
                    self.mm(pb, pb.ap[:, 0:W], s3, a3[:, k, :], sg.xb, sg.xb.ap[:, k, 0:W], k == 0, k == DC - 1)
                sa = self.tmp(kd)
                saa = sa.ap[:, 0:W]
                self.act(sa, saa, pa, pa.ap[:, 0:W], AF.Silu)
                ho = sg.h.ap[:, o, 0:W]
                pba = pb.ap[:, 0:W]
                self.dve(lambda e, ho=ho, pba=pba, saa=saa: e.scalar_tensor_tensor(ho, pba, 0.5, saa, ALU.mult, ALU.mult),
                         [pb, sa], [sg.h])
        hs = [(sg.h, sg.h.ap) for sg in segs]
        self.linear(segs, hs, w2[l, j], FC, range(DC), self.resid_epi(segs))
        for sg in segs:
            self.layernorm(sg, l * 4 + (0 if j == 0 else 2))

    def ple(self, segs, psrcs, l):
        wg, wp = self.dram['wgate'], self.dram['wproj']
        for o in range(DC):
            sl = self.wslot()
            sl3 = sl.ap.rearrange("p (k c) -> p k c", k=16)
            self.dma('pool', sl3, wg[l, o], writes=[sl])
            sp_ = self.wslot()
            sp3 = sp_.ap[:, 0:256].rearrange("p (k c) -> p k c", k=2)
            self.dma('pool', sp3, wp[l, o], writes=[sp_])
            for si, sg in enumerate(segs):
                W = sg.W
                kd = 'p' if W > 1 else 's'
                pg = self.psum(kd)
                pp = self.psum(kd)
                for k in range(DC):
                    self.mm(pg, pg.ap[:, 0:W], sl, sl3[:, k, :], sg.xb, sg.xb.ap[:, k, 0:W], k == 0, k == DC - 1)
                pt_t, pt_ap = psrcs[si]
                for k in range(2):
                    self.mm(pp, pp.ap[:, 0:W], sp_, sp3[:, k, :], pt_t, pt_ap[:, k, 0:W], k == 0, k == 1)
                sgm = self.tmp(kd)
                sa = sgm.ap[:, 0:W]
                self.act(sgm, sa, pg, pg.ap[:, 0:W], AF.Sigmoid)
                ppa = pp.ap[:, 0:W]
                self.dve(lambda e, sa=sa, ppa=ppa: e.tensor_tensor(sa, sa, ppa, ALU.mult), [sgm, pp], [sgm])
                xk = sg.xf.ap[:, o, 0:W]
                self.dve(lambda e, sa=sa, xk=xk: e.scalar_tensor_tensor(xk, xk, ALPHA, sa, ALU.mult, ALU.add),
                         [sg.xf, sgm], [sg.xf])
        for sg in segs:
            self.layernorm(sg, l * 4 + 3)

    def setup(self, es):
        nc = self.nc
        self.sb = es.enter_context(nc.sbuf_tensor("arena", [128, ARENA_WORDS], F32))
        self.A = Arena(self.sb, ARENA_WORDS)
        self.ps = []
        for i in range(8):
            p = es.enter_context(nc.psum_tensor(f"ps{i}", [128, 512], F32))
            self.ps.append(T(p[:, :], f"ps{i}"))
        self.n_ps_p = 6
        self.ps_rr = 0
        self.pss_rr = 0
        self.tmp_rr = 0
        self.tmps_rr = 0
        self.ws_rr = 0
        A = self.A
        cst = self.din('consts', [128, 1024])
        self.c_all = A.alloc([128, 1024], F32, 'consts')
        self.dma('sp', self.c_all.ap, cst, writes=[self.c_all])
        ca = self.c_all
        self.c_ones = T(ca.ap[:, 0:128], 'ones')
        self.c_ones.buf = ca.buf
        self.c_ident = T(ca.ap[:, 128:256], 'ident')
        self.c_ident.buf = ca.buf
        self.c_eps = T(ca.ap[:, 256:257], 'eps')
        self.c_eps.buf = ca.buf
        self.c_mask_lt = T(ca.ap[:, 384:512], 'mask_lt')
        self.c_mask_lt.buf = ca.buf
        self.c_bias_le = T(ca.ap[:, 512:640], 'bias_le')
        self.c_bias_le.buf = ca.buf
        self.c_tri_le = T(ca.ap[:, 640:768], 'tri_le')
        self.c_tri_le.buf = ca.buf
        self.c_tri_gt = T(ca.ap[:, 768:896], 'tri_gt')
        self.c_tri_gt.buf = ca.buf
        self.c_bias_lt = T(ca.ap[:, 896:1024], 'bias_lt')
        self.c_bias_lt.buf = ca.buf
        self.c_identb = A.alloc([128, 128], BF16, 'identb')
        self.dve(lambda e: e.tensor_copy(self.c_identb.ap, self.c_ident.ap), [ca], [self.c_identb])
        lg = self.din('lng', [128, 16 * 16])
        lb = self.din('lnb', [128, 16 * 16])
        self.lng = A.alloc([128, 16, 16], F32, 'lng')
        self.lnb = A.alloc([128, 16, 16], F32, 'lnb')
        self.dma('sp', self.lng.ap, lg.rearrange("p (a b) -> p a b", a=16), writes=[self.lng])
        self.dma('sp', self.lnb.ap, lb.rearrange("p (a b) -> p a b", a=16), writes=[self.lnb])
        cv = self.din('cvec', [128, 2048])
        self.cvec = A.alloc([128, 2048], F32, 'cvec')
        self.dma('sp', self.cvec.ap, cv, writes=[self.cvec])
        self.wslots = [A.alloc([128, 2048], BF16, f'ws{i}') for i in range(N_WSLOT)]
        self.tmps = [A.alloc([128, 512], F32, f'tmp{i}') for i in range(4)]
        self.tmps_s = [A.alloc([128, 1], F32, f'tmps{i}') for i in range(4)]
        self.ln_tiles = {'p': [A.alloc([128, 512], F32, f'ln{i}') for i in range(3)],
                         's': [A.alloc([128, 1], F32, f'lns{i}') for i in range(3)]}
        self.sseg = Seg(A, 1, 'ss')
        self.base_off = A.off

    def declare(self):
        d = self.din
        d('xT', [D, T_]); d('xsT', [128, DC])
        d('pT', [NL, 256, T_]); d('psT', [NL, 128, 2])
        d('w1', [NL, 2, FC, 128, DC, 128]); d('w3', [NL, 2, FC, 128, DC, 128]); d('w2', [NL, 2, DC, 128, FC, 128])
        d('wgate', [NL, DC, 128, DC, 128]); d('wproj', [NL, DC, 128, 2, 128])
        d('sbqkv', [24, 128, DC, 128]); d('sbo', [DC, 128, DC, 128])
        d('mbqkv', [24, 128, DC, 128]); d('mbo', [DC, 128, DC, 128])
        d('ssmin', [81, 128, DC, 128]); d('ssmout', [DC, 128, 32, 128]); d('gexp', [64, 4096])
        d('sst', [4096, 128]); d('scstT', [128, 48, 3])
        for nm in ('sbkc', 'sbvc', 'mbkc', 'mbvc'):
            d(nm, [1280 * 128, 512])
        d('ptab', [1, 128], I32); d('consts2', [128, 2048])
        d('cpw1', [32, 128, DC, 128]); d('cpw2', [DC, 128, DC, 128]); d('cstT', [128, DC, 30])
        o = self.dout
        o('ccT', [D, 30]); o('ccsT', [128, DC, 30])
        o('shp', [4096, 128]); o('shs', [4096, 128]); o('scT', [6144, 3]); o('scsT', [128, 48, 3])
        o('yT', [D, T_]); o('ysT', [128, DC])
        o('sbkT', [512, T_]); o('sbvT', [512, T_]); o('sbksT', [128, 4]); o('sbvsT', [128, 4])
        o('mbkT', [512, T_]); o('mbvT', [512, T_]); o('mbksT', [128, 4]); o('mbvsT', [128, 4])
        s = self.dscr
        s('XS', [NG, 128, DC, TG])
        s('QT', [16, 128, T_], BF16)
        s('AO', [32, 128, T_], BF16)
        s('UT', [D, T_]); s('CT', [D, T_])
        s('ZT', [4096, T_]); s('XBC', [6144, T_]); s('DTr', [64, T_]); s('XC', [6144, T_]); s('YT', [4096, T_])

    def fm(self, ap2, g):
        return ap2.rearrange("(k p) t -> p k t", p=128)[:, :, g * TG:(g + 1) * TG]

    def fms(self, ap2):
        return ap2

    def stage(self, dtype):
        if dtype == BF16:
            i = self.stb_rr % len(self.stb)
            self.stb_rr += 1
            return self.stb[i]
        i = self.stf_rr % len(self.stf)
        self.stf_rr += 1
        return self.stf[i]

    def qkv_proj(self, segs, g, wname, kname, vname):
        dr = self.dram

        def epi(o, si, ps_t, ps_ap):
            if si == 0:
                if o < 16:
                    st = self.stage(BF16)
                    self.act(st, st.ap, ps_t, ps_ap, AF.Copy)
                    self.dma('sp', dr['QT'][o, :, g * TG:(g + 1) * TG], st.ap, reads=[st], writes=[self.b_qt])
                else:
                    st = self.stage(F32)
                    self.act(st, st.ap, ps_t, ps_ap, AF.Copy)
                    nm = kname if o < 20 else vname
                    c = (o - 16) % 4
                    self.dma('sp', dr[nm + 'T'][c * 128:(c + 1) * 128, g * TG:(g + 1) * TG], st.ap,
                             reads=[st], writes=[self.b_kv], is_out=True)
            else:
                if o < 16:
                    self.act(self.qs, self.qs.ap[:, o, :], ps_t, ps_ap, AF.Copy)
                else:
                    self.act(self.kvs, self.kvs.ap[:, o - 16, :], ps_t, ps_ap, AF.Copy)
        srcs = [(sg.xb, sg.xb.ap) for sg in segs]
        self.linear(segs, srcs, dr[wname], DC, range(24), epi)
        if len(segs) > 1:
            self.dma('sp', dr[kname + 'sT'], self.kvs.ap[:, 0:4, 0], reads=[self.kvs], is_out=True)
            self.dma('sp', dr[vname + 'sT'], self.kvs.ap[:, 4:8, 0], reads=[self.kvs], is_out=True)

    def build(self):
        cfg = self.cfg
        with ExitStack() as es:
            self.setup(es)
            self.declare()
            A = self.A
            dr = self.dram
            self.b_qt = Buf('QT')
            self.b_kv = Buf('KV')
            self.b_xs = [Buf(f'XS{g}') for g in range(NG)]
            self.b_ao = Buf('AO')
            self.qs = A.alloc([128, 16, 1], F32, 'qs')
            self.kvs = A.alloc([128, 8, 1], F32, 'kvs')
            self.aos = A.alloc([128, 32, 1], BF16, 'aos')
            self.pst = A.alloc([128, 2, 1], BF16, 'pst')
            self.stb = [A.alloc([128, 512], BF16, f'stb{i}') for i in range(2)]
            self.stf = [A.alloc([128, 512], F32, f'stf{i}') for i in range(2)]
            self.stb_rr = 0
            self.stf_rr = 0
            self.base_off = A.off
            n_layers = cfg.get('n_layers', NL)
            l0 = cfg.get('l0', 0)
            n_layers = l0 + n_layers
            self.b_ut = Buf('UT')
            self.b_ct = Buf('CT')
            self.b_zt = Buf('ZT'); self.b_xbc = Buf('XBC'); self.b_dt = Buf('DT'); self.b_xc = Buf('XC'); self.b_yt = Buf('YT')
            self.zs = A.alloc([128, 32, 1], F32, 'zs')
            self.xbcs = A.alloc([128, 48, 1], F32, 'xbcs')
            self.dts = A.alloc([128, 1], F32, 'dts')
            self.ys = A.alloc([128, 32, 1], F32, 'ys')
            self.us = A.alloc([128, DC, 1], F32, 'us')
            self.cs = A.alloc([128, DC, 1], F32, 'cs')
            self.base_off = A.off
            for r in range(l0, n_layers + 1):
                A.off = self.base_off
                seg = Seg(A, TG, f'pseg')
                pt = A.alloc([128, 2, TG], BF16, 'pt')
                ao = A.alloc([128, 32, TG], BF16, 'ao')
                for g in range(NG):
                    segs = [seg] + ([self.sseg] if g == 0 else [])
                    if r == l0:
                        self.dma('sp', seg.xf.ap, self.fm(dr['xT'], g), writes=[seg.xf])
                        self.dma('pool', seg.xb.ap, self.fm(dr['xT'], g), writes=[seg.xb])
                        if g == 0:
                            self.dma('sp', self.sseg.xf.ap[:, :, 0], dr['xsT'], writes=[self.sseg.xf])
                            self.dma('pool', self.sseg.xb.ap[:, :, 0], dr['xsT'], writes=[self.sseg.xb])
                    else:
                        l = r - 1
                        self.dma('sp', seg.xf.ap, dr['XS'][g], reads=[self.b_xs[g]], writes=[seg.xf])
                        self.mixer_out(l, segs, g, ao)
                        for sg in segs:
                            self.layernorm(sg, l * 4 + 1)
                        if cfg.get('stop') == f'pn{l}':
                            self.write_y(segs, g)
                            continue
                        self.ffn(segs, l, 1)
                        self.dma('pool', pt.ap, self.fm(dr['pT'][l], g), writes=[pt])
                        psrcs = [(pt, pt.ap)]
                        if g == 0:
                            self.dma('pool', self.pst.ap[:, :, 0], dr['psT'][l], writes=[self.pst])
                            psrcs.append((self.pst, self.pst.ap))
                        self.ple(segs, psrcs, l)
                    if r < n_layers:
                        self.ffn(segs, r, 0)
                        if cfg.get('stop') == f'ffn1_{r}':
                            self.write_y(segs, g)
                            continue
                        self.mixer_in(r, segs, g)
                        self.dma('sp', dr['XS'][g], seg.xf.ap, reads=[seg.xf], writes=[self.b_xs[g]])
                    else:
                        self.write_y(segs, g)
                if cfg.get('stop') in (f'ffn1_{r}', f'pn{r - 1}'):
                    break
                self.P.barrier()
                if r < n_layers:
                    A.off = self.base_off
                    self.mixer_core(r)
                    self.P.barrier()
            self.P.finish()
            sems = {}
            for nm in self.P.sem_names():
                sems[nm] = es.enter_context(self.nc.semaphore("s_" + "_".join(str(x) for x in nm)))
            block = es.enter_context(self.nc.Block())
            self.P.emit(block, sems)

    def write_y(self, segs, g):
        dr = self.dram
        self.dma('sp', self.fm(dr['yT'], g), segs[0].xf.ap, reads=[segs[0].xf], is_out=True)
        if len(segs) > 1:
            self.dma('sp', dr['ysT'], segs[1].xf.ap[:, :, 0], reads=[segs[1].xf], is_out=True)

    def mixer_in(self, l, segs, g):
        if l == 0:
            self.qkv_proj(segs, g, 'sbqkv', 'sbk', 'sbv')
        elif l == 3:
            self.qkv_proj(segs, g, 'mbqkv', 'mbk', 'mbv')
        elif l == 2:
            self.conf_in(segs, g)
        elif l == 1:
            self.ssm_in(segs, g)

    def mixer_out(self, l, segs, g, ao):
        dr = self.dram
        if l in (0, 3):
            ao16 = ao.ap[:, 0:16, :]
            self.dma('sp', ao16, dr['AO'][0:16].rearrange("k p t -> p k t")[:, :, g * TG:(g + 1) * TG],
                     reads=[self.b_ao], writes=[ao])
            srcs = [(ao, ao16)]
            if len(segs) > 1:
                srcs.append((self.aos, self.aos.ap[:, 0:16, :]))
            self.linear(segs, srcs, dr['sbo' if l == 0 else 'mbo'], DC, range(DC), self.resid_epi(segs))
        elif l == 1:
            self.ssm_out(segs, g, ao)
        elif l == 2:
            seg = segs[0]
            cv = self.cvec.ap
            self.dma('sp', seg.hf.ap, self.fm(dr['CT'], g), reads=[self.b_ct], writes=[seg.hf])
            gb = lambda k: (cv[:, 48 + k:49 + k], cv[:, 64 + k:65 + k])
            ao16 = ao.ap[:, 0:16, :]
            self.ln_generic(TG, seg.hf, seg.hf.ap, gb, None, (ao, ao16), AF.Silu)
            srcs = [(ao, ao16)]
            if len(segs) > 1:
                self.ln_generic(1, self.cs, self.cs.ap, gb, None, (self.aos, self.aos.ap[:, 0:16, :]), AF.Silu)
                srcs.append((self.aos, self.aos.ap[:, 0:16, :]))

            def epi(o, si, ps_t, ps_ap):
                sg = segs[si]
                tp = self.tmp('p' if sg.W > 1 else 's')
                ta = tp.ap[:, 0:sg.W]
                self.act(tp, ta, ps_t, ps_ap, AF.Identity, bias=cv[:, 80 + o:81 + o], reads=[self.cvec])
                xk = sg.xf.ap[:, o, 0:sg.W]
                self.dve(lambda e: e.scalar_tensor_tensor(xk, xk, ALPHA, ta, ALU.mult, ALU.add), [sg.xf, tp], [sg.xf])
            self.linear(segs, srcs, dr['cpw2'], DC, range(DC), epi)

    def ssm_in(self, segs, g):
        dr = self.dram
        gs = slice(g * TG, (g + 1) * TG)

        def epi(o, si, ps_t, ps_ap):
            if si == 0:
                st = self.stage(F32)
                self.act(st, st.ap, ps_t, ps_ap, AF.Copy)
                if o < 32:
                    self.dma('sp', dr['ZT'][o * 128:(o + 1) * 128, gs], st.ap, reads=[st], writes=[self.b_zt])
                elif o < 80:
                    self.dma('sp', dr['XBC'][(o - 32) * 128:(o - 31) * 128, gs], st.ap, reads=[st], writes=[self.b_xbc])
                else:
                    self.dma('sp', dr['DTr'][:, gs], st.ap[0:64, :], reads=[st], writes=[self.b_dt])
            else:
                if o < 32:
                    self.act(self.zs, self.zs.ap[:, o, :], ps_t, ps_ap, AF.Copy)
                elif o < 80:
                    self.act(self.xbcs, self.xbcs.ap[:, o - 32, :], ps_t, ps_ap, AF.Copy)
                else:
                    self.act(self.dts, self.dts.ap, ps_t, ps_ap, AF.Copy)
        srcs = [(sg.xb, sg.xb.ap) for sg in segs]
        self.linear(segs, srcs, dr['ssmin'], DC, range(81), epi)

    def ssm_out(self, segs, g, ao):
        dr = self.dram
        seg = segs[0]
        cv = self.cvec.ap
        gs = slice(g * TG, (g + 1) * TG)
        quarters = []
        for i in range(4):
            t = T(seg.hf.ap[:, i * 4:(i + 1) * 4, :], f'hfq{i}')
            quarters.append(t)
        def load(gq):
            y4 = quarters[(gq % 2) * 2]
            z4 = quarters[(gq % 2) * 2 + 1]
            self.dma('sp', y4.ap, dr['YT'][gq * 512:(gq + 1) * 512, gs].rearrange("(k p) t -> p k t", p=128), reads=[self.b_yt], writes=[y4, seg.h])
            self.dma('sp', z4.ap, dr['ZT'][gq * 512:(gq + 1) * 512, gs].rearrange("(k p) t -> p k t", p=128), reads=[self.b_zt], writes=[z4, seg.h])
        load(0)
        for gq in range(8):
            if gq + 1 < 8:
                load(gq + 1)
            y4 = quarters[(gq % 2) * 2]
            z4 = quarters[(gq % 2) * 2 + 1]
            items = [(TG, 'p', y4, y4.ap, z4, z4.ap, ao, ao.ap[:, gq * 4:(gq + 1) * 4, :], gq)]
            if len(segs) > 1:
                items.append((1, 's', self.ys, self.ys.ap[:, gq * 4:(gq + 1) * 4, :], self.zs, self.zs.ap[:, gq * 4:(gq + 1) * 4, :],
                              self.aos, self.aos.ap[:, gq * 4:(gq + 1) * 4, :], gq))
            self._ssm_norm_items(items, cv)
        srcs = [(ao, ao.ap)]
        if len(segs) > 1:
            srcs.append((self.aos, self.aos.ap))
        self.linear(segs, srcs, dr['ssmout'], 32, range(DC), self.resid_epi(segs))

    def _ssm_norm_items(self, items, cv):
        for (W, kd, y_t, y_ap, z_t, z_ap, o_t, o_ap, gq) in items:
            ps = self.psum(kd)
            for k in range(4):
                sz = self.tmp(kd)
                sza = sz.ap[:, 0:W]
                self.act(sz, sza, z_t, z_ap[:, k, :], AF.Silu)
                yk = y_ap[:, k, :]
                self.dve(lambda e, yk=yk, sza=sza: e.tensor_tensor(yk, yk, sza, ALU.mult), [y_t, sz], [y_t])
                sq = self.tmp(kd)
                sqa = sq.ap[:, 0:W]
                self.dve(lambda e, yk=yk, sqa=sqa: e.tensor_tensor(sqa, yk, yk, ALU.mult), [y_t], [sq])
                self.mm(ps, ps.ap[:, 0:W], self.c_ones, self.c_ones.ap, sq, sqa, k == 0, k == 3)
            rs = self.ln_tiles[kd][2]
            ra = rs.ap[:, 0:W]
            self.dve(lambda e, ra=ra, p=ps.ap[:, 0:W]: e.tensor_scalar(ra, p, 1.0 / 512, None, ALU.mult), [ps], [rs])
            self.act(rs, ra, rs, ra, AF.Sqrt, bias=self.c_eps.ap[:, 0:1])
            self.dve(lambda e, ra=ra: e.reciprocal(ra, ra), [rs], [rs])
            for k in range(4):
                t1 = self.tmp(kd)
                ta = t1.ap[:, 0:W]
                yk = y_ap[:, k, :]
                self.dve(lambda e, ta=ta, yk=yk, ra=ra: e.tensor_tensor(ta, yk, ra, ALU.mult), [y_t, rs], [t1])
                gcol = cv[:, 1328 + gq * 4 + k:1329 + gq * 4 + k]
                self.dve(lambda e, o=o_ap[:, k, :], ta=ta, gcol=gcol: e.tensor_scalar(o, ta, gcol, None, ALU.mult), [t1, self.cvec], [o_t])

    def ssm_core(self):
        A = self.A
        dr = self.dram
        cv = self.cvec.ap
        sv = A.alloc([128, 8], F32, 'sv')
        mark = A.off

        def bcl(ap, n):
            return bass.AP(ap.tensor, ap.offset, [list(x) for x in ap.ap] + [[0, n]])

        def bcm(ap, m):
            l = [list(x) for x in ap.ap]
            return bass.AP(ap.tensor, ap.offset, [l[0], [0, m]] + l[1:])
        xpad = [A.alloc([128, 3 + T_], F32, f'xpad{i}') for i in range(2)]
        acc = [A.alloc([128, T_], F32, f'sacc{i}') for i in range(2)]
        for k in range(48):
            up, ac = xpad[k % 2], acc[k % 2]
            self.dve(lambda e, a=up.ap[:, 0:3]: e.memset(a, 0.0), [], [up])
            self.dma('sp', up.ap[:, 3:3 + T_], dr['XBC'][k * 128:(k + 1) * 128, :], reads=[self.b_xbc], writes=[up])
            wcol = lambda w: cv[:, 1088 + w * 48 + k:1089 + w * 48 + k]
            self.dve(lambda e, o=ac.ap, i=up.ap[:, 0:T_], w0=wcol(0), b=cv[:, 1280 + k:1281 + k]: e.tensor_scalar(o, i, w0, b, ALU.mult, ALU.add),
                     [up, self.cvec], [ac])
            for w in range(1, 4):
                self.dve(lambda e, o=ac.ap, i=up.ap[:, w:w + T_], ww=wcol(w): e.scalar_tensor_tensor(o, i, ww, o, ALU.mult, ALU.add),
                         [up, ac, self.cvec], [ac])
            self.act(ac, ac.ap, ac, ac.ap, AF.Silu)
            self.dma('sp', dr['XC'][k * 128:(k + 1) * 128, :], ac.ap, reads=[ac], writes=[self.b_xc])
        self.dma('sp', dr['scT'], dr['XBC'][:, T_ - 3:T_], reads=[self.b_xbc], is_out=True)
        self.P.barrier()
        A.off = mark
        ident = self.c_ident
        ones = self.c_ones
        tri_le = self.c_tri_le
        tri_gt = self.c_tri_gt
        self.act(sv, sv.ap[0:64, 0:1], self.cvec, cv[0:64, 1361:1362], AF.Exp)
        self.dve(lambda e: e.tensor_scalar(sv.ap[0:64, 0:1], sv.ap[0:64, 0:1], -1.0, None, ALU.mult), [sv], [sv])
        a_col = sv.ap[0:64, 0:1]
        hT = A.alloc([128, 4096], F32, 'hT')
        hTb = A.alloc([128, 4096], BF16, 'hTb')
        self.dve(lambda e: e.memset(hT.ap, 0.0), [], [hT])
        self.dve(lambda e: e.memset(hTb.ap, 0.0), [], [hTb])
        xcs = A.alloc([128, 48, 128], F32, 'xcs')
        xcb = A.alloc([128, 16, 128], BF16, 'xcb')
        Xtm = A.alloc([128, 4096], BF16, 'Xtm')
        Xw = A.alloc([128, 4096], BF16, 'Xw')
        yTo = A.alloc([128, 32, 128], F32, 'yTo')
        Btm = A.alloc([128, 8, 128], BF16, 'Btm')
        CBm = A.alloc([128, 8, 128], F32, 'CBm')
        dtf = A.alloc([128, 128], F32, 'dtf')
        daf = A.alloc([128, 128], F32, 'daf')
        dtda = A.alloc([128, 128], F32, 'dtda')
        ecum = A.alloc([128, 64], F32, 'ecum')
        edec = A.alloc([128, 64], F32, 'edec')
        totb = A.alloc([128, 64], F32, 'totb')
        wtm = A.alloc([128, 64], F32, 'wtm')
        Ut = [A.alloc([128, 128], F32, f'Ut{i}') for i in range(16)]
        Et = [A.alloc([128, 512], F32, f'Et{i}') for i in range(4)]
        Mt = [A.alloc([128, 128], BF16, f'Mt{i}') for i in range(8)]
        ytmp = [A.alloc([128, 512], F32, f'ytmp{i}') for i in range(2)]
        ytm = [A.alloc([128, 512], F32, f'ytm{i}') for i in range(2)]
        dt_tm = dtda.ap[:, 0:64]
        da_tm = dtda.ap[:, 64:128]
        id64 = ident.ap[0:64, 0:64]
        rr = 0
        for c in range(16):
            cs = slice(c * 128, (c + 1) * 128)
            self.dma('sp', xcs.ap, dr['XC'].rearrange("(k p) t -> p k t", p=128)[:, :, cs], reads=[self.b_xc], writes=[xcs])
            self.dma('sp', dtf.ap[0:64, :], dr['DTr'][:, cs], reads=[self.b_dt], writes=[dtf])
            self.act(dtf, dtf.ap[0:64, :], dtf, dtf.ap[0:64, :], AF.Exp, bias=cv[0:64, 1360:1361], reads=[self.cvec])
            self.act(dtf, dtf.ap[0:64, :], dtf, dtf.ap[0:64, :], AF.Ln, bias=ones.ap[0:64, 0:1])
            self.dve(lambda e: e.tensor_scalar(daf.ap[0:64, :], dtf.ap[0:64, :], a_col, None, ALU.mult), [dtf, sv], [daf])
            pA = self.psum()
            self.tr(pA, pA.ap[:, 0:64], dtf, dtf.ap[0:64, :], id64)
            self.tr(pA, pA.ap[:, 64:128], daf, daf.ap[0:64, :], id64)
            self.act(dtda, dtda.ap, pA, pA.ap[:, 0:128], AF.Copy)
            pB = self.psum()
            self.mm(pB, pB.ap[:, 0:64], tri_le, tri_le.ap, dtda, da_tm, True, True)
            self.mm(pB, pB.ap[:, 64:128], ones, ones.ap, dtda, da_tm, True, True)
            self.act(ecum, ecum.ap, pB, pB.ap[:, 0:64], AF.Exp)
            self.act(edec, edec.ap, pB, pB.ap[:, 64:128], AF.Exp)
            self.act(totb, totb.ap, pB, pB.ap[:, 64:128], AF.Copy)
            self.dve(lambda e, p=pB.ap[:, 0:64]: e.tensor_tensor(wtm.ap, totb.ap, p, ALU.subtract), [totb, pB], [wtm])
            self.act(wtm, wtm.ap, wtm, wtm.ap, AF.Exp)
            self.dve(lambda e: e.tensor_tensor(wtm.ap, wtm.ap, dt_tm, ALU.mult), [wtm, dtda], [wtm])
            self.act(xcb, xcb.ap, xcs, xcs.ap[:, 32:48, :], AF.Copy)
            for kb in range(8):
                pt_ = self.psum()
                for q in range(4):
                    self.tr(pt_, pt_.ap[:, q * 128:(q + 1) * 128], xcs, xcs.ap[:, kb * 4 + q, :], ident.ap)
                xo = Xtm.ap[:, kb * 512:(kb + 1) * 512]
                if kb % 2 == 0:
                    self.act(Xtm, xo, pt_, pt_.ap, AF.Copy)
                else:
                    self.dve(lambda e, xo=xo, p=pt_.ap: e.tensor_copy(xo, p), [pt_], [Xtm])
            for kb in range(2):
                pt_ = self.psum()
                for q in range(4):
                    self.tr(pt_, pt_.ap[:, q * 128:(q + 1) * 128], xcs, xcs.ap[:, 32 + kb * 4 + q, :], ident.ap)
                self.act(Btm, Btm.ap[:, kb * 4:(kb + 1) * 4, :], pt_, pt_.ap.rearrange("p (k c) -> p k c", k=4), AF.Copy)
            self.dve(lambda e: e.tensor_tensor(Xw.ap.rearrange("p (h q) -> p h q", h=64), Xtm.ap.rearrange("p (h q) -> p h q", h=64),
                                               bcl(wtm.ap, 64), ALU.mult), [Xtm, wtm], [Xw])
            for g4 in range(2):
                pC = self.psum()
                for q in range(4):
                    g = g4 * 4 + q
                    self.mm(pC, pC.ap[:, q * 128:(q + 1) * 128], xcb, xcb.ap[:, g, :], xcb, xcb.ap[:, 8 + g, :], True, True)
                self.dve(lambda e, o=CBm.ap[:, g4 * 4:(g4 + 1) * 4, :], p=pC.ap.rearrange("p (k c) -> p k c", k=4):
                         e.tensor_tensor(o, p, bcm(tri_le.ap, 4), ALU.mult), [pC, self.c_all], [CBm])
            def s1(g):
                gsl = slice(g * 512, (g + 1) * 512)
                pY = self.ps[g % 2]
                self.mm(pY, pY.ap, xcb, xcb.ap[:, 8 + g, :], hTb, hTb.ap[:, gsl], True, True)
                yt_ = ytmp[g % 2]
                self.dve(lambda e, o=yt_.ap.rearrange("p (h q) -> p h q", h=8), p=pY.ap.rearrange("p (h q) -> p h q", h=8),
                         b=bcl(ecum.ap[:, g * 8:(g + 1) * 8], 64): e.tensor_tensor(o, p, b, ALU.mult), [pY, ecum], [yt_])
                for hb in range(2):
                    pD = self.ps[2 + hb]
                    for q in range(4):
                        h = g * 8 + hb * 4 + q
                        U = Ut[(g % 2) * 8 + hb * 4 + q]
                        self.dve(lambda e, u=U.ap, sc=da_tm[:, h:h + 1]: e.tensor_scalar(u, tri_gt.ap, sc, None, ALU.mult),
                                 [self.c_all, dtda], [U])
                        self.mm(pD, pD.ap[:, q * 128:(q + 1) * 128], U, U.ap, tri_le, tri_le.ap, True, True)
                    E = Et[(g % 2) * 2 + hb]
                    self.act(E, E.ap, pD, pD.ap, AF.Exp)

            def s2(g):
                gsl = slice(g * 512, (g + 1) * 512)
                yt_ = ytmp[g % 2]
                pI = self.ps[4]
                for hb in range(2):
                    E = Et[(g % 2) * 2 + hb]
                    for q in range(4):
                        h = g * 8 + hb * 4 + q
                        M = Mt[hb * 4 + q]
                        self.dve(lambda e, m=M.ap, ea=E.ap[:, q * 128:(q + 1) * 128], sc=dt_tm[:, h:h + 1], cb=CBm.ap[:, g, :]:
                                 e.scalar_tensor_tensor(m, ea, sc, cb, ALU.mult, ALU.mult), [E, dtda, CBm], [M])
                        co = (hb * 4 + q) * 64
                        self.mm(pI, pI.ap[:, co:co + 64], M, M.ap, Xtm, Xtm.ap[:, h * 64:(h + 1) * 64], True, True)
                ym = ytm[g % 2]
                self.dve(lambda e, o=ym.ap, a=yt_.ap, p=pI.ap: e.tensor_tensor(o, a, p, ALU.add), [yt_, pI], [ym])
                pT = self.ps[5]
                for q in range(4):
                    self.tr(pT, pT.ap[:, q * 128:(q + 1) * 128], ym, ym.ap[:, q * 128:(q + 1) * 128], ident.ap)
                for q in range(4):
                    k = g * 4 + q
                    self.dve(lambda e, o=yTo.ap[:, k, :], x=xcs.ap[:, k, :], dcol=cv[:, 1555 + k:1556 + k], p=pT.ap[:, q * 128:(q + 1) * 128]:
                             e.scalar_tensor_tensor(o, x, dcol, p, ALU.mult, ALU.add), [xcs, self.cvec, pT], [yTo])
                pS = self.ps[6]
                self.mm(pS, pS.ap, Btm, Btm.ap[:, g, :], Xw, Xw.ap[:, gsl], True, True)
                hv = hT.ap[:, gsl].rearrange("p (h q) -> p h q", h=8)
                self.dve(lambda e, hv=hv, b=bcl(edec.ap[:, g * 8:(g + 1) * 8], 64): e.tensor_tensor(hv, hv, b, ALU.mult), [hT, edec], [hT])
                self.dve(lambda e, hg=hT.ap[:, gsl], p=pS.ap: e.tensor_tensor(hg, hg, p, ALU.add), [hT, pS], [hT])
                self.act(hTb, hTb.ap[:, gsl], hT, hT.ap[:, gsl], AF.Copy)
            s1(0)
            for g in range(8):
                if g + 1 < 8:
                    s1(g + 1)
                s2(g)
            self.dma('sp', dr['YT'].rearrange("(k p) t -> p k t", p=128)[:, :, cs], yTo.ap, reads=[yTo], writes=[self.b_yt])
        for kb in range(8):
            pt_ = self.psum()
            for q in range(4):
                k = kb * 4 + q
                self.tr(pt_, pt_.ap[:, q * 128:(q + 1) * 128], hT, hT.ap[:, k * 128:(k + 1) * 128], ident.ap)
            self.act(yTo, yTo.ap[:, kb * 4:(kb + 1) * 4, :], pt_, pt_.ap.rearrange("p (k c) -> p k c", k=4), AF.Copy)
        self.dma('sp', dr['shp'].rearrange("(k p) n -> p k n", p=128), yTo.ap, reads=[yTo], is_out=True)
        self.P.barrier()
        A.off = mark
        ups = A.alloc([128, 48, 4], F32, 'sups')
        tp3 = A.alloc([128, 48, 4], F32, 'stp3')
        xcv = A.alloc([128, 48], F32, 'xcv')
        self.dma('sp', ups.ap[:, :, 0:3], dr['scstT'], writes=[ups])
        self.dve(lambda e: e.tensor_copy(ups.ap[:, :, 3:4], self.xbcs.ap), [self.xbcs], [ups])
        w2 = cv[:, 1363:1555].rearrange("p (k w) -> p k w", k=48)
        self.dve(lambda e: e.tensor_tensor(tp3.ap, ups.ap, w2, ALU.mult), [ups, self.cvec], [tp3])
        self.dve(lambda e: e.tensor_reduce(xcv.ap, tp3.ap, AX.X, ALU.add), [tp3], [xcv])
        self.dve(lambda e: e.tensor_tensor(xcv.ap, xcv.ap, cv[:, 1280:1328], ALU.add), [xcv, self.cvec], [xcv])
        self.act(xcv, xcv.ap, xcv, xcv.ap, AF.Silu)
        self.dma('sp', dr['scsT'], ups.ap[:, :, 1:4], reads=[ups], is_out=True)
        dd = A.alloc([128, 2], F32, 'dd')
        self.act(dd, dd.ap[0:64, 0:1], self.dts, self.dts.ap[0:64, :], AF.Exp, bias=cv[0:64, 1360:1361], reads=[self.cvec])
        self.act(dd, dd.ap[0:64, 0:1], dd, dd.ap[0:64, 0:1], AF.Ln, bias=ones.ap[0:64, 0:1])
        self.dve(lambda e: e.tensor_scalar(dd.ap[0:64, 1:2], dd.ap[0:64, 0:1], a_col, None, ALU.mult), [dd, sv], [dd])
        Gx = A.alloc([64, 4096], F32, 'Gx')
        self.dma('sp', Gx.ap, dr['gexp'], writes=[Gx])
        pE = self.psum()
        for k in range(32):
            self.mm(pE, pE.ap[:, k * 2:(k + 1) * 2], Gx, Gx.ap[:, k * 128:(k + 1) * 128], dd, dd.ap[0:64, 0:2], True, True)
        ex = A.alloc([128, 32, 2], F32, 'ex')
        self.act(ex, ex.ap, pE, pE.ap[:, 0:64].rearrange("p (k c) -> p k c", k=32), AF.Copy)
        edA = A.alloc([128, 32], F32, 'edA')
        dtx = A.alloc([128, 32], F32, 'dtx')
        self.act(edA, edA.ap, ex, ex.ap[:, :, 1], AF.Exp)
        self.dve(lambda e: e.tensor_tensor(dtx.ap, ex.ap[:, :, 0], xcv.ap[:, 0:32], ALU.mult), [ex, xcv], [dtx])
        BC = A.alloc([128, 2, 8, 128], F32, 'BC')
        dg = [A.alloc([128, 128], F32, f'dg{i}') for i in range(2)]
        i_ = 0
        for g in range(8):
            for wh, off in ((0, 32), (1, 40)):
                d_ = dg[i_ % 2]
                i_ += 1
                self.dve(lambda e, d=d_.ap, sc=xcv.ap[:, off + g:off + g + 1]: e.tensor_scalar(d, ident.ap, sc, None, ALU.mult),
                         [self.c_all, xcv], [d_])
                pb = self.psum()
                self.mm(pb, pb.ap[:, 0:128], ones, ones.ap, d_, d_.ap, True, True)
                self.act(BC, BC.ap[:, wh, g, :], pb, pb.ap[:, 0:128], AF.Copy)
        h0 = A.alloc([128, 32, 128], F32, 'h0')
        self.dma('sp', h0.ap, dr['sst'].rearrange("(k p) n -> p k n", p=128), writes=[h0])
        tn = [A.alloc([128, 128], F32, f'tn{i}') for i in range(2)]
        for k in range(32):
            g = k // 4
            hk = h0.ap[:, k, :]
            self.dve(lambda e, hk=hk, sc=edA.ap[:, k:k + 1]: e.tensor_scalar(hk, hk, sc, None, ALU.mult), [h0, edA], [h0])
            self.dve(lambda e, hk=hk, b=BC.ap[:, 0, g, :], sc=dtx.ap[:, k:k + 1]: e.scalar_tensor_tensor(hk, b, sc, hk, ALU.mult, ALU.add),
                     [h0, BC, dtx], [h0])
            t_ = tn[k % 2]
            self.dve(lambda e, t=t_.ap, hk=hk, cc=BC.ap[:, 1, g, :]: e.tensor_tensor(t, hk, cc, ALU.mult), [h0, BC], [t_])
            self.dve(lambda e, o=self.ys.ap[:, k, :], t=t_.ap: e.tensor_reduce(o, t, AX.X, ALU.add), [t_], [self.ys])
        ysa = self.ys.ap[:, :, 0]
        self.dve(lambda e: e.tensor_tensor(dtx.ap, xcv.ap[:, 0:32], cv[:, 1555:1587], ALU.mult), [xcv, self.cvec], [dtx])
        self.dve(lambda e: e.tensor_tensor(ysa, ysa, dtx.ap, ALU.add), [self.ys, dtx], [self.ys])
        self.dma('sp', dr['shs'].rearrange("(k p) n -> p k n", p=128), h0.ap, reads=[h0], is_out=True)

    def conf_in(self, segs, g):
        dr = self.dram
        w = dr['cpw1']
        cv = self.cvec.ap
        for oc in range(DC):
            s1 = self.wslot()
            s2 = self.wslot()
            a1 = s1.ap.rearrange("p (k c) -> p k c", k=16)
            a2 = s2.ap.rearrange("p (k c) -> p k c", k=16)
            self.dma('pool', a1, w[oc], writes=[s1])
            self.dma('pool', a2, w[oc + 16], writes=[s2])
            for si, sg in enumerate(segs):
                W = sg.W
                kd = 'p' if W > 1 else 's'
                pa = self.psum(kd)
                pg = self.psum(kd)
                for k in range(DC):
                    self.mm(pa, pa.ap[:, 0:W], s1, a1[:, k, :], sg.xb, sg.xb.ap[:, k, 0:W], k == 0, k == DC - 1)
                for k in range(DC):
                    self.mm(pg, pg.ap[:, 0:W], s2, a2[:, k, :], sg.xb, sg.xb.ap[:, k, 0:W], k == 0, k == DC - 1)
                sgm = self.tmp(kd)
                sa = sgm.ap[:, 0:W]
                self.act(sgm, sa, pg, pg.ap[:, 0:W], AF.Sigmoid, bias=cv[:, 16 + oc:17 + oc], reads=[self.cvec])
                paa = pa.ap[:, 0:W]
                ba = cv[:, oc:oc + 1]
                if si == 0:
                    st = self.stage(F32)
                    self.dve(lambda e, o=st.ap, paa=paa, ba=ba, sa=sa: e.scalar_tensor_tensor(o, paa, ba, sa, ALU.add, ALU.mult),
                             [pa, sgm, self.cvec], [st])
                    self.dma('sp', dr['UT'][oc * 128:(oc + 1) * 128, g * TG:(g + 1) * TG], st.ap, reads=[st], writes=[self.b_ut])
                else:
                    self.dve(lambda e, o=self.us.ap[:, oc, :], paa=paa, ba=ba, sa=sa: e.scalar_tensor_tensor(o, paa, ba, sa, ALU.add, ALU.mult),
                             [pa, sgm, self.cvec], [self.us])

    def conf_core(self):
        A = self.A
        dr = self.dram
        cv = self.cvec.ap
        upad = [A.alloc([128, 30 + T_], F32, f'upad{i}') for i in range(2)]
        acc = [A.alloc([128, T_], F32, f'acc{i}') for i in range(2)]
        for k in range(DC):
            up, ac = upad[k % 2], acc[k % 2]
            self.dve(lambda e, a=up.ap[:, 0:30]: e.memset(a, 0.0), [], [up])
            self.dma('sp', up.ap[:, 30:30 + T_], dr['UT'][k * 128:(k + 1) * 128, :], reads=[self.b_ut], writes=[up])
            wcol = lambda w: cv[:, 96 + w * 16 + k:97 + w * 16 + k]
            self.dve(lambda e, o=ac.ap, i=up.ap[:, 0:T_], w0=wcol(0), b=cv[:, 32 + k:33 + k]: e.tensor_scalar(o, i, w0, b, ALU.mult, ALU.add),
                     [up, self.cvec], [ac])
            for w in range(1, 31):
                self.dve(lambda e, o=ac.ap, i=up.ap[:, w:w + T_], ww=wcol(w): e.scalar_tensor_tensor(o, i, ww, o, ALU.mult, ALU.add),
                         [up, ac, self.cvec], [ac])
            self.dma('sp', dr['CT'][k * 128:(k + 1) * 128, :], ac.ap, reads=[ac], writes=[self.b_ct])
        self.dma('sp', dr['ccT'], dr['UT'][:, T_ - 30:T_], reads=[self.b_ut], is_out=True)
        ups = A.alloc([128, DC, 31], F32, 'ups')
        tp3 = A.alloc([128, DC, 31], F32, 'tp3')
        self.dma('sp', ups.ap[:, :, 0:30], dr['cstT'], writes=[ups])
        self.dve(lambda e: e.tensor_copy(ups.ap[:, :, 30:31], self.us.ap), [self.us], [ups])
        w2 = cv[:, 592:592 + 496].rearrange("p (k w) -> p k w", k=DC)
        self.dve(lambda e: e.tensor_tensor(tp3.ap, ups.ap, w2, ALU.mult), [ups, self.cvec], [tp3])
        csa = self.cs.ap[:, :, 0]
        self.dve(lambda e: e.tensor_reduce(csa, tp3.ap, AX.X, ALU.add), [tp3], [self.cs])
        self.dve(lambda e: e.tensor_tensor(csa, csa, cv[:, 32:48], ALU.add), [self.cs, self.cvec], [self.cs])
        self.dma('sp', dr['ccsT'], ups.ap[:, :, 1:31], reads=[ups], is_out=True)

    def mixer_core(self, l):
        if l == 1:
            self.ssm_core()
        if l == 2:
            self.conf_core()
        if l in (0, 3):
            kind = 'sb' if l == 0 else 'mb'
            mark = self.A.off
            self.attn_prompt(kind)
            self.P.barrier()
            self.A.off = mark
            if not self.cfg.get('no_sample_attn'):
                self.attn_sample(kind)

    def attn_sample(self, kind):
        A = self.A
        dr = self.dram
        ident = self.c_ident
        ones = self.c_ones
        c2d = dr['consts2']
        c2 = A.alloc([128, 2048], F32, 'c2')
        self.dma('sp', c2.ap, c2d, writes=[c2])
        Mx = c2.ap[:, 0:128]
        Hsame = c2.ap[:, 128:256]
        bmask = c2.ap[0:16, 256:260]
        Lall = c2.ap[0:16, 512:1536]
        kc_d = dr[kind + 'kc']
        vc_d = dr[kind + 'vc']
        ptd = dr['ptab']
        ptb = A.alloc([128, 128], I32, 'ptb')
        idx = A.alloc([128, 128], I32, 'idx')
        self.dma('sp', ptb.ap, bass.AP(ptd.tensor, 0, [[0, 128], [1, 128]]), writes=[ptb])
        siota = self.c_all.ap[:, 257:258]
        self.dve(lambda e: e.tensor_scalar(idx.ap, ptb.ap, 128.0, siota, ALU.mult, ALU.add), [ptb, self.c_all], [idx])
        qbc = A.alloc([128, 16, 128], F32, 'qbc')
        dg = [A.alloc([128, 128], F32, f'adg{i}') for i in range(2)]
        for hb in range(4):
            pb = self.psum()
            for q in range(4):
                h = hb * 4 + q
                d_ = dg[h % 2]
                self.dve(lambda e, d=d_.ap, sc=self.qs.ap[:, h, :]: e.tensor_scalar(d, ident.ap, sc, None, ALU.mult), [self.c_all, self.qs], [d_])
                self.mm(pb, pb.ap[:, q * 128:(q + 1) * 128], ones, ones.ap, d_, d_.ap, True, True)
            self.act(qbc, qbc.ap[:, hb * 4:(hb + 1) * 4, :], pb, pb.ap.rearrange("p (k c) -> p k c", k=4), AF.Copy)
        KP = [A.alloc([128, 512], F32, f'KP{i}') for i in range(4)]
        prod = [A.alloc([128, 16, 128], F32, f'prod{i}') for i in range(2)]
        zT2 = A.alloc([128, 16, 8, 16], F32, 'zT2')
        q4 = qbc.ap.rearrange("p (a b) d -> p a b d", a=4)

        def gather(dst, src_d, i):
            ia = idx.ap[:, i:i + 1]
            return self.P.op('pool', lambda e: e.indirect_dma_start(out=dst.ap, out_offset=None, in_=src_d,
                                                                    in_offset=bass.IndirectOffsetOnAxis(ap=ia, axis=0)),
                             reads=[idx], writes=[dst], dma=True)
        for i in range(128):
            g8, pl = divmod(i, 16)
            kp = KP[i % 4]
            gather(kp, kc_d, i)
            pr = prod[i % 2]
            ka = kp.ap
            kl = [list(x) for x in ka.ap]
            k4 = bass.AP(ka.tensor, ka.offset, [kl[0], [128, 4], [0, 4], [1, 128]])
            self.dve(lambda e, o=pr.ap.rearrange("p (a b) d -> p a b d", a=4), k4=k4: e.tensor_tensor(o, k4, q4, ALU.mult), [kp, qbc], [pr])
            self.dve(lambda e, o=zT2.ap[:, pl, g8, :], p=pr.ap: e.tensor_reduce(o, p, AX.X, ALU.add), [pr], [zT2])
        Z = A.alloc([128, 2048], F32, 'sZ')
        E = A.alloc([128, 2048], F32, 'sE')
        ZS = A.alloc([128, 2048], F32, 'sZS')
        for pb4 in range(4):
            pt_ = self.psum()
            for q in range(4):
                pl = pb4 * 4 + q
                self.tr(pt_, pt_.ap[:, q * 128:(q + 1) * 128], zT2, zT2.ap[:, pl].rearrange("p a b -> p (a b)"), ident.ap)
            self.act(Z, Z.ap[:, pb4 * 512:(pb4 + 1) * 512], pt_, pt_.ap, AF.Copy)
        sm = A.alloc([128, 64], F32, 'ssm')
        if kind == 'sb':
            C = A.alloc([128, 2048], F32, 'sC')
            o2k = A.alloc([128, 2048], F32, 'so2k')
            self.dve(lambda e: e.memset(o2k.ap, 1.0), [], [o2k])
            self.act(E, E.ap, Z, Z.ap, AF.Exp, scale=SCALE)
            self.act(E, E.ap, E, E.ap, AF.Ln, bias=ones.ap[:, 0:1])
            self.dve(lambda e: e.scalar_tensor_tensor(ZS.ap, Z.ap, SCALE, E.ap, ALU.mult, ALU.subtract), [Z, E], [ZS])
            self.dve(lambda e: e.tensor_tensor_scan(C.ap, o2k.ap, E.ap, 0.0, ALU.mult, ALU.add), [E, o2k], [C])
            tot = sm.ap[:, 0:1]
            self.dve(lambda e: e.tensor_copy(tot, C.ap[:, 2047:2048]), [C], [sm])
            pc = self.psum()
            self.mm(pc, pc.ap[:, 0:1], c2, Mx, sm, tot, True, True)
            nb = sm.ap[:, 1:2]
            self.dve(lambda e: e.scalar_tensor_tensor(nb, pc.ap[:, 0:1], -1.0, tot, ALU.mult, ALU.subtract), [pc, sm], [sm])
            self.dve(lambda e: e.tensor_tensor(ZS.ap, ZS.ap, C.ap, ALU.add), [ZS, C], [ZS])
            Pm = Z
            self.act(Pm, Pm.ap, ZS, ZS.ap, AF.Exp, bias=nb, reads=[sm])
        else:
            grow = sm.ap[:, 0:8]
            self.dve(lambda e: e.tensor_reduce(grow, Z.ap.rearrange("p (b s) -> p b s", b=8), AX.X, ALU.add), [Z], [sm])
            pg = self.psum()
            for g8 in range(8):
                self.mm(pg, pg.ap[0:16, g8 * 8:(g8 + 1) * 8], self.c_all, ident.ap[:, g8 * 16:(g8 + 1) * 16], sm, grow, True, True)
            s16 = A.alloc([16, 160], F32, 's16')
            g16 = s16.ap[:, 0:64]
            self.act(s16, g16, pg, pg.ap[0:16, 0:64], AF.Copy)
            t8 = s16.ap[:, 64:72]
            self.dve(lambda e: e.max(t8, g16), [s16], [s16])
            sb16 = s16.ap[:, 80:144]
            self.dve(lambda e: e.tensor_scalar(sb16, g16, t8[:, 2:3], -NEG, ALU.is_ge, ALU.mult), [s16], [s16])
            self.dve(lambda e: e.tensor_scalar(sb16, sb16, NEG, None, ALU.add), [s16], [s16])
            psb = self.psum()
            for g8 in range(8):
                self.mm(psb, psb.ap[:, 0:8], c2, Lall[:, g8 * 128:(g8 + 1) * 128], s16, sb16[:, g8 * 8:(g8 + 1) * 8], g8 == 0, g8 == 7)
            sbr = sm.ap[:, 8:16]
            self.act(sm, sbr, psb, psb.ap[:, 0:8], AF.Copy)
            for bl in range(8):
                self.dve(lambda e, o=E.ap[:, bl * 256:(bl + 1) * 256], z=Z.ap[:, bl * 256:(bl + 1) * 256], b=sbr[:, bl:bl + 1]:
                         e.tensor_scalar(o, z, SCALE, b, ALU.mult, ALU.add), [Z, sm], [E])
            p16 = A.alloc([128, 16], F32, 'p16')
            ksa = self.kvs.ap[:, 0:4, 0]
            kl2 = [list(x) for x in ksa.ap]
            k44 = bass.AP(ksa.tensor, ksa.offset, [kl2[0], kl2[1], [0, 4]])
            self.dve(lambda e: e.tensor_tensor(p16.ap.rearrange("p (a b) -> p a b", a=4), self.qs.ap[:, :, 0].rearrange("p (a b) -> p a b", a=4), k44, ALU.mult),
                     [self.qs, self.kvs], [p16])
            prep = A.alloc([128, 8, 16], F32, 'prep')
            pl_ = [list(x) for x in p16.ap.ap]
            p16b = bass.AP(p16.ap.tensor, p16.ap.offset, [pl_[0], [0, 8], pl_[1]])
            self.dve(lambda e: e.tensor_copy(prep.ap, p16b), [p16], [prep])
            pz = self.psum()
            self.mm(pz, pz.ap[:, 0:1], prep, prep.ap.rearrange("p a b -> p (a b)"), self.c_all, ones.ap[:, 0:1], True, True)
            negm = sm.ap[:, 16:17]
            self.dve(lambda e: e.tensor_scalar(negm, pz.ap[:, 0:1], -SCALE, None, ALU.mult), [pz], [sm])
            self.act(ZS, ZS.ap, E, E.ap, AF.Exp, bias=negm, reads=[sm])
            rs = sm.ap[:, 17:18]
            self.dve(lambda e: e.tensor_reduce(rs, ZS.ap, AX.X, ALU.add), [ZS], [sm])
            pd = self.psum()
            self.mm(pd, pd.ap[:, 0:1], c2, Hsame, sm, rs, True, True)
            rden = sm.ap[:, 18:19]
            self.dve(lambda e: e.tensor_scalar(rden, pd.ap[:, 0:1], 1.0, None, ALU.add), [pd], [sm])
            self.dve(lambda e: e.reciprocal(rden, rden), [sm], [sm])
            Pm = Z
            self.dve(lambda e: e.tensor_scalar(Pm.ap, ZS.ap, rden, None, ALU.mult), [ZS, sm], [Pm])
            Rm = dg[0]
            self.dve(lambda e: e.tensor_scalar(Rm.ap, ones.ap, rden, None, ALU.mult), [self.c_all, sm], [Rm])
            pbc = self.psum()
            self.mm(pbc, pbc.ap[:, 0:16], Rm, Rm.ap, self.c_all, ident.ap[:, 0:16], True, True)
            pown = A.alloc([128, 16], F32, 'pown')
            self.act(pown, pown.ap, pbc, pbc.ap[:, 0:16], AF.Copy)
            diagV = A.alloc([128, 4, 128], F32, 'diagV')
            for kvh in range(4):
                self.dve(lambda e, o=diagV.ap[:, kvh, :], sc=self.kvs.ap[:, 4 + kvh, :]: e.tensor_scalar(o, ident.ap, sc, None, ALU.mult),
                         [self.c_all, self.kvs], [diagV])
        PT = A.alloc([128, 16, 128], F32, 'sPT')
        for pb4 in range(4):
            pt_ = self.psum()
            for q in range(4):
                pl = pb4 * 4 + q
                self.tr(pt_, pt_.ap[:, q * 128:(q + 1) * 128], Pm, Pm.ap[:, pl * 128:(pl + 1) * 128], ident.ap)
            self.act(PT, PT.ap[:, pb4 * 4:(pb4 + 1) * 4, :], pt_, pt_.ap.rearrange("p (k c) -> p k c", k=4), AF.Copy)
        po = self.ps[7]
        for i in range(128):
            g8, pl = divmod(i, 16)
            vp = KP[i % 4]
            gather(vp, vc_d, i)
            self.mm(po, po.ap[0:16, :], PT, PT.ap[:, pl, g8 * 16:(g8 + 1) * 16], vp, vp.ap, i == 0, (i == 127 and kind == 'sb'))
        if kind == 'mb':
            self.mm(po, po.ap[0:16, :], pown, pown.ap, diagV, diagV.ap.rearrange("p a b -> p (a b)"), False, True)
        t16 = A.alloc([16, 4, 128], F32, 't16')
        r16 = A.alloc([16, 128], F32, 'r16')
        bl_ = [list(x) for x in bmask.ap]
        bm3 = bass.AP(bmask.tensor, bmask.offset, [bl_[0], bl_[1], [0, 128]])
        self.dve(lambda e: e.tensor_tensor(t16.ap, po.ap[0:16, :].rearrange("p (k d) -> p k d", k=4), bm3, ALU.mult), [po, c2], [t16])
        self.dve(lambda e: e.tensor_reduce(r16.ap, t16.ap.rearrange("p k d -> p d k"), AX.X, ALU.add), [t16], [r16])
        pf = self.psum()
        self.tr(pf, pf.ap[:, 0:16], r16, r16.ap, ident.ap[0:16, 0:16])
        self.act(self.aos, self.aos.ap[:, 0:16, 0], pf, pf.ap[:, 0:16], AF.Copy)

    def attn_prompt(self, kind):
        A = self.A
        dr = self.dram
        pfx = kind
        KTs = [A.alloc([128, T_], BF16, f'KT{i}') for i in range(2)]
        VTs = [A.alloc([128, T_], BF16, f'VT{i}') for i in range(2)]
        Vs = [A.alloc([128, 16, 128], BF16, f'V{i}') for i in range(2)]
        QTh = [A.alloc([128, T_], BF16, f'QTh{i}') for i in range(2)]
        AOh = [A.alloc([128, T_], BF16, f'AOh{i}') for i in range(2)]
        NS = 3
        WE = [A.alloc([128, T_], F32, f'WE{i}') for i in range(NS)]
        WZ = [A.alloc([128, T_], F32, f'WZ{i}') for i in range(NS)]
        WP = [A.alloc([128, T_], BF16, f'WP{i}') for i in range(NS)]
        PTs = [A.alloc([128, 512], BF16, f'PTs{i}') for i in range(3)]
        sm = [A.alloc([128, 32], F32, f'sm{i}') for i in range(NS)]
        if kind == 'sb':
            WC = [A.alloc([128, T_], F32, f'WC{i}') for i in range(2)]
            ones2k = A.alloc([128, T_], F32, 'ones2k')
            self.dve(lambda e: e.memset(ones2k.ap, 1.0), [], [ones2k])
        else:
            KMs = [A.alloc([128, 8], F32, f'KM{i}') for i in range(2)]
            KMbs = [A.alloc([128, 8], BF16, f'KMb{i}') for i in range(2)]
        identb = self.c_identb
        st = {'pv_rr': 0}

        def kv_prologue(kvh):
            KT, VT, V = KTs[kvh % 2], VTs[kvh % 2], Vs[kvh % 2]
            self.dma('pool', KT.ap, dr[pfx + 'kT'][kvh * 128:(kvh + 1) * 128, :], reads=[self.b_kv], writes=[KT])
            self.dma('pool', VT.ap, dr[pfx + 'vT'][kvh * 128:(kvh + 1) * 128, :], reads=[self.b_kv], writes=[VT])
            for jb in range(4):
                pst = self.ps[4 + (jb % 2)]
                psb = pst.ap.bitcast(BF16)
                for q in range(4):
                    j = jb * 4 + q
                    self.tr(pst, psb[:, q * 128:(q + 1) * 128], VT, VT.ap[:, j * 128:(j + 1) * 128], identb.ap)
                va = V.ap[:, jb * 4:(jb + 1) * 4, :]
                self.act(V, va, pst, psb[:, 0:512].rearrange("p (k c) -> p k c", k=4), AF.Copy)
            if kind == 'mb':
                KM, KMb = KMs[kvh % 2], KMbs[kvh % 2]
                self.dve(lambda e: e.tensor_reduce(KM.ap, KT.ap.rearrange("p (b s) -> p b s", b=8), AX.X, ALU.add), [KT], [KM])
                self.dve(lambda e: e.tensor_scalar(KMb.ap, KM.ap, 1.0 / 256, None, ALU.mult), [KM], [KMb])

        def stage_a1(it, kvh, hq, tt):
            h = kvh * 4 + hq
            KT = KTs[kvh % 2]
            Q = QTh[h % 2]
            if tt == 0:
                if hq == 0:
                    kv_prologue(kvh)
                self.dma('sp', Q.ap, dr['QT'][h], reads=[self.b_qt], writes=[Q])
            L = (tt + 1) * 128
            nb = (L + 511) // 512
            E, Z, s_ = WE[it % NS], WZ[it % NS], sm[it % NS]
            qa = Q.ap[:, tt * 128:(tt + 1) * 128]
            for j in range(nb):
                cols = min(512, L - j * 512)
                self.mm(self.ps[j], self.ps[j].ap[:, 0:cols], Q, qa, KT, KT.ap[:, j * 512:j * 512 + cols], True, True)
            if kind == 'sb':
                for j in range(nb):
                    cols = min(512, L - j * 512)
                    sl = slice(j * 512, j * 512 + cols)
                    pa = self.ps[j].ap[:, 0:cols]
                    ea, za = E.ap[:, sl], Z.ap[:, sl]
                    self.act(E, ea, self.ps[j], pa, AF.Exp, scale=SCALE)
                    self.act(E, ea, E, ea, AF.Ln, bias=self.c_ones.ap[:, 0:1])
                    self.dve(lambda e, za=za, pa=pa, ea=ea: e.scalar_tensor_tensor(za, pa, SCALE, ea, ALU.mult, ALU.subtract),
                             [self.ps[j], E], [Z])
            else:
                n = tt // 2
                selb = s_.ap[:, 8:16]
                if n >= 4:
                    KMb = KMbs[kvh % 2]
                    gp = self.ps[5]
                    self.mm(gp, gp.ap[:, 0:8], Q, qa, KMb, KMb.ap, True, True)
                    gt = s_.ap[:, 0:8]
                    self.dve(lambda e, gt=gt: e.memset(gt, -1e30), [], [s_])
                    self.dve(lambda e, gt=gt, n=n, gp=gp: e.tensor_copy(gt[:, 0:n], gp.ap[:, 0:n]), [gp], [s_])
                    t8 = s_.ap[:, 16:24]
                    self.dve(lambda e, t8=t8, gt=gt: e.max(t8, gt), [s_], [s_])
                    self.dve(lambda e, selb=selb, gt=gt, t8=t8: e.tensor_scalar(selb, gt, t8[:, 2:3], -NEG, ALU.is_ge, ALU.mult), [s_], [s_])
                    self.dve(lambda e, selb=selb: e.tensor_scalar(selb, selb, NEG, None, ALU.add), [s_], [s_])
                else:
                    self.dve(lambda e, selb=selb: e.memset(selb, 0.0), [], [s_])
                for blk in range(n + 1):
                    j = blk // 2
                    c0 = (blk % 2) * 256
                    pst = self.ps[j]
                    if blk < n:
                        self.dve(lambda e, z=Z.ap[:, blk * 256:(blk + 1) * 256], p=pst.ap[:, c0:c0 + 256], b=selb[:, blk:blk + 1]:
                                 e.tensor_scalar(z, p, SCALE, b, ALU.mult, ALU.add), [pst, s_], [Z])
                    else:
                        wd = L - n * 256
                        if wd == 256:
                            self.dve(lambda e, z=Z.ap[:, blk * 256:blk * 256 + 128], p=pst.ap[:, c0:c0 + 128]:
                                     e.tensor_scalar(z, p, SCALE, None, ALU.mult), [pst], [Z])
                        self.dve(lambda e, z=Z.ap[:, L - 128:L], p=pst.ap[:, c0 + wd - 128:c0 + wd]:
                                 e.scalar_tensor_tensor(z, p, SCALE, self.c_bias_le.ap, ALU.mult, ALU.add), [pst, self.c_all], [Z])

        def stage_a2(it, kvh, hq, tt):
            L = (tt + 1) * 128
            E, Z, Pb, s_ = WE[it % NS], WZ[it % NS], WP[it % NS], sm[it % NS]
            if kind == 'sb':
                C = WC[it % 2]
                dg = slice(L - 128, L)
                self.dve(lambda e, a=E.ap[:, dg]: e.tensor_tensor(a, a, self.c_mask_lt.ap, ALU.mult), [E, self.c_all], [E])
                self.dve(lambda e, c=C.ap[:, 0:L], o1=ones2k.ap[:, 0:L], sp=E.ap[:, 0:L]:
                         e.tensor_tensor_scan(c, o1, sp, 0.0, ALU.mult, ALU.add), [E, ones2k], [C])
                nt = s_.ap[:, 0:1]
                self.dve(lambda e, nt=nt, c=C.ap[:, L - 1:L]: e.tensor_scalar(nt, c, -1.0, None, ALU.mult), [C], [s_])
                self.dve(lambda e, z=Z.ap[:, 0:L], c=C.ap[:, 0:L]: e.tensor_tensor(z, z, c, ALU.add), [Z, C], [Z])
                self.dve(lambda e, z=Z.ap[:, dg]: e.tensor_tensor(z, z, self.c_bias_lt.ap, ALU.add), [Z, self.c_all], [Z])
                self.act(Pb, Pb.ap[:, 0:L], Z, Z.ap[:, 0:L], AF.Exp, bias=nt, reads=[s_])
            else:
                nm = s_.ap[:, 24:25]
                self.dve(lambda e, nm=nm, z=Z.ap[:, 0:L]: e.tensor_reduce(nm, z, AX.X, ALU.max, negate=True), [Z], [s_])
                self.act(E, E.ap[:, 0:L], Z, Z.ap[:, 0:L], AF.Exp, bias=nm, reads=[s_])
                dn = s_.ap[:, 25:26]
                self.dve(lambda e, dn=dn, ea=E.ap[:, 0:L]: e.tensor_reduce(dn, ea, AX.X, ALU.add), [E], [s_])
                self.dve(lambda e, dn=dn: e.reciprocal(dn, dn), [s_], [s_])
                self.dve(lambda e, dn=dn, pb=Pb.ap[:, 0:L], ea=E.ap[:, 0:L]: e.tensor_scalar(pb, ea, dn, None, ALU.mult), [E, s_], [Pb])

        def stage_b(it, kvh, hq, tt):
            h = kvh * 4 + hq
            V = Vs[kvh % 2]
            Pb = WP[it % NS]
            AOt = AOh[h % 2]
            aops = self.ps[6 + (it % 2)]
            for jb in range((tt + 4) // 4):
                nq = min(4, tt + 1 - jb * 4)
                pv_rr = st['pv_rr']
                pst = self.ps[4 + (pv_rr % 2)]
                psb = pst.ap.bitcast(BF16)
                pts = PTs[pv_rr % 3]
                for q in range(nq):
                    j = jb * 4 + q
                    self.tr(pst, psb[:, q * 128:(q + 1) * 128], Pb, Pb.ap[:, j * 128:(j + 1) * 128], identb.ap)
                if pv_rr % 2 == 0:
                    self.act(pts, pts.ap[:, 0:nq * 128], pst, psb[:, 0:nq * 128], AF.Copy)
                else:
                    self.dve(lambda e, o=pts.ap[:, 0:nq * 128], i=psb[:, 0:nq * 128]: e.tensor_copy(o, i), [pst], [pts])
                for q in range(nq):
                    j = jb * 4 + q
                    self.mm(aops, aops.ap[:, 0:128], V, V.ap[:, j, :], pts, pts.ap[:, q * 128:(q + 1) * 128], j == 0, j == tt)
                st['pv_rr'] += 1
            self.act(AOt, AOt.ap[:, tt * 128:(tt + 1) * 128], aops, aops.ap[:, 0:128], AF.Copy)
            if tt == 15:
                self.dma('sp', dr['AO'][h], AOt.ap, reads=[AOt], writes=[self.b_ao])

        iters = [(kvh, hq, tt) for kvh in range(4) for hq in range(4) for tt in range(16)]
        N = len(iters)
        for i in range(N + 2):
            if i < N:
                stage_a1(i, *iters[i])
            if 0 <= i - 1 < N:
                stage_a2(i - 1, *iters[i - 1])
            if 0 <= i - 2 < N:
                stage_b(i - 2, *iters[i - 2])


def tile_w(W):
    K_, N_ = W.shape
    return np.ascontiguousarray(W.reshape(K_ // 128, 128, N_ // 128, 128).transpose(2, 1, 0, 3))


def make_consts():
    c = np.zeros((128, 1024), np.float32)
    c[:, 0:128] = 1.0
    c[:, 128:256] = np.eye(128, dtype=np.float32)
    c[:, 256] = EPS
    c[:, 257] = np.arange(128)
    t = np.arange(128)[:, None]
    s = np.arange(128)[None, :]
    c[:, 384:512] = (s < t).astype(np.float32)
    c[:, 512:640] = np.where(s <= t, 0.0, NEG).astype(np.float32)
    c[:, 640:768] = (t <= s).astype(np.float32)
    c[:, 768:896] = (t > s).astype(np.float32)
    c[:, 896:1024] = np.where(s < t, 0.0, NEG).astype(np.float32)
    return c


def pk(v):
    return np.ascontiguousarray(np.asarray(v).reshape(-1, 128).T)


def make_consts2():
    c = np.zeros((128, 2048), np.float32)
    r = np.arange(128)
    g, h = r // 16, r % 16
    c[:, 0:128] = ((h[:, None] == h[None, :]) & (g[:, None] > g[None, :])).astype(np.float32)
    c[:, 128:256] = (h[:, None] == h[None, :]).astype(np.float32)
    c[0:16, 256:260] = (np.arange(16)[:, None] // 4 == np.arange(4)[None, :]).astype(np.float32)
    for g8 in range(8):
        for hh in range(16):
            c[hh, 512 + g8 * 128 + g8 * 16 + hh] = 1.0
    return c


def make_cvec(inp):
    c = np.zeros((128, 2048), np.float32)
    c[:, 0:32] = pk(inp['conf_b_pw1'][0])
    c[:, 32:48] = pk(inp['conf_b_dw'][0])
    c[:, 48:64] = pk(inp['conf_ln_g'][0])
    c[:, 64:80] = pk(inp['conf_ln_b'][0])
    c[:, 80:96] = pk(inp['conf_b_pw2'][0])
    wdw = inp['conf_w_dw'][0]
    w3 = wdw.reshape(31, 16, 128).transpose(2, 0, 1)
    c[:, 96:592] = w3.reshape(128, 496)
    c[:, 592:1088] = w3.transpose(0, 2, 1).reshape(128, 496)
    cw = inp['ssm_conv_w'][0].reshape(4, 48, 128).transpose(2, 0, 1)
    c[:, 1088:1280] = cw.reshape(128, 192)
    c[:, 1280:1328] = pk(inp['ssm_conv_b'][0])
    c[:, 1328:1360] = pk(inp['ssm_norm_g'][0])
    c[0:64, 1360] = inp['ssm_dt_bias'][0]
    c[0:64, 1361] = inp['ssm_a_log'][0]
    c[0:64, 1362] = inp['ssm_d'][0]
    c[:, 1363:1555] = cw.transpose(0, 2, 1).reshape(128, 192)
    c[:, 1555:1587] = pk(np.repeat(inp['ssm_d'][0], 64))
    return c


def shared_builders(inp):
    B = {}
    B['consts'] = make_consts
    B['lng'] = lambda: np.ascontiguousarray(inp['ln_g'].reshape(16, 16, 128).transpose(2, 0, 1).reshape(128, 256))
    B['lnb'] = lambda: np.ascontiguousarray(inp['ln_b'].reshape(16, 16, 128).transpose(2, 0, 1).reshape(128, 256))
    for nm, src in (('w1', 'ffn_w1'), ('w3', 'ffn_w3'), ('w2', 'ffn_w2')):
        B[nm] = lambda src=src: np.stack([np.stack([tile_w(inp[src][l, j]) for j in range(2)]) for l in range(NL)])
    B['wgate'] = lambda: np.stack([tile_w(inp['ple_w_gate'][l]) for l in range(NL)])
    B['wproj'] = lambda: np.stack([tile_w(inp['ple_w_proj'][l]) for l in range(NL)])
    B['sbqkv'] = lambda: tile_w(inp['sb_w_qkv'][0])
    B['sbo'] = lambda: tile_w(inp['sb_w_o'][0])
    B['mbqkv'] = lambda: tile_w(inp['moba_w_qkv'][0])
    B['mbo'] = lambda: tile_w(inp['moba_w_o'][0])
    B['cpw1'] = lambda: tile_w(inp['conf_w_pw1'][0])
    B['cpw2'] = lambda: tile_w(inp['conf_w_pw2'][0])
    B['cvec'] = lambda: make_cvec(inp)
    B['consts2'] = make_consts2
    B['sbkc'] = lambda: inp['cache_sb_k'][0].reshape(1280 * 128, 512)
    B['sbvc'] = lambda: inp['cache_sb_v'][0].reshape(1280 * 128, 512)
    B['mbkc'] = lambda: inp['cache_moba_k'][0].reshape(1280 * 128, 512)
    B['mbvc'] = lambda: inp['cache_moba_v'][0].reshape(1280 * 128, 512)
    B['ssmin'] = lambda: tile_w(np.concatenate([inp['ssm_w_in'][0], np.zeros((D, 64), np.float32)], axis=1))
    B['ssmout'] = lambda: tile_w(inp['ssm_w_out'][0])
    B['gexp'] = lambda: (np.arange(64)[:, None] == (np.arange(4096)[None, :] // 64)).astype(np.float32)
    return B


def prep_shared(inp):
    return {k: f() for k, f in shared_builders(inp).items()}


def prep_core(inp, c):
    b = c // 2
    m = {}
    m['xT'] = np.ascontiguousarray(inp['x_prompt'][b].T)
    m['xsT'] = np.ascontiguousarray(inp['x_sample'][c, 0].reshape(DC, 128).T)
    m['pT'] = np.ascontiguousarray(inp['p_prompt'][:, b].transpose(0, 2, 1))
    m['psT'] = np.ascontiguousarray(inp['p_sample'][:, c, 0].reshape(NL, 2, 128).transpose(0, 2, 1))
    m['ptab'] = np.ascontiguousarray(inp['page_table'][c:c + 1].astype(np.int32))
    m['sst'] = np.ascontiguousarray(inp['state_ssm'][0, c].reshape(4096, 128))
    m['scstT'] = np.ascontiguousarray(inp['state_ssm_conv'][0, c].T.reshape(48, 128, 3).transpose(1, 0, 2))
    m['cstT'] = np.ascontiguousarray(inp['state_conf_conv'][0, c].T.reshape(DC, 128, 30).transpose(1, 0, 2))
    return m


_CACHE = {}


def get_nc(cfg):
    key = tuple(sorted(cfg.items()))
    if key not in _CACHE:
        nc = bass.Bass("TRN2", target_bir_lowering=False)
        k = K(nc, cfg)
        k.build()
        _CACHE[key] = (nc, k)
    return _CACHE[key]


def run(inp, cfg, cores=8):
    nc, k = get_nc(cfg)
    sh = prep_shared(inp)
    names = [n for n, t in k.dram.items()]
    in_maps = []
    for c in range(cores):
        m = dict(sh)
        m.update(prep_core(inp, c))
        in_maps.append(m)
    res = run_bass_kernel_spmd(nc, in_maps, core_ids=list(range(cores)))
    return res.results


def kernel(**inp):
    inp = {k: np.asarray(v) for k, v in inp.items()}
    r = run(inp, {})
    f32 = np.float32
    yp = np.stack([r[2 * b]['yT'].T for b in range(4)]).astype(f32)
    ys = np.stack([r[c]['ysT'].T.reshape(1, D) for c in range(8)]).astype(f32)

    def kvp(nm):
        return np.stack([r[2 * b][nm].T.reshape(T_, 4, 128) for b in range(4)])[None].astype(f32)

    def kvs(nm):
        return np.stack([r[c][nm].T.reshape(1, 4, 128) for c in range(8)])[None].astype(f32)
    shp = np.stack([r[2 * b]['shp'].reshape(64, 64, 128) for b in range(4)])[None].astype(f32)
    shs = np.stack([r[c]['shs'].reshape(64, 64, 128) for c in range(8)])[None].astype(f32)
    scp = np.stack([r[2 * b]['scT'].T for b in range(4)])[None].astype(f32)
    scs = np.stack([r[c]['scsT'].transpose(1, 0, 2).reshape(6144, 3).T for c in range(8)])[None].astype(f32)
    ccp = np.stack([r[2 * b]['ccT'].T for b in range(4)])[None].astype(f32)
    ccs = np.stack([r[c]['ccsT'].transpose(1, 0, 2).reshape(D, 30).T for c in range(8)])[None].astype(f32)
    return (yp, ys, kvp('sbkT'), kvp('sbvT'), kvs('sbksT'), kvs('sbvsT'), shp, shs, scp, scs, ccp, ccs,
            kvp('mbkT'), kvp('mbvT'), kvs('mbksT'), kvs('mbvsT'))
```

```python
import numpy as np
import concourse.bass as bass
import concourse.mybir as mybir

F32 = mybir.dt.float32
BF16 = mybir.dt.bfloat16
I32 = mybir.dt.int32
U32 = mybir.dt.uint32
AF = mybir.ActivationFunctionType
ALU = mybir.AluOpType
AX = mybir.AxisListType

ENGS = ['pe', 'act', 'dve', 'pool', 'sp']
N_DMA_SEMS = 12


class Buf:
    __slots__ = ('name', 'w', 'rs')

    def __init__(self, name=''):
        self.name = name
        self.w = None
        self.rs = []


class T:
    __slots__ = ('ap', 'buf')

    def __init__(self, ap, name=''):
        self.ap = ap
        self.buf = Buf(name)


class Op:
    __slots__ = ('eng', 'fn', 'deps', 'dma', 'sem', 'val', 'marked', 'idx')

    def __init__(self, eng, fn, dma):
        self.eng = eng
        self.fn = fn
        self.deps = set()
        self.dma = dma
        self.sem = None
        self.val = None
        self.marked = False
        self.idx = -1


class Prog:
    def __init__(self, nc):
        self.nc = nc
        self.streams = {e: [] for e in ENGS}
        self.dma_count = {e: 0 for e in ENGS}
        self.dma_last = {}
        self.last_real = {e: None for e in ENGS}
        self.pending_dmas = []
        self.out_dmas = []
        self.arena_off = 0

    def op(self, eng, fn, reads=(), writes=(), dma=False, out=False):
        o = Op(eng, fn, dma)
        for t in reads:
            b = t.buf if isinstance(t, T) else t
            if b.w is not None:
                o.deps.add(b.w)
        for t in writes:
            b = t.buf if isinstance(t, T) else t
            if b.w is not None:
                o.deps.add(b.w)
            for r in b.rs:
                o.deps.add(r)
        if dma:
            slot = self.dma_count[eng] % N_DMA_SEMS
            self.dma_count[eng] += 1
            prev = self.dma_last.get((eng, slot))
            if prev is not None:
                o.deps.add(prev)
            self.dma_last[(eng, slot)] = o
            o.sem = ('dma', eng, slot)
            self.pending_dmas.append(o)
            if out:
                self.out_dmas.append(o)
        else:
            if eng == 'pe':
                o.deps = {d for d in o.deps if not (d.eng == 'pe' and not d.dma)}
            if fn is not None:
                self.last_real[eng] = o
        o.deps.discard(o)
        for t in reads:
            b = t.buf if isinstance(t, T) else t
            b.rs.append(o)
        for t in writes:
            b = t.buf if isinstance(t, T) else t
            b.w = o
            b.rs = []
        o.idx = len(self.streams[eng])
        self.streams[eng].append(o)
        return o

    def barrier(self):
        lasts = [self.last_real[e] for e in ENGS if self.last_real[e] is not None]
        pend = list(self.pending_dmas)
        self.pending_dmas = []
        for e in ENGS:
            o = Op(e, None, False)
            o.deps = set(lasts) | set(pend)
            o.idx = len(self.streams[e])
            self.streams[e].append(o)

    def finish(self):
        o = Op('sp', None, False)
        o.deps = set(self.out_dmas) | set(self.pending_dmas)
        self.streams['sp'].append(o)

    def emit(self, block, sems):
        for e in ENGS:
            for o in self.streams[e]:
                for d in o.deps:
                    d.marked = True
        for e in ENGS:
            cnt = 0
            dcnt = {}
            for o in self.streams[e]:
                if o.dma:
                    dcnt[o.sem] = dcnt.get(o.sem, 0) + 16
                    o.val = dcnt[o.sem]
                elif o.fn is not None and o.marked:
                    cnt += 1
                    o.sem = ('e', e)
                    o.val = cnt
        prog = self

        def run(e, eng):
            seen = {}
            n_wait = 0
            for o in prog.streams[e]:
                need = {}
                for d in o.deps:
                    if d.val is None:
                        continue
                    if need.get(d.sem, 0) < d.val:
                        need[d.sem] = d.val
                for s, v in need.items():
                    if seen.get(s, 0) < v:
                        eng.wait_ge(sems[s], v)
                        seen[s] = v
                        n_wait += 1
                if o.fn is None:
                    continue
                ins = o.fn(eng)
                if o.dma:
                    ins.then_inc(sems[o.sem], 16)
                elif o.marked:
                    ins.then_inc(sems[o.sem], 1)

        @block.tensor
        def _(eng):
            run('pe', eng)

        @block.scalar
        def _(eng):
            run('act', eng)

        @block.vector
        def _(eng):
            run('dve', eng)

        @block.gpsimd
        def _(eng):
            run('pool', eng)

        @block.sync
        def _(eng):
            run('sp', eng)

    def sem_names(self):
        names = [('e', e) for e in ENGS]
        for e in ('sp', 'act', 'pool'):
            for s in range(N_DMA_SEMS):
                names.append(('dma', e, s))
        return names


import math
from contextlib import ExitStack
from concourse.bass_utils import run_bass_kernel_spmd

D = 2048
DC = 16
FF = 5632
FC = 44
T_ = 2048
TG = 512
NG = 4
NL = 4
ALPHA = (2 * NL) ** 0.25
EPS = 1e-5
SCALE = 128 ** -0.5
NEG = -30000.0
N_WSLOT = 8
ARENA_WORDS = 51500


def _prod(s):
    r = 1
    for v in s:
        r *= v
    return r


class Arena:
    def __init__(self, sb, nwords):
        self.sb = sb
        self.n = nwords
        self.off = 0

    def alloc(self, shape, dtype, name=''):
        nfree = _prod(shape[1:])
        words = nfree if dtype in (F32, I32, U32) else (nfree + 1) // 2
        a = self.sb[0:shape[0], self.off:self.off + words]
        self.off += words
        assert self.off <= self.n, f"arena overflow at {name}: {self.off}"
        if dtype == BF16:
            a = a.bitcast(BF16)[:, 0:nfree]
        elif dtype != F32:
            a = a.bitcast(dtype)
        if len(shape) == 3:
            a = a.rearrange("p (k w) -> p k w", k=shape[1])
        elif len(shape) == 4:
            a = a.rearrange("p (k j w) -> p k j w", k=shape[1], j=shape[2])
        return T(a, name)


class Seg:
    def __init__(self, A, W, name):
        self.W = W
        self.xf = A.alloc([128, DC, W], F32, name + '_xf')
        self.xb = A.alloc([128, DC, W], BF16, name + '_xb')
        off = A.off
        self.h = A.alloc([128, FC, W], BF16, name + '_h')
        if W == TG:
            self.hf = T(A.sb[:, off:off + DC * W].rearrange("p (k w) -> p k w", k=DC), name + '_hf')
            self.hf.buf = self.h.buf


class K:
    def __init__(self, nc, cfg):
        self.nc = nc
        self.cfg = cfg
        self.P = Prog(nc)
        self.dram = {}

    def din(self, name, shape, dtype=F32):
        t = self.nc.dram_tensor(name, list(shape), dtype, kind="ExternalInput").ap()
        self.dram[name] = t
        return t

    def dout(self, name, shape, dtype=F32):
        t = self.nc.dram_tensor(name, list(shape), dtype, kind="ExternalOutput").ap()
        self.dram[name] = t
        return t

    def dscr(self, name, shape, dtype=F32):
        t = self.nc.dram_tensor(name, list(shape), dtype, kind="Internal").ap()
        self.dram[name] = t
        return t

    def dma(self, q, out, in_, reads=(), writes=(), is_out=False):
        return self.P.op(q, lambda e: e.dma_start(out=out, in_=in_), reads=reads, writes=writes, dma=True, out=is_out)

    def mm(self, out_t, out_ap, l_t, l_ap, r_t, r_ap, start, stop):
        return self.P.op('pe', lambda e: e.matmul(out_ap, lhsT=l_ap, rhs=r_ap, start=start, stop=stop),
                         reads=[l_t, r_t], writes=[out_t])

    def tr(self, out_t, out_ap, in_t, in_ap, ident_ap):
        return self.P.op('pe', lambda e: e.transpose(out_ap, in_ap, ident_ap), reads=[in_t], writes=[out_t])

    def act(self, out_t, out_ap, in_t, in_ap, func, bias=None, scale=None, reads=(), accum=None, extra_w=()):
        kw = {}
        if bias is not None:
            kw['bias'] = bias
        if scale is not None:
            kw['scale'] = scale
        if accum is not None:
            kw['accum_out'] = accum
        return self.P.op('act', lambda e: e.activation(out_ap, in_ap, func, **kw),
                         reads=[in_t] + list(reads), writes=[out_t] + list(extra_w))

    def dve(self, fn, reads, writes):
        return self.P.op('dve', fn, reads=reads, writes=writes)

    def psum(self, kind='p'):
        if kind == 'p':
            i = self.ps_rr % self.n_ps_p
            self.ps_rr += 1
            return self.ps[i]
        i = self.n_ps_p + (self.pss_rr % (8 - self.n_ps_p))
        self.pss_rr += 1
        return self.ps[i]

    def tmp(self, kind='p'):
        if kind == 'p':
            i = self.tmp_rr % len(self.tmps)
            self.tmp_rr += 1
            return self.tmps[i]
        i = self.tmps_rr % len(self.tmps_s)
        self.tmps_rr += 1
        return self.tmps_s[i]

    def wslot(self):
        i = self.ws_rr % N_WSLOT
        self.ws_rr += 1
        return self.wslots[i]

    def linear(self, segs, srcs, wt, KC, o_list, epi):
        for o in o_list:
            pss = [None] * len(segs)
            for kg in range(0, KC, 16):
                kc = min(16, KC - kg)
                sl = self.wslot()
                sl3 = sl.ap[:, 0:kc * 128].rearrange("p (k c) -> p k c", k=kc)
                self.dma('pool', sl3, wt[o, :, kg:kg + kc, :], writes=[sl])
                for si, sg in enumerate(segs):
                    if kg == 0:
                        pss[si] = self.psum('p' if sg.W > 1 else 's')
                    src_t, src_ap = srcs[si]
                    for k in range(kc):
                        self.mm(pss[si], pss[si].ap[:, 0:sg.W], sl, sl3[:, k, :], src_t, src_ap[:, kg + k, 0:sg.W],
                                start=(kg + k == 0), stop=(kg + k == KC - 1))
            for si, sg in enumerate(segs):
                epi(o, si, pss[si], pss[si].ap[:, 0:sg.W])

    def layernorm(self, sg, gcol):
        self.ln_generic(sg.W, sg.xf, sg.xf.ap, lambda k: (self.lng.ap[:, gcol, k:k + 1], self.lnb.ap[:, gcol, k:k + 1]),
                        (sg.xf, sg.xf.ap), (sg.xb, sg.xb.ap), AF.Identity)

    def ln_generic(self, W, x_t, x_ap, gb, out_f, out_b, func):
        kd = 'p' if W > 1 else 's'
        ps1 = self.psum(kd)
        ps2 = self.psum(kd)
        ones = self.c_ones
        for k in range(DC):
            sq = self.tmp(kd)
            xk = x_ap[:, k, 0:W]
            sqa = sq.ap[:, 0:W]
            self.dve(lambda e, sqa=sqa, xk=xk: e.tensor_tensor(sqa, xk, xk, ALU.mult), [x_t], [sq])
            self.mm(ps1, ps1.ap[:, 0:W], ones, ones.ap, x_t, xk, k == 0, k == DC - 1)
            self.mm(ps2, ps2.ap[:, 0:W], ones, ones.ap, sq, sqa, k == 0, k == DC - 1)
        mean, msq, rstd = self.ln_tiles[kd]
        ma, qa, ra = mean.ap[:, 0:W], msq.ap[:, 0:W], rstd.ap[:, 0:W]
        p1, p2 = ps1.ap[:, 0:W], ps2.ap[:, 0:W]
        self.dve(lambda e: e.tensor_scalar(ma, p1, 1.0 / D, None, ALU.mult), [ps1], [mean])
        self.dve(lambda e: e.tensor_tensor(qa, ma, ma, ALU.mult), [mean], [msq])
        self.dve(lambda e: e.scalar_tensor_tensor(qa, p2, 1.0 / D, qa, ALU.mult, ALU.subtract), [ps2, msq], [msq])
        self.act(rstd, ra, msq, qa, AF.Sqrt, bias=self.c_eps.ap[:, 0:1])
        self.dve(lambda e: e.reciprocal(ra, ra), [rstd], [rstd])
        t1s = {}

        def sub(k):
            t1 = self.tmp(kd)
            t1s[k] = t1
            self.dve(lambda e, ta=t1.ap[:, 0:W], xk=x_ap[:, k, 0:W]: e.tensor_tensor(ta, xk, ma, ALU.subtract), [x_t, mean], [t1])

        def mul(k):
            t1 = t1s[k]
            ta = t1.ap[:, 0:W]
            self.dve(lambda e, ta=ta: e.tensor_tensor(ta, ta, ra, ALU.mult), [t1, rstd], [t1])
            g, b = gb(k)
            if out_f is not None:
                self.act(out_f[0], out_f[1][:, k, 0:W], t1, ta, func, bias=b, scale=g, reads=[self.cvec])
            if out_b is not None:
                self.act(out_b[0], out_b[1][:, k, 0:W], t1, ta, func, bias=b, scale=g, reads=[self.cvec])
        sub(0)
        sub(1)
        for k in range(DC):
            if k + 2 < DC:
                sub(k + 2)
            mul(k)

    def resid_epi(self, segs):
        def epi(o, si, ps_t, ps_ap):
            sg = segs[si]
            xk = sg.xf.ap[:, o, 0:sg.W]
            self.dve(lambda e: e.scalar_tensor_tensor(xk, xk, ALPHA, ps_ap, ALU.mult, ALU.add), [sg.xf, ps_t], [sg.xf])
        return epi

    def ffn(self, segs, l, j):
        w1, w3, w2 = self.dram['w1'], self.dram['w3'], self.dram['w2']
        srcs = [(sg.xb, sg.xb.ap) for sg in segs]
        for o in range(FC):
            s1 = self.wslot()
            s3 = self.wslot()
            a1 = s1.ap.rearrange("p (k c) -> p k c", k=16)
            a3 = s3.ap.rearrange("p (k c) -> p k c", k=16)
            self.dma('pool', a1, w1[l, j, o], writes=[s1])
            self.dma('pool', a3, w3[l, j, o], writes=[s3])
            for si, sg in enumerate(segs):
                kd = 'p' if sg.W > 1 else 's'
                pa = self.psum(kd)
                pb = self.psum(kd)
                W = sg.W
                for k in range(DC):
                    self.mm(pa, pa.ap[:, 0:W], s1, a1[:, k, :], sg.xb, sg.xb.ap[:, k, 0:W], k == 0, k == DC - 1)
                for k in range(DC):
                    self.mm(pb, pb.ap[:, 0:W], s3, a3[:, k, :], sg.xb, sg.xb.ap[:, k, 0:W], k == 0, k == DC - 1)
                sa = self.tmp(kd)
                saa = sa.ap[:, 0:W]
                self.act(sa, saa, pa, pa.ap[:, 0:W], AF.Silu)
                ho = sg.h.ap[:, o, 0:W]
                pba = pb.ap[:, 0:W]
                self.dve(lambda e, ho=ho, pba=pba, saa=saa: e.scalar_tensor_tensor(ho, pba, 0.5, saa, ALU.mult, ALU.mult),
                         [pb, sa], [sg.h])
        hs = [(sg.h, sg.h.ap) for sg in segs]
        self.linear(segs, hs, w2[l, j], FC, range(DC), self.resid_epi(segs))
        for sg in segs:
            self.layernorm(sg, l * 4 + (0 if j == 0 else 2))

    def ple(self, segs, psrcs, l):
        wg, wp = self.dram['wgate'], self.dram['wproj']
        for o in range(DC):
            sl = self.wslot()
            sl3 = sl.ap.rearrange("p (k c) -> p k c", k=16)
            self.dma('pool', sl3, wg[l, o], writes=[sl])
            sp_ = self.wslot()
            sp3 = sp_.ap[:, 0:256].rearrange("p (k c) -> p k c", k=2)
            self.dma('pool', sp3, wp[l, o], writes=[sp_])
            for si, sg in enumerate(segs):
                W = sg.W
                kd = 'p' if W > 1 else 's'
                pg = self.psum(kd)
                pp = self.psum(kd)
                for k in range(DC):
                    self.mm(pg, pg.ap[:, 0:W], sl, sl3[:, k, :], sg.xb, sg.xb.ap[:, k, 0:W], k == 0, k == DC - 1)
                pt_t, pt_ap = psrcs[si]
                for k in range(2):
                    self.mm(pp, pp.ap[:, 0:W], sp_, sp3[:, k, :], pt_t, pt_ap[:, k, 0:W], k == 0, k == 1)
                sgm = self.tmp(kd)
                sa = sgm.ap[:, 0:W]
                self.act(sgm, sa, pg, pg.ap[:, 0:W], AF.Sigmoid)
                ppa = pp.ap[:, 0:W]
                self.dve(lambda e, sa=sa, ppa=ppa: e.tensor_tensor(sa, sa, ppa, ALU.mult), [sgm, pp], [sgm])
                xk = sg.xf.ap[:, o, 0:W]
                self.dve(lambda e, sa=sa, xk=xk: e.scalar_tensor_tensor(xk, xk, ALPHA, sa, ALU.mult, ALU.add),
                         [sg.xf, sgm], [sg.xf])
        for sg in segs:
            self.layernorm(sg, l * 4 + 3)

    def setup(self, es):
        nc = self.nc
        self.sb = es.enter_context(nc.sbuf_tensor("arena", [128, ARENA_WORDS], F32))
        self.A = Arena(self.sb, ARENA_WORDS)
        self.ps = []
        for i in range(8):
            p = es.enter_context(nc.psum_tensor(f"ps{i}", [128, 512], F32))
            self.ps.append(T(p[:, :], f"ps{i}"))
        self.n_ps_p = 6
        self.ps_rr = 0
        self.pss_rr = 0
        self.tmp_rr = 0
        self.tmps_rr = 0
        self.ws_rr = 0
        A = self.A
        cst = self.din('consts', [128, 1024])
        self.c_all = A.alloc([128, 1024], F32, 'consts')
        self.dma('sp', self.c_all.ap, cst, writes=[self.c_all])
        ca = self.c_all
        self.c_ones = T(ca.ap[:, 0:128], 'ones')
        self.c_ones.buf = ca.buf
        self.c_ident = T(ca.ap[:, 128:256], 'ident')
        self.c_ident.buf = ca.buf
        self.c_eps = T(ca.ap[:, 256:257], 'eps')
        self.c_eps.buf = ca.buf
        self.c_mask_lt = T(ca.ap[:, 384:512], 'mask_lt')
        self.c_mask_lt.buf = ca.buf
        self.c_bias_le = T(ca.ap[:, 512:640], 'bias_le')
        self.c_bias_le.buf = ca.buf
        self.c_tri_le = T(ca.ap[:, 640:768], 'tri_le')
        self.c_tri_le.buf = ca.buf
        self.c_tri_gt = T(ca.ap[:, 768:896], 'tri_gt')
        self.c_tri_gt.buf = ca.buf
        self.c_bias_lt = T(ca.ap[:, 896:1024], 'bias_lt')
        self.c_bias_lt.buf = ca.buf
        self.c_identb = A.alloc([128, 128], BF16, 'identb')
        self.dve(lambda e: e.tensor_copy(self.c_identb.ap, self.c_ident.ap), [ca], [self.c_identb])
        self.c_blt_b = A.alloc([128, 128], BF16, 'blt_b')
        self.c_ble_b = A.alloc([128, 128], BF16, 'ble_b')
        self.dve(lambda e: e.tensor_scalar(self.c_blt_b.ap, self.c_bias_lt.ap, 1.0 / SCALE, None, ALU.mult), [ca], [self.c_blt_b])
        self.dve(lambda e: e.tensor_scalar(self.c_ble_b.ap, self.c_bias_le.ap, 1.0 / SCALE, None, ALU.mult), [ca], [self.c_ble_b])
        lg = self.din('lng', [128, 16 * 16])
        lb = self.din('lnb', [128, 16 * 16])
        self.lng = A.alloc([128, 16, 16], F32, 'lng')
        self.lnb = A.alloc([128, 16, 16], F32, 'lnb')
        self.dma('sp', self.lng.ap, lg.rearrange("p (a b) -> p a b", a=16), writes=[self.lng])
        self.dma('sp', self.lnb.ap, lb.rearrange("p (a b) -> p a b", a=16), writes=[self.lnb])
        cv = self.din('cvec', [128, 2048])
        self.cvec = A.alloc([128, 2048], F32, 'cvec')
        self.dma('sp', self.cvec.ap, cv, writes=[self.cvec])
        self.wslots = [A.alloc([128, 2048], BF16, f'ws{i}') for i in range(N_WSLOT)]
        self.tmps = [A.alloc([128, 512], F32, f'tmp{i}') for i in range(6)]
        self.tmps_s = [A.alloc([128, 1], F32, f'tmps{i}') for i in range(6)]
        self.ln_tiles = {'p': [A.alloc([128, 512], F32, f'ln{i}') for i in range(3)],
                         's': [A.alloc([128, 1], F32, f'lns{i}') for i in range(3)]}
        self.sseg = Seg(A, 1, 'ss')
        self.base_off = A.off

    def declare(self):
        d = self.din
        d('xT', [D, T_]); d('xsT', [128, DC])
        d('pT', [NL, 256, T_]); d('psT', [NL, 128, 2])
        d('w1', [NL, 2, FC, 128, DC, 128]); d('w3', [NL, 2, FC, 128, DC, 128]); d('w2', [NL, 2, DC, 128, FC, 128])
        d('wgate', [NL, DC, 128, DC, 128]); d('wproj', [NL, DC, 128, 2, 128])
        d('sbqkv', [24, 128, DC, 128]); d('sbo', [DC, 128, DC, 128])
        d('mbqkv', [24, 128, DC, 128]); d('mbo', [DC, 128, DC, 128])
        d('ssmin', [81, 128, DC, 128]); d('ssmout', [DC, 128, 32, 128]); d('gexp', [64, 4096])
        d('sst', [4096, 128]); d('scstT', [128, 48, 3])
        for nm in ('sbkc', 'sbvc', 'mbkc', 'mbvc'):
            d(nm, [1280 * 128, 512])
        d('ptab', [1, 128], I32); d('consts2', [128, 2048])
        d('cpw1', [32, 128, DC, 128]); d('cpw2', [DC, 128, DC, 128]); d('cstT', [128, DC, 30])
        o = self.dout
        o('ccT', [D, 30]); o('ccsT', [128, DC, 30])
        o('shp', [4096, 128]); o('shs', [4096, 128]); o('scT', [6144, 3]); o('scsT', [128, 48, 3])
        o('yT', [D, T_]); o('ysT', [128, DC])
        o('sbkT', [512, T_]); o('sbvT', [512, T_]); o('sbksT', [128, 4]); o('sbvsT', [128, 4])
        o('mbkT', [512, T_]); o('mbvT', [512, T_]); o('mbksT', [128, 4]); o('mbvsT', [128, 4])
        s = self.dscr
        s('XS', [NG, 128, DC, TG])
        s('QT', [16, 128, T_], BF16)
        s('AO', [32, 128, T_], BF16)
        s('UT', [D, T_]); s('CT', [D, T_])
        s('ZT', [4096, T_]); s('XBC', [6144, T_]); s('DTr', [64, T_]); s('XC', [6144, T_]); s('YT', [4096, T_])

    def fm(self, ap2, g):
        return ap2.rearrange("(k p) t -> p k t", p=128)[:, :, g * TG:(g + 1) * TG]

    def fms(self, ap2):
        return ap2

    def stage(self, dtype):
        if dtype == BF16:
            i = self.stb_rr % len(self.stb)
            self.stb_rr += 1
            return self.stb[i]
        i = self.stf_rr % len(self.stf)
        self.stf_rr += 1
        return self.stf[i]

    def qkv_proj(self, segs, g, wname, kname, vname):
        dr = self.dram

        def epi(o, si, ps_t, ps_ap):
            if si == 0:
                if o < 16:
                    st = self.stage(BF16)
                    self.act(st, st.ap, ps_t, ps_ap, AF.Copy)
                    self.dma('sp', dr['QT'][o, :, g * TG:(g + 1) * TG], st.ap, reads=[st], writes=[self.b_qt])
                else:
                    st = self.stage(F32)
                    self.act(st, st.ap, ps_t, ps_ap, AF.Copy)
                    nm = kname if o < 20 else vname
                    c = (o - 16) % 4
                    self.dma('sp', dr[nm + 'T'][c * 128:(c + 1) * 128, g * TG:(g + 1) * TG], st.ap,
                             reads=[st], writes=[self.b_kv], is_out=True)
            else:
                if o < 16:
                    self.act(self.qs, self.qs.ap[:, o, :], ps_t, ps_ap, AF.Copy)
                else:
                    self.act(self.kvs, self.kvs.ap[:, o - 16, :], ps_t, ps_ap, AF.Copy)
        srcs = [(sg.xb, sg.xb.ap) for sg in segs]
        self.linear(segs, srcs, dr[wname], DC, range(24), epi)
        if len(segs) > 1:
            self.dma('sp', dr[kname + 'sT'], self.kvs.ap[:, 0:4, 0], reads=[self.kvs], is_out=True)
            self.dma('sp', dr[vname + 'sT'], self.kvs.ap[:, 4:8, 0], reads=[self.kvs], is_out=True)

    def build(self):
        cfg = self.cfg
        with ExitStack() as es:
            self.setup(es)
            self.declare()
            A = self.A
            dr = self.dram
            self.b_qt = Buf('QT')
            self.b_kv = Buf('KV')
            self.b_xs = [Buf(f'XS{g}') for g in range(NG)]
            self.b_ao = Buf('AO')
            self.qs = A.alloc([128, 16, 1], F32, 'qs')
            self.kvs = A.alloc([128, 8, 1], F32, 'kvs')
            self.aos = A.alloc([128, 32, 1], BF16, 'aos')
            self.pst = A.alloc([128, 2, 1], BF16, 'pst')
            self.stb = [A.alloc([128, 512], BF16, f'stb{i}') for i in range(2)]
            self.stf = [A.alloc([128, 512], F32, f'stf{i}') for i in range(2)]
            self.stb_rr = 0
            self.stf_rr = 0
            self.base_off = A.off
            n_layers = cfg.get('n_layers', NL)
            l0 = cfg.get('l0', 0)
            n_layers = l0 + n_layers
            self.b_ut = Buf('UT')
            self.b_ct = Buf('CT')
            self.b_zt = Buf('ZT'); self.b_xbc = Buf('XBC'); self.b_dt = Buf('DT'); self.b_xc = Buf('XC'); self.b_yt = Buf('YT')
            self.zs = A.alloc([128, 32, 1], F32, 'zs')
            self.xbcs = A.alloc([128, 48, 1], F32, 'xbcs')
            self.dts = A.alloc([128, 1], F32, 'dts')
            self.ys = A.alloc([128, 32, 1], F32, 'ys')
            self.us = A.alloc([128, DC, 1], F32, 'us')
            self.cs = A.alloc([128, DC, 1], F32, 'cs')
            self.base_off = A.off
            for r in range(l0, n_layers + 1):
                A.off = self.base_off
                seg = Seg(A, TG, f'pseg')
                pt = A.alloc([128, 2, TG], BF16, 'pt')
                ao = A.alloc([128, 32, TG], BF16, 'ao')
                for g in range(NG):
                    segs = [seg] + ([self.sseg] if g == 0 else [])
                    if r == l0:
                        self.dma('sp', seg.xf.ap, self.fm(dr['xT'], g), writes=[seg.xf])
                        self.dma('pool', seg.xb.ap, self.fm(dr['xT'], g), writes=[seg.xb])
                        if g == 0:
                            self.dma('sp', self.sseg.xf.ap[:, :, 0], dr['xsT'], writes=[self.sseg.xf])
                            self.dma('pool', self.sseg.xb.ap[:, :, 0], dr['xsT'], writes=[self.sseg.xb])
                    else:
                        l = r - 1
                        self.dma('sp', seg.xf.ap, dr['XS'][g], reads=[self.b_xs[g]], writes=[seg.xf])
                        self.mixer_out(l, segs, g, ao)
                        for sg in segs:
                            self.layernorm(sg, l * 4 + 1)
                        if cfg.get('stop') == f'pn{l}':
                            self.write_y(segs, g)
                            continue
                        self.ffn(segs, l, 1)
                        self.dma('pool', pt.ap, self.fm(dr['pT'][l], g), writes=[pt])
                        psrcs = [(pt, pt.ap)]
                        if g == 0:
                            self.dma('pool', self.pst.ap[:, :, 0], dr['psT'][l], writes=[self.pst])
                            psrcs.append((self.pst, self.pst.ap))
                        self.ple(segs, psrcs, l)
                    if r < n_layers:
                        self.ffn(segs, r, 0)
                        if cfg.get('stop') == f'ffn1_{r}':
                            self.write_y(segs, g)
                            continue
                        self.mixer_in(r, segs, g)
                        self.dma('sp', dr['XS'][g], seg.xf.ap, reads=[seg.xf], writes=[self.b_xs[g]])
                    else:
                        self.write_y(segs, g)
                if cfg.get('stop') in (f'ffn1_{r}', f'pn{r - 1}'):
                    break
                self.P.barrier()
                if r < n_layers:
                    A.off = self.base_off
                    self.mixer_core(r)
                    self.P.barrier()
            self.P.finish()
            sems = {}
            for nm in self.P.sem_names():
                sems[nm] = es.enter_context(self.nc.semaphore("s_" + "_".join(str(x) for x in nm)))
            block = es.enter_context(self.nc.Block())
            self.P.emit(block, sems)

    def write_y(self, segs, g):
        dr = self.dram
        self.dma('sp', self.fm(dr['yT'], g), segs[0].xf.ap, reads=[segs[0].xf], is_out=True)
        if len(segs) > 1:
            self.dma('sp', dr['ysT'], segs[1].xf.ap[:, :, 0], reads=[segs[1].xf], is_out=True)

    def mixer_in(self, l, segs, g):
        if l == 0:
            self.qkv_proj(segs, g, 'sbqkv', 'sbk', 'sbv')
        elif l == 3:
            self.qkv_proj(segs, g, 'mbqkv', 'mbk', 'mbv')
        elif l == 2:
            self.conf_in(segs, g)
        elif l == 1:
            self.ssm_in(segs, g)

    def mixer_out(self, l, segs, g, ao):
        dr = self.dram
        if l in (0, 3):
            ao16 = ao.ap[:, 0:16, :]
            self.dma('sp', ao16, dr['AO'][0:16].rearrange("k p t -> p k t")[:, :, g * TG:(g + 1) * TG],
                     reads=[self.b_ao], writes=[ao])
            srcs = [(ao, ao16)]
            if len(segs) > 1:
                srcs.append((self.aos, self.aos.ap[:, 0:16, :]))
            self.linear(segs, srcs, dr['sbo' if l == 0 else 'mbo'], DC, range(DC), self.resid_epi(segs))
        elif l == 1:
            self.ssm_out(segs, g, ao)
        elif l == 2:
            seg = segs[0]
            cv = self.cvec.ap
            self.dma('sp', seg.hf.ap, self.fm(dr['CT'], g), reads=[self.b_ct], writes=[seg.hf])
            gb = lambda k: (cv[:, 48 + k:49 + k], cv[:, 64 + k:65 + k])
            ao16 = ao.ap[:, 0:16, :]
            self.ln_generic(TG, seg.hf, seg.hf.ap, gb, None, (ao, ao16), AF.Silu)
            srcs = [(ao, ao16)]
            if len(segs) > 1:
                self.ln_generic(1, self.cs, self.cs.ap, gb, None, (self.aos, self.aos.ap[:, 0:16, :]), AF.Silu)
                srcs.append((self.aos, self.aos.ap[:, 0:16, :]))

            def epi(o, si, ps_t, ps_ap):
                sg = segs[si]
                tp = self.tmp('p' if sg.W > 1 else 's')
                ta = tp.ap[:, 0:sg.W]
                self.act(tp, ta, ps_t, ps_ap, AF.Identity, bias=cv[:, 80 + o:81 + o], reads=[self.cvec])
                xk = sg.xf.ap[:, o, 0:sg.W]
                self.dve(lambda e: e.scalar_tensor_tensor(xk, xk, ALPHA, ta, ALU.mult, ALU.add), [sg.xf, tp], [sg.xf])
            self.linear(segs, srcs, dr['cpw2'], DC, range(DC), epi)

    def ssm_in(self, segs, g):
        dr = self.dram
        gs = slice(g * TG, (g + 1) * TG)

        def epi(o, si, ps_t, ps_ap):
            if si == 0:
                st = self.stage(F32)
                self.act(st, st.ap, ps_t, ps_ap, AF.Copy)
                if o < 32:
                    self.dma('sp', dr['ZT'][o * 128:(o + 1) * 128, gs], st.ap, reads=[st], writes=[self.b_zt])
                elif o < 80:
                    self.dma('sp', dr['XBC'][(o - 32) * 128:(o - 31) * 128, gs], st.ap, reads=[st], writes=[self.b_xbc])
                else:
                    self.dma('sp', dr['DTr'][:, gs], st.ap[0:64, :], reads=[st], writes=[self.b_dt])
            else:
                if o < 32:
                    self.act(self.zs, self.zs.ap[:, o, :], ps_t, ps_ap, AF.Copy)
                elif o < 80:
                    self.act(self.xbcs, self.xbcs.ap[:, o - 32, :], ps_t, ps_ap, AF.Copy)
                else:
                    self.act(self.dts, self.dts.ap, ps_t, ps_ap, AF.Copy)
        srcs = [(sg.xb, sg.xb.ap) for sg in segs]
        self.linear(segs, srcs, dr['ssmin'], DC, range(81), epi)

    def ssm_out(self, segs, g, ao):
        dr = self.dram
        seg = segs[0]
        cv = self.cvec.ap
        gs = slice(g * TG, (g + 1) * TG)
        quarters = []
        for i in range(4):
            t = T(seg.hf.ap[:, i * 4:(i + 1) * 4, :], f'hfq{i}')
            quarters.append(t)
        def load(gq):
            y4 = quarters[(gq % 2) * 2]
            z4 = quarters[(gq % 2) * 2 + 1]
            self.dma('sp', y4.ap, dr['YT'][gq * 512:(gq + 1) * 512, gs].rearrange("(k p) t -> p k t", p=128), reads=[self.b_yt], writes=[y4, seg.h])
            self.dma('sp', z4.ap, dr['ZT'][gq * 512:(gq + 1) * 512, gs].rearrange("(k p) t -> p k t", p=128), reads=[self.b_zt], writes=[z4, seg.h])
        load(0)
        for gq in range(8):
            if gq + 1 < 8:
                load(gq + 1)
            y4 = quarters[(gq % 2) * 2]
            z4 = quarters[(gq % 2) * 2 + 1]
            items = [(TG, 'p', y4, y4.ap, z4, z4.ap, ao, ao.ap[:, gq * 4:(gq + 1) * 4, :], gq)]
            if len(segs) > 1:
                items.append((1, 's', self.ys, self.ys.ap[:, gq * 4:(gq + 1) * 4, :], self.zs, self.zs.ap[:, gq * 4:(gq + 1) * 4, :],
                              self.aos, self.aos.ap[:, gq * 4:(gq + 1) * 4, :], gq))
            self._ssm_norm_items(items, cv)
        srcs = [(ao, ao.ap)]
        if len(segs) > 1:
            srcs.append((self.aos, self.aos.ap))
        self.linear(segs, srcs, dr['ssmout'], 32, range(DC), self.resid_epi(segs))

    def _ssm_norm_items(self, items, cv):
        for (W, kd, y_t, y_ap, z_t, z_ap, o_t, o_ap, gq) in items:
            ps = self.psum(kd)
            szs = [self.tmp(kd) for _ in range(4)]
            for k in range(4):
                self.act(szs[k], szs[k].ap[:, 0:W], z_t, z_ap[:, k, :], AF.Silu)
            for k in range(4):
                yk = y_ap[:, k, :]
                self.dve(lambda e, yk=yk, sza=szs[k].ap[:, 0:W]: e.tensor_tensor(yk, yk, sza, ALU.mult), [y_t, szs[k]], [y_t])
            for k in range(4):
                yk = y_ap[:, k, :]
                sqa = szs[k].ap[:, 0:W]
                self.dve(lambda e, yk=yk, sqa=sqa: e.tensor_tensor(sqa, yk, yk, ALU.mult), [y_t], [szs[k]])
                self.mm(ps, ps.ap[:, 0:W], self.c_ones, self.c_ones.ap, szs[k], sqa, k == 0, k == 3)
            rs = self.ln_tiles[kd][2]
            ra = rs.ap[:, 0:W]
            self.dve(lambda e, ra=ra, p=ps.ap[:, 0:W]: e.tensor_scalar(ra, p, 1.0 / 512, None, ALU.mult), [ps], [rs])
            self.act(rs, ra, rs, ra, AF.Sqrt, bias=self.c_eps.ap[:, 0:1])
            self.dve(lambda e, ra=ra: e.reciprocal(ra, ra), [rs], [rs])
            for k in range(4):
                yk = y_ap[:, k, :]
                self.dve(lambda e, ta=szs[k].ap[:, 0:W], yk=yk, ra=ra: e.tensor_tensor(ta, yk, ra, ALU.mult), [y_t, rs], [szs[k]])
            for k in range(4):
                gcol = cv[:, 1328 + gq * 4 + k:1329 + gq * 4 + k]
                self.dve(lambda e, o=o_ap[:, k, :], ta=szs[k].ap[:, 0:W], gcol=gcol: e.tensor_scalar(o, ta, gcol, None, ALU.mult), [szs[k], self.cvec], [o_t])

    def ssm_core(self):
        A = self.A
        dr = self.dram
        cv = self.cvec.ap
        sv = A.alloc([128, 8], F32, 'sv')
        mark = A.off

        def bcl(ap, n):
            return bass.AP(ap.tensor, ap.offset, [list(x) for x in ap.ap] + [[0, n]])

        def bcm(ap, m):
            l = [list(x) for x in ap.ap]
            return bass.AP(ap.tensor, ap.offset, [l[0], [0, m]] + l[1:])
        xpad = [A.alloc([128, 3 + T_], F32, f'xpad{i}') for i in range(2)]
        acc = [A.alloc([128, T_], F32, f'sacc{i}') for i in range(2)]
        for kp in range(24):
            ks = (2 * kp, 2 * kp + 1)
            for k in ks:
                up = xpad[k % 2]
                self.dve(lambda e, a=up.ap[:, 0:3]: e.memset(a, 0.0), [], [up])
                self.dma('sp', up.ap[:, 3:3 + T_], dr['XBC'][k * 128:(k + 1) * 128, :], reads=[self.b_xbc], writes=[up])
            for w in range(4):
                for k in ks:
                    up, ac = xpad[k % 2], acc[k % 2]
                    wc_ = cv[:, 1088 + w * 48 + k:1089 + w * 48 + k]
                    if w == 0:
                        self.dve(lambda e, o=ac.ap, i=up.ap[:, 0:T_], w0=wc_, b=cv[:, 1280 + k:1281 + k]: e.tensor_scalar(o, i, w0, b, ALU.mult, ALU.add),
                                 [up, self.cvec], [ac])
                    else:
                        self.dve(lambda e, o=ac.ap, i=up.ap[:, w:w + T_], ww=wc_: e.scalar_tensor_tensor(o, i, ww, o, ALU.mult, ALU.add),
                                 [up, ac, self.cvec], [ac])
            for k in ks:
                ac = acc[k % 2]
                self.act(ac, ac.ap, ac, ac.ap, AF.Silu)
                self.dma('sp', dr['XC'][k * 128:(k + 1) * 128, :], ac.ap, reads=[ac], writes=[self.b_xc])
        self.dma('sp', dr['scT'], dr['XBC'][:, T_ - 3:T_], reads=[self.b_xbc], is_out=True)
        self.P.barrier()
        A.off = mark
        ident = self.c_ident
        ones = self.c_ones
        tri_le = self.c_tri_le
        tri_gt = self.c_tri_gt
        self.act(sv, sv.ap[0:64, 0:1], self.cvec, cv[0:64, 1361:1362], AF.Exp)
        self.dve(lambda e: e.tensor_scalar(sv.ap[0:64, 0:1], sv.ap[0:64, 0:1], -1.0, None, ALU.mult), [sv], [sv])
        a_col = sv.ap[0:64, 0:1]
        hT = A.alloc([128, 4096], F32, 'hT')
        hTb = A.alloc([128, 4096], BF16, 'hTb')
        self.dve(lambda e: e.memset(hT.ap, 0.0), [], [hT])
        self.dve(lambda e: e.memset(hTb.ap, 0.0), [], [hTb])
        xcs = A.alloc([128, 48, 128], F32, 'xcs')
        xcb = A.alloc([128, 16, 128], BF16, 'xcb')
        Xtm = A.alloc([128, 4096], BF16, 'Xtm')
        Xw = A.alloc([128, 4096], BF16, 'Xw')
        yTo = A.alloc([128, 32, 128], F32, 'yTo')
        Btm = A.alloc([128, 8, 128], BF16, 'Btm')
        CBm = A.alloc([128, 8, 128], F32, 'CBm')
        dtf = A.alloc([128, 128], F32, 'dtf')
        daf = A.alloc([128, 128], F32, 'daf')
        dtda = A.alloc([128, 128], F32, 'dtda')
        ecum = A.alloc([128, 64], F32, 'ecum')
        edec = A.alloc([128, 64], F32, 'edec')
        totb = A.alloc([128, 64], F32, 'totb')
        wtm = A.alloc([128, 64], F32, 'wtm')
        Ut = [A.alloc([128, 128], F32, f'Ut{i}') for i in range(16)]
        Et = [A.alloc([128, 512], F32, f'Et{i}') for i in range(4)]
        Mt = [A.alloc([128, 128], BF16, f'Mt{i}') for i in range(8)]
        ytmp = [A.alloc([128, 512], F32, f'ytmp{i}') for i in range(2)]
        ytm = [A.alloc([128, 512], F32, f'ytm{i}') for i in range(2)]
        dt_tm = dtda.ap[:, 0:64]
        da_tm = dtda.ap[:, 64:128]
        id64 = ident.ap[0:64, 0:64]
        rr = 0
        for c in range(16):
            cs = slice(c * 128, (c + 1) * 128)
            self.dma('sp', xcs.ap, dr['XC'].rearrange("(k p) t -> p k t", p=128)[:, :, cs], reads=[self.b_xc], writes=[xcs])
            self.dma('sp', dtf.ap[0:64, :], dr['DTr'][:, cs], reads=[self.b_dt], writes=[dtf])
            self.act(dtf, dtf.ap[0:64, :], dtf, dtf.ap[0:64, :], AF.Exp, bias=cv[0:64, 1360:1361], reads=[self.cvec])
            self.act(dtf, dtf.ap[0:64, :], dtf, dtf.ap[0:64, :], AF.Ln, bias=ones.ap[0:64, 0:1])
            self.dve(lambda e: e.tensor_scalar(daf.ap[0:64, :], dtf.ap[0:64, :], a_col, None, ALU.mult), [dtf, sv], [daf])
            pA = self.psum()
            self.tr(pA, pA.ap[:, 0:64], dtf, dtf.ap[0:64, :], id64)
            self.tr(pA, pA.ap[:, 64:128], daf, daf.ap[0:64, :], id64)
            self.act(dtda, dtda.ap, pA, pA.ap[:, 0:128], AF.Copy)
            pB = self.psum()
            self.mm(pB, pB.ap[:, 0:64], tri_le, tri_le.ap, dtda, da_tm, True, True)
            self.mm(pB, pB.ap[:, 64:128], ones, ones.ap, dtda, da_tm, True, True)
            self.act(ecum, ecum.ap, pB, pB.ap[:, 0:64], AF.Exp)
            self.act(edec, edec.ap, pB, pB.ap[:, 64:128], AF.Exp)
            self.act(totb, totb.ap, pB, pB.ap[:, 64:128], AF.Copy)
            self.dve(lambda e, p=pB.ap[:, 0:64]: e.tensor_tensor(wtm.ap, totb.ap, p, ALU.subtract), [totb, pB], [wtm])
            self.act(wtm, wtm.ap, wtm, wtm.ap, AF.Exp)
            self.dve(lambda e: e.tensor_tensor(wtm.ap, wtm.ap, dt_tm, ALU.mult), [wtm, dtda], [wtm])
            self.act(xcb, xcb.ap, xcs, xcs.ap[:, 32:48, :], AF.Copy)
            for kb in range(8):
                pt_ = self.psum()
                for q in range(4):
                    self.tr(pt_, pt_.ap[:, q * 128:(q + 1) * 128], xcs, xcs.ap[:, kb * 4 + q, :], ident.ap)
                xo = Xtm.ap[:, kb * 512:(kb + 1) * 512]
                if kb % 2 == 0:
                    self.act(Xtm, xo, pt_, pt_.ap, AF.Copy)
                else:
                    self.dve(lambda e, xo=xo, p=pt_.ap: e.tensor_copy(xo, p), [pt_], [Xtm])
            for kb in range(2):
                pt_ = self.psum()
                for q in range(4):
                    self.tr(pt_, pt_.ap[:, q * 128:(q + 1) * 128], xcs, xcs.ap[:, 32 + kb * 4 + q, :], ident.ap)
                self.act(Btm, Btm.ap[:, kb * 4:(kb + 1) * 4, :], pt_, pt_.ap.rearrange("p (k c) -> p k c", k=4), AF.Copy)
            self.dve(lambda e: e.tensor_tensor(Xw.ap.rearrange("p (h q) -> p h q", h=64), Xtm.ap.rearrange("p (h q) -> p h q", h=64),
                                               bcl(wtm.ap, 64), ALU.mult), [Xtm, wtm], [Xw])
            for g4 in range(2):
                pC = self.psum()
                for q in range(4):
                    g = g4 * 4 + q
                    self.mm(pC, pC.ap[:, q * 128:(q + 1) * 128], xcb, xcb.ap[:, g, :], xcb, xcb.ap[:, 8 + g, :], True, True)
                self.dve(lambda e, o=CBm.ap[:, g4 * 4:(g4 + 1) * 4, :], p=pC.ap.rearrange("p (k c) -> p k c", k=4):
                         e.tensor_tensor(o, p, bcm(tri_le.ap, 4), ALU.mult), [pC, self.c_all], [CBm])
            def s1(g):
                gsl = slice(g * 512, (g + 1) * 512)
                pY = self.ps[g % 2]
                self.mm(pY, pY.ap, xcb, xcb.ap[:, 8 + g, :], hTb, hTb.ap[:, gsl], True, True)
                yt_ = ytmp[g % 2]
                self.dve(lambda e, o=yt_.ap.rearrange("p (h q) -> p h q", h=8), p=pY.ap.rearrange("p (h q) -> p h q", h=8),
                         b=bcl(ecum.ap[:, g * 8:(g + 1) * 8], 64): e.tensor_tensor(o, p, b, ALU.mult), [pY, ecum], [yt_])
                for hb in range(2):
                    pD = self.ps[2 + hb]
                    for q in range(4):
                        h = g * 8 + hb * 4 + q
                        U = Ut[(g % 2) * 8 + hb * 4 + q]
                        self.dve(lambda e, u=U.ap, sc=da_tm[:, h:h + 1]: e.tensor_scalar(u, tri_gt.ap, sc, None, ALU.mult),
                                 [self.c_all, dtda], [U])
                        self.mm(pD, pD.ap[:, q * 128:(q + 1) * 128], U, U.ap, tri_le, tri_le.ap, True, True)
                    E = Et[(g % 2) * 2 + hb]
                    self.act(E, E.ap, pD, pD.ap, AF.Exp)

            def s2(g):
                gsl = slice(g * 512, (g + 1) * 512)
                yt_ = ytmp[g % 2]
                pI = self.ps[4]
                for hb in range(2):
                    E = Et[(g % 2) * 2 + hb]
                    for q in range(4):
                        h = g * 8 + hb * 4 + q
                        M = Mt[hb * 4 + q]
                        self.dve(lambda e, m=M.ap, ea=E.ap[:, q * 128:(q + 1) * 128], sc=dt_tm[:, h:h + 1], cb=CBm.ap[:, g, :]:
                                 e.scalar_tensor_tensor(m, ea, sc, cb, ALU.mult, ALU.mult), [E, dtda, CBm], [M])
                        co = (hb * 4 + q) * 64
                        self.mm(pI, pI.ap[:, co:co + 64], M, M.ap, Xtm, Xtm.ap[:, h * 64:(h + 1) * 64], True, True)
                ym = ytm[g % 2]
                self.dve(lambda e, o=ym.ap, a=yt_.ap, p=pI.ap: e.tensor_tensor(o, a, p, ALU.add), [yt_, pI], [ym])
                pT = self.ps[5]
                for q in range(4):
                    self.tr(pT, pT.ap[:, q * 128:(q + 1) * 128], ym, ym.ap[:, q * 128:(q + 1) * 128], ident.ap)
                for q in range(4):
                    k = g * 4 + q
                    self.dve(lambda e, o=yTo.ap[:, k, :], x=xcs.ap[:, k, :], dcol=cv[:, 1555 + k:1556 + k], p=pT.ap[:, q * 128:(q + 1) * 128]:
                             e.scalar_tensor_tensor(o, x, dcol, p, ALU.mult, ALU.add), [xcs, self.cvec, pT], [yTo])
                pS = self.ps[6]
                self.mm(pS, pS.ap, Btm, Btm.ap[:, g, :], Xw, Xw.ap[:, gsl], True, True)
                hv = hT.ap[:, gsl].rearrange("p (h q) -> p h q", h=8)
                self.dve(lambda e, hv=hv, b=bcl(edec.ap[:, g * 8:(g + 1) * 8], 64): e.tensor_tensor(hv, hv, b, ALU.mult), [hT, edec], [hT])
                self.dve(lambda e, hg=hT.ap[:, gsl], p=pS.ap: e.tensor_tensor(hg, hg, p, ALU.add), [hT, pS], [hT])
                self.act(hTb, hTb.ap[:, gsl], hT, hT.ap[:, gsl], AF.Copy)
            s1(0)
            for g in range(8):
                if g + 1 < 8:
                    s1(g + 1)
                s2(g)
            self.dma('sp', dr['YT'].rearrange("(k p) t -> p k t", p=128)[:, :, cs], yTo.ap, reads=[yTo], writes=[self.b_yt])
        for kb in range(8):
            pt_ = self.psum()
            for q in range(4):
                k = kb * 4 + q
                self.tr(pt_, pt_.ap[:, q * 128:(q + 1) * 128], hT, hT.ap[:, k * 128:(k + 1) * 128], ident.ap)
            self.act(yTo, yTo.ap[:, kb * 4:(kb + 1) * 4, :], pt_, pt_.ap.rearrange("p (k c) -> p k c", k=4), AF.Copy)
        self.dma('sp', dr['shp'].rearrange("(k p) n -> p k n", p=128), yTo.ap, reads=[yTo], is_out=True)
        self.P.barrier()
        A.off = mark
        ups = A.alloc([128, 48, 4], F32, 'sups')
        tp3 = A.alloc([128, 48, 4], F32, 'stp3')
        xcv = A.alloc([128, 48], F32, 'xcv')
        self.dma('sp', ups.ap[:, :, 0:3], dr['scstT'], writes=[ups])
        self.dve(lambda e: e.tensor_copy(ups.ap[:, :, 3:4], self.xbcs.ap), [self.xbcs], [ups])
        w2 = cv[:, 1363:1555].rearrange("p (k w) -> p k w", k=48)
        self.dve(lambda e: e.tensor_tensor(tp3.ap, ups.ap, w2, ALU.mult), [ups, self.cvec], [tp3])
        self.dve(lambda e: e.tensor_reduce(xcv.ap, tp3.ap, AX.X, ALU.add), [tp3], [xcv])
        self.dve(lambda e: e.tensor_tensor(xcv.ap, xcv.ap, cv[:, 1280:1328], ALU.add), [xcv, self.cvec], [xcv])
        self.act(xcv, xcv.ap, xcv, xcv.ap, AF.Silu)
        self.dma('sp', dr['scsT'], ups.ap[:, :, 1:4], reads=[ups], is_out=True)
        dd = A.alloc([128, 2], F32, 'dd')
        self.act(dd, dd.ap[0:64, 0:1], self.dts, self.dts.ap[0:64, :], AF.Exp, bias=cv[0:64, 1360:1361], reads=[self.cvec])
        self.act(dd, dd.ap[0:64, 0:1], dd, dd.ap[0:64, 0:1], AF.Ln, bias=ones.ap[0:64, 0:1])
        self.dve(lambda e: e.tensor_scalar(dd.ap[0:64, 1:2], dd.ap[0:64, 0:1], a_col, None, ALU.mult), [dd, sv], [dd])
        Gx = A.alloc([64, 4096], F32, 'Gx')
        self.dma('sp', Gx.ap, dr['gexp'], writes=[Gx])
        pE = self.psum()
        for k in range(32):
            self.mm(pE, pE.ap[:, k * 2:(k + 1) * 2], Gx, Gx.ap[:, k * 128:(k + 1) * 128], dd, dd.ap[0:64, 0:2], True, True)
        ex = A.alloc([128, 32, 2], F32, 'ex')
        self.act(ex, ex.ap, pE, pE.ap[:, 0:64].rearrange("p (k c) -> p k c", k=32), AF.Copy)
        edA = A.alloc([128, 32], F32, 'edA')
        dtx = A.alloc([128, 32], F32, 'dtx')
        self.act(edA, edA.ap, ex, ex.ap[:, :, 1], AF.Exp)
        self.dve(lambda e: e.tensor_tensor(dtx.ap, ex.ap[:, :, 0], xcv.ap[:, 0:32], ALU.mult), [ex, xcv], [dtx])
        BC = A.alloc([128, 2, 8, 128], F32, 'BC')
        dg = [A.alloc([128, 128], F32, f'dg{i}') for i in range(2)]
        i_ = 0
        for g in range(8):
            for wh, off in ((0, 32), (1, 40)):
                d_ = dg[i_ % 2]
                i_ += 1
                self.dve(lambda e, d=d_.ap, sc=xcv.ap[:, off + g:off + g + 1]: e.tensor_scalar(d, ident.ap, sc, None, ALU.mult),
                         [self.c_all, xcv], [d_])
                pb = self.psum()
                self.mm(pb, pb.ap[:, 0:128], ones, ones.ap, d_, d_.ap, True, True)
                self.act(BC, BC.ap[:, wh, g, :], pb, pb.ap[:, 0:128], AF.Copy)
        h0 = A.alloc([128, 32, 128], F32, 'h0')
        self.dma('sp', h0.ap, dr['sst'].rearrange("(k p) n -> p k n", p=128), writes=[h0])
        tn = [A.alloc([128, 128], F32, f'tn{i}') for i in range(2)]
        for k in range(32):
            g = k // 4
            hk = h0.ap[:, k, :]
            self.dve(lambda e, hk=hk, sc=edA.ap[:, k:k + 1]: e.tensor_scalar(hk, hk, sc, None, ALU.mult), [h0, edA], [h0])
            self.dve(lambda e, hk=hk, b=BC.ap[:, 0, g, :], sc=dtx.ap[:, k:k + 1]: e.scalar_tensor_tensor(hk, b, sc, hk, ALU.mult, ALU.add),
                     [h0, BC, dtx], [h0])
            t_ = tn[k % 2]
            self.dve(lambda e, t=t_.ap, hk=hk, cc=BC.ap[:, 1, g, :]: e.tensor_tensor(t, hk, cc, ALU.mult), [h0, BC], [t_])
            self.dve(lambda e, o=self.ys.ap[:, k, :], t=t_.ap: e.tensor_reduce(o, t, AX.X, ALU.add), [t_], [self.ys])
        ysa = self.ys.ap[:, :, 0]
        self.dve(lambda e: e.tensor_tensor(dtx.ap, xcv.ap[:, 0:32], cv[:, 1555:1587], ALU.mult), [xcv, self.cvec], [dtx])
        self.dve(lambda e: e.tensor_tensor(ysa, ysa, dtx.ap, ALU.add), [self.ys, dtx], [self.ys])
        self.dma('sp', dr['shs'].rearrange("(k p) n -> p k n", p=128), h0.ap, reads=[h0], is_out=True)

    def conf_in(self, segs, g):
        dr = self.dram
        w = dr['cpw1']
        cv = self.cvec.ap
        for oc in range(DC):
            s1 = self.wslot()
            s2 = self.wslot()
            a1 = s1.ap.rearrange("p (k c) -> p k c", k=16)
            a2 = s2.ap.rearrange("p (k c) -> p k c", k=16)
            self.dma('pool', a1, w[oc], writes=[s1])
            self.dma('pool', a2, w[oc + 16], writes=[s2])
            for si, sg in enumerate(segs):
                W = sg.W
                kd = 'p' if W > 1 else 's'
                pa = self.psum(kd)
                pg = self.psum(kd)
                for k in range(DC):
                    self.mm(pa, pa.ap[:, 0:W], s1, a1[:, k, :], sg.xb, sg.xb.ap[:, k, 0:W], k == 0, k == DC - 1)
                for k in range(DC):
                    self.mm(pg, pg.ap[:, 0:W], s2, a2[:, k, :], sg.xb, sg.xb.ap[:, k, 0:W], k == 0, k == DC - 1)
                sgm = self.tmp(kd)
                sa = sgm.ap[:, 0:W]
                self.act(sgm, sa, pg, pg.ap[:, 0:W], AF.Sigmoid, bias=cv[:, 16 + oc:17 + oc], reads=[self.cvec])
                paa = pa.ap[:, 0:W]
                ba = cv[:, oc:oc + 1]
                if si == 0:
                    st = self.stage(F32)
                    self.dve(lambda e, o=st.ap, paa=paa, ba=ba, sa=sa: e.scalar_tensor_tensor(o, paa, ba, sa, ALU.add, ALU.mult),
                             [pa, sgm, self.cvec], [st])
                    self.dma('sp', dr['UT'][oc * 128:(oc + 1) * 128, g * TG:(g + 1) * TG], st.ap, reads=[st], writes=[self.b_ut])
                else:
                    self.dve(lambda e, o=self.us.ap[:, oc, :], paa=paa, ba=ba, sa=sa: e.scalar_tensor_tensor(o, paa, ba, sa, ALU.add, ALU.mult),
                             [pa, sgm, self.cvec], [self.us])

    def conf_core(self):
        A = self.A
        dr = self.dram
        cv = self.cvec.ap
        upad = [A.alloc([128, 30 + T_], F32, f'upad{i}') for i in range(2)]
        acc = [A.alloc([128, T_], F32, f'acc{i}') for i in range(2)]
        for kp in range(DC // 2):
            ks = (2 * kp, 2 * kp + 1)
            for k in ks:
                up = upad[k % 2]
                self.dve(lambda e, a=up.ap[:, 0:30]: e.memset(a, 0.0), [], [up])
                self.dma('sp', up.ap[:, 30:30 + T_], dr['UT'][k * 128:(k + 1) * 128, :], reads=[self.b_ut], writes=[up])
            for w in range(31):
                for k in ks:
                    up, ac = upad[k % 2], acc[k % 2]
                    wc_ = cv[:, 96 + w * 16 + k:97 + w * 16 + k]
                    if w == 0:
                        self.dve(lambda e, o=ac.ap, i=up.ap[:, 0:T_], w0=wc_, b=cv[:, 32 + k:33 + k]: e.tensor_scalar(o, i, w0, b, ALU.mult, ALU.add),
                                 [up, self.cvec], [ac])
                    else:
                        self.dve(lambda e, o=ac.ap, i=up.ap[:, w:w + T_], ww=wc_: e.scalar_tensor_tensor(o, i, ww, o, ALU.mult, ALU.add),
                                 [up, ac, self.cvec], [ac])
            for k in ks:
                ac = acc[k % 2]
                self.dma('sp', dr['CT'][k * 128:(k + 1) * 128, :], ac.ap, reads=[ac], writes=[self.b_ct])
        self.dma('sp', dr['ccT'], dr['UT'][:, T_ - 30:T_], reads=[self.b_ut], is_out=True)
        ups = A.alloc([128, DC, 31], F32, 'ups')
        tp3 = A.alloc([128, DC, 31], F32, 'tp3')
        self.dma('sp', ups.ap[:, :, 0:30], dr['cstT'], writes=[ups])
        self.dve(lambda e: e.tensor_copy(ups.ap[:, :, 30:31], self.us.ap), [self.us], [ups])
        w2 = cv[:, 592:592 + 496].rearrange("p (k w) -> p k w", k=DC)
        self.dve(lambda e: e.tensor_tensor(tp3.ap, ups.ap, w2, ALU.mult), [ups, self.cvec], [tp3])
        csa = self.cs.ap[:, :, 0]
        self.dve(lambda e: e.tensor_reduce(csa, tp3.ap, AX.X, ALU.add), [tp3], [self.cs])
        self.dve(lambda e: e.tensor_tensor(csa, csa, cv[:, 32:48], ALU.add), [self.cs, self.cvec], [self.cs])
        self.dma('sp', dr['ccsT'], ups.ap[:, :, 1:31], reads=[ups], is_out=True)

    def mixer_core(self, l):
        if l == 1:
            self.ssm_core()
        if l == 2:
            self.conf_core()
        if l in (0, 3):
            kind = 'sb' if l == 0 else 'mb'
            mark = self.A.off
            self.attn_prompt(kind)
            self.P.barrier()
            self.A.off = mark
            if not self.cfg.get('no_sample_attn'):
                self.attn_sample(kind)

    def attn_sample(self, kind):
        A = self.A
        dr = self.dram
        ident = self.c_ident
        ones = self.c_ones
        c2d = dr['consts2']
        c2 = A.alloc([128, 2048], F32, 'c2')
        self.dma('sp', c2.ap, c2d, writes=[c2])
        Mx = c2.ap[:, 0:128]
        Hsame = c2.ap[:, 128:256]
        bmask = c2.ap[0:16, 256:260]
        Lall = c2.ap[0:16, 512:1536]
        kc_d = dr[kind + 'kc']
        vc_d = dr[kind + 'vc']
        ptd = dr['ptab']
        ptb = A.alloc([128, 128], I32, 'ptb')
        idx = A.alloc([128, 128], I32, 'idx')
        self.dma('sp', ptb.ap, bass.AP(ptd.tensor, 0, [[0, 128], [1, 128]]), writes=[ptb])
        siota = self.c_all.ap[:, 257:258]
        self.dve(lambda e: e.tensor_scalar(idx.ap, ptb.ap, 128.0, siota, ALU.mult, ALU.add), [ptb, self.c_all], [idx])
        qbc = A.alloc([128, 16, 128], F32, 'qbc')
        dg = [A.alloc([128, 128], F32, f'adg{i}') for i in range(2)]
        for hb in range(4):
            pb = self.psum()
            for q in range(4):
                h = hb * 4 + q
                d_ = dg[h % 2]
                self.dve(lambda e, d=d_.ap, sc=self.qs.ap[:, h, :]: e.tensor_scalar(d, ident.ap, sc, None, ALU.mult), [self.c_all, self.qs], [d_])
                self.mm(pb, pb.ap[:, q * 128:(q + 1) * 128], ones, ones.ap, d_, d_.ap, True, True)
            self.act(qbc, qbc.ap[:, hb * 4:(hb + 1) * 4, :], pb, pb.ap.rearrange("p (k c) -> p k c", k=4), AF.Copy)
        KP = [A.alloc([128, 512], F32, f'KP{i}') for i in range(4)]
        prod = [A.alloc([128, 16, 128], F32, f'prod{i}') for i in range(2)]
        zT2 = A.alloc([128, 16, 8, 16], F32, 'zT2')
        q4 = qbc.ap.rearrange("p (a b) d -> p a b d", a=4)

        def gather(dst, src_d, i):
            ia = idx.ap[:, i:i + 1]
            return self.P.op('pool', lambda e: e.indirect_dma_start(out=dst.ap, out_offset=None, in_=src_d,
                                                                    in_offset=bass.IndirectOffsetOnAxis(ap=ia, axis=0)),
                             reads=[idx], writes=[dst], dma=True)
        def mulstep(i):
            kp = KP[i % 4]
            gather(kp, kc_d, i)
            pr = prod[i % 2]
            ka = kp.ap
            kl = [list(x) for x in ka.ap]
            k4 = bass.AP(ka.tensor, ka.offset, [kl[0], [128, 4], [0, 4], [1, 128]])
            self.dve(lambda e, o=pr.ap.rearrange("p (a b) d -> p a b d", a=4), k4=k4: e.tensor_tensor(o, k4, q4, ALU.mult), [kp, qbc], [pr])

        def redstep(i):
            g8, pl = divmod(i, 16)
            pr = prod[i % 2]
            self.dve(lambda e, o=zT2.ap[:, pl, g8, :], p=pr.ap: e.tensor_reduce(o, p, AX.X, ALU.add), [pr], [zT2])
        mulstep(0)
        for i in range(128):
            if i + 1 < 128:
                mulstep(i + 1)
            redstep(i)
        Z = A.alloc([128, 2048], F32, 'sZ')
        E = A.alloc([128, 2048], F32, 'sE')
        ZS = A.alloc([128, 2048], F32, 'sZS')
        for pb4 in range(4):
            pt_ = self.psum()
            for q in range(4):
                pl = pb4 * 4 + q
                self.tr(pt_, pt_.ap[:, q * 128:(q + 1) * 128], zT2, zT2.ap[:, pl].rearrange("p a b -> p (a b)"), ident.ap)
            self.act(Z, Z.ap[:, pb4 * 512:(pb4 + 1) * 512], pt_, pt_.ap, AF.Copy)
        sm = A.alloc([128, 64], F32, 'ssm')
        if kind == 'sb':
            C = A.alloc([128, 2048], F32, 'sC')
            o2k = A.alloc([128, 2048], F32, 'so2k')
            self.dve(lambda e: e.memset(o2k.ap, 1.0), [], [o2k])
            self.act(E, E.ap, Z, Z.ap, AF.Exp, scale=SCALE)
            self.act(E, E.ap, E, E.ap, AF.Ln, bias=ones.ap[:, 0:1])
            self.dve(lambda e: e.scalar_tensor_tensor(ZS.ap, Z.ap, SCALE, E.ap, ALU.mult, ALU.subtract), [Z, E], [ZS])
            self.dve(lambda e: e.tensor_tensor_scan(C.ap, o2k.ap, E.ap, 0.0, ALU.mult, ALU.add), [E, o2k], [C])
            tot = sm.ap[:, 0:1]
            self.dve(lambda e: e.tensor_copy(tot, C.ap[:, 2047:2048]), [C], [sm])
            pc = self.psum()
            self.mm(pc, pc.ap[:, 0:1], c2, Mx, sm, tot, True, True)
            nb = sm.ap[:, 1:2]
            self.dve(lambda e: e.scalar_tensor_tensor(nb, pc.ap[:, 0:1], -1.0, tot, ALU.mult, ALU.subtract), [pc, sm], [sm])
            self.dve(lambda e: e.tensor_tensor(ZS.ap, ZS.ap, C.ap, ALU.add), [ZS, C], [ZS])
            Pm = Z
            self.act(Pm, Pm.ap, ZS, ZS.ap, AF.Exp, bias=nb, reads=[sm])
        else:
            grow = sm.ap[:, 0:8]
            self.dve(lambda e: e.tensor_reduce(grow, Z.ap.rearrange("p (b s) -> p b s", b=8), AX.X, ALU.add), [Z], [sm])
            pg = self.psum()
            for g8 in range(8):
                self.mm(pg, pg.ap[0:16, g8 * 8:(g8 + 1) * 8], self.c_all, ident.ap[:, g8 * 16:(g8 + 1) * 16], sm, grow, True, True)
            s16 = A.alloc([16, 160], F32, 's16')
            g16 = s16.ap[:, 0:64]
            self.act(s16, g16, pg, pg.ap[0:16, 0:64], AF.Copy)
            t8 = s16.ap[:, 64:72]
            self.dve(lambda e: e.max(t8, g16), [s16], [s16])
            sb16 = s16.ap[:, 80:144]
            self.dve(lambda e: e.tensor_scalar(sb16, g16, t8[:, 2:3], -NEG, ALU.is_ge, ALU.mult), [s16], [s16])
            self.dve(lambda e: e.tensor_scalar(sb16, sb16, NEG, None, ALU.add), [s16], [s16])
            psb = self.psum()
            for g8 in range(8):
                self.mm(psb, psb.ap[:, 0:8], c2, Lall[:, g8 * 128:(g8 + 1) * 128], s16, sb16[:, g8 * 8:(g8 + 1) * 8], g8 == 0, g8 == 7)
            sbr = sm.ap[:, 8:16]
            self.act(sm, sbr, psb, psb.ap[:, 0:8], AF.Copy)
            for bl in range(8):
                self.dve(lambda e, o=E.ap[:, bl * 256:(bl + 1) * 256], z=Z.ap[:, bl * 256:(bl + 1) * 256], b=sbr[:, bl:bl + 1]:
                         e.tensor_scalar(o, z, SCALE, b, ALU.mult, ALU.add), [Z, sm], [E])
            p16 = A.alloc([128, 16], F32, 'p16')
            ksa = self.kvs.ap[:, 0:4, 0]
            kl2 = [list(x) for x in ksa.ap]
            k44 = bass.AP(ksa.tensor, ksa.offset, [kl2[0], kl2[1], [0, 4]])
            self.dve(lambda e: e.tensor_tensor(p16.ap.rearrange("p (a b) -> p a b", a=4), self.qs.ap[:, :, 0].rearrange("p (a b) -> p a b", a=4), k44, ALU.mult),
                     [self.qs, self.kvs], [p16])
            prep = A.alloc([128, 8, 16], F32, 'prep')
            pl_ = [list(x) for x in p16.ap.ap]
            p16b = bass.AP(p16.ap.tensor, p16.ap.offset, [pl_[0], [0, 8], pl_[1]])
            self.dve(lambda e: e.tensor_copy(prep.ap, p16b), [p16], [prep])
            pz = self.psum()
            self.mm(pz, pz.ap[:, 0:1], prep, prep.ap.rearrange("p a b -> p (a b)"), self.c_all, ones.ap[:, 0:1], True, True)
            negm = sm.ap[:, 16:17]
            self.dve(lambda e: e.tensor_scalar(negm, pz.ap[:, 0:1], -SCALE, None, ALU.mult), [pz], [sm])
            self.act(ZS, ZS.ap, E, E.ap, AF.Exp, bias=negm, reads=[sm])
            rs = sm.ap[:, 17:18]
            self.dve(lambda e: e.tensor_reduce(rs, ZS.ap, AX.X, ALU.add), [ZS], [sm])
            pd = self.psum()
            self.mm(pd, pd.ap[:, 0:1], c2, Hsame, sm, rs, True, True)
            rden = sm.ap[:, 18:19]
            self.dve(lambda e: e.tensor_scalar(rden, pd.ap[:, 0:1], 1.0, None, ALU.add), [pd], [sm])
            self.dve(lambda e: e.reciprocal(rden, rden), [sm], [sm])
            Pm = Z
            self.dve(lambda e: e.tensor_scalar(Pm.ap, ZS.ap, rden, None, ALU.mult), [ZS, sm], [Pm])
            Rm = dg[0]
            self.dve(lambda e: e.tensor_scalar(Rm.ap, ones.ap, rden, None, ALU.mult), [self.c_all, sm], [Rm])
            pbc = self.psum()
            self.mm(pbc, pbc.ap[:, 0:16], Rm, Rm.ap, self.c_all, ident.ap[:, 0:16], True, True)
            pown = A.alloc([128, 16], F32, 'pown')
            self.act(pown, pown.ap, pbc, pbc.ap[:, 0:16], AF.Copy)
            diagV = A.alloc([128, 4, 128], F32, 'diagV')
            for kvh in range(4):
                self.dve(lambda e, o=diagV.ap[:, kvh, :], sc=self.kvs.ap[:, 4 + kvh, :]: e.tensor_scalar(o, ident.ap, sc, None, ALU.mult),
                         [self.c_all, self.kvs], [diagV])
        PT = A.alloc([128, 16, 128], F32, 'sPT')
        for pb4 in range(4):
            pt_ = self.psum()
            for q in range(4):
                pl = pb4 * 4 + q
                self.tr(pt_, pt_.ap[:, q * 128:(q + 1) * 128], Pm, Pm.ap[:, pl * 128:(pl + 1) * 128], ident.ap)
            self.act(PT, PT.ap[:, pb4 * 4:(pb4 + 1) * 4, :], pt_, pt_.ap.rearrange("p (k c) -> p k c", k=4), AF.Copy)
        po = self.ps[7]
        for i in range(128):
            g8, pl = divmod(i, 16)
            vp = KP[i % 4]
            gather(vp, vc_d, i)
            self.mm(po, po.ap[0:16, :], PT, PT.ap[:, pl, g8 * 16:(g8 + 1) * 16], vp, vp.ap, i == 0, (i == 127 and kind == 'sb'))
        if kind == 'mb':
            self.mm(po, po.ap[0:16, :], pown, pown.ap, diagV, diagV.ap.rearrange("p a b -> p (a b)"), False, True)
        t16 = A.alloc([16, 4, 128], F32, 't16')
        r16 = A.alloc([16, 128], F32, 'r16')
        bl_ = [list(x) for x in bmask.ap]
        bm3 = bass.AP(bmask.tensor, bmask.offset, [bl_[0], bl_[1], [0, 128]])
        self.dve(lambda e: e.tensor_tensor(t16.ap, po.ap[0:16, :].rearrange("p (k d) -> p k d", k=4), bm3, ALU.mult), [po, c2], [t16])
        self.dve(lambda e: e.tensor_reduce(r16.ap, t16.ap.rearrange("p k d -> p d k"), AX.X, ALU.add), [t16], [r16])
        pf = self.psum()
        self.tr(pf, pf.ap[:, 0:16], r16, r16.ap, ident.ap[0:16, 0:16])
        self.act(self.aos, self.aos.ap[:, 0:16, 0], pf, pf.ap[:, 0:16], AF.Copy)

    def attn_prompt(self, kind):
        A = self.A
        dr = self.dram
        pfx = kind
        KTs = [A.alloc([128, T_], BF16, f'KT{i}') for i in range(2)]
        VTs = [A.alloc([128, T_], BF16, f'VT{i}') for i in range(2)]
        Vs = [A.alloc([128, 16, 128], BF16, f'V{i}') for i in range(2)]
        QTh = [A.alloc([128, T_], BF16, f'QTh{i}') for i in range(2)]
        AOh = [A.alloc([128, T_], BF16, f'AOh{i}') for i in range(2)]
        NS = 3
        WE = [A.alloc([128, T_], F32, f'WE{i}') for i in range(NS)]
        WZ = [A.alloc([128, T_], F32, f'WZ{i}') for i in range(NS)]
        WP = [A.alloc([128, T_], BF16, f'WP{i}') for i in range(NS)]
        PTs = [A.alloc([128, 512], BF16, f'PTs{i}') for i in range(3)]
        sm = [A.alloc([128, 32], F32, f'sm{i}') for i in range(NS)]
        if kind == 'sb':
            WC = [A.alloc([128, T_], F32, f'WC{i}') for i in range(2)]
            ones2k = A.alloc([128, T_], F32, 'ones2k')
            self.dve(lambda e: e.memset(ones2k.ap, 1.0), [], [ones2k])
        else:
            KMs = [A.alloc([128, 8], F32, f'KM{i}') for i in range(2)]
            KMbs = [A.alloc([128, 8], BF16, f'KMb{i}') for i in range(2)]
        identb = self.c_identb
        st = {'pv_rr': 0}

        def kv_prologue(kvh):
            KT, VT, V = KTs[kvh % 2], VTs[kvh % 2], Vs[kvh % 2]
            self.dma('pool', KT.ap, dr[pfx + 'kT'][kvh * 128:(kvh + 1) * 128, :], reads=[self.b_kv], writes=[KT])
            self.dma('pool', VT.ap, dr[pfx + 'vT'][kvh * 128:(kvh + 1) * 128, :], reads=[self.b_kv], writes=[VT])
            for jb in range(4):
                pst = self.ps[4 + (jb % 2)]
                psb = pst.ap.bitcast(BF16)
                for q in range(4):
                    j = jb * 4 + q
                    self.tr(pst, psb[:, q * 128:(q + 1) * 128], VT, VT.ap[:, j * 128:(j + 1) * 128], identb.ap)
                va = V.ap[:, jb * 4:(jb + 1) * 4, :]
                self.act(V, va, pst, psb[:, 0:512].rearrange("p (k c) -> p k c", k=4), AF.Copy)
            if kind == 'mb':
                KM, KMb = KMs[kvh % 2], KMbs[kvh % 2]
                self.dve(lambda e: e.tensor_reduce(KM.ap, KT.ap.rearrange("p (b s) -> p b s", b=8), AX.X, ALU.add), [KT], [KM])
                self.dve(lambda e: e.tensor_scalar(KMb.ap, KM.ap, 1.0 / 256, None, ALU.mult), [KM], [KMb])

        def stage_a1(it, kvh, hq, tt):
            h = kvh * 4 + hq
            KT = KTs[kvh % 2]
            Q = QTh[h % 2]
            if tt == 0:
                if hq == 0:
                    kv_prologue(kvh)
                self.dma('sp', Q.ap, dr['QT'][h], reads=[self.b_qt], writes=[Q])
            L = (tt + 1) * 128
            nb = (L + 511) // 512
            E, Z, s_ = WE[it % NS], WZ[it % NS], sm[it % NS]
            qa = Q.ap[:, tt * 128:(tt + 1) * 128]
            bmat = self.c_blt_b if kind == 'sb' else self.c_ble_b
            for j in range(nb):
                cols = min(512, L - j * 512)
                last = (j == nb - 1)
                self.mm(self.ps[j], self.ps[j].ap[:, 0:cols], Q, qa, KT, KT.ap[:, j * 512:j * 512 + cols], True, not last)
                if last:
                    self.mm(self.ps[j], self.ps[j].ap[:, cols - 128:cols], identb, identb.ap, bmat, bmat.ap, False, True)
            if kind == 'sb':
                for j in range(nb):
                    cols = min(512, L - j * 512)
                    sl = slice(j * 512, j * 512 + cols)
                    self.act(E, E.ap[:, sl], self.ps[j], self.ps[j].ap[:, 0:cols], AF.Exp, scale=SCALE)
                for j in range(nb):
                    cols = min(512, L - j * 512)
                    sl = slice(j * 512, j * 512 + cols)
                    self.act(Z, Z.ap[:, sl], self.ps[j], self.ps[j].ap[:, 0:cols], AF.Copy, scale=SCALE)
                self.act(E, E.ap[:, 0:L], E, E.ap[:, 0:L], AF.Ln, bias=self.c_ones.ap[:, 0:1])
            else:
                n = tt // 2
                selb = s_.ap[:, 8:16]
                if n >= 4:
                    KMb = KMbs[kvh % 2]
                    gp = self.ps[5]
                    self.mm(gp, gp.ap[:, 0:8], Q, qa, KMb, KMb.ap, True, True)
                    gt = s_.ap[:, 0:8]
                    self.dve(lambda e, gt=gt: e.memset(gt, -1e30), [], [s_])
                    self.dve(lambda e, gt=gt, n=n, gp=gp: e.tensor_copy(gt[:, 0:n], gp.ap[:, 0:n]), [gp], [s_])
                    t8 = s_.ap[:, 16:24]
                    self.dve(lambda e, t8=t8, gt=gt: e.max(t8, gt), [s_], [s_])
                    self.dve(lambda e, selb=selb, gt=gt, t8=t8: e.tensor_scalar(selb, gt, t8[:, 2:3], -NEG, ALU.is_ge, ALU.mult), [s_], [s_])
                    self.dve(lambda e, selb=selb: e.tensor_scalar(selb, selb, NEG, None, ALU.add), [s_], [s_])
                else:
                    self.dve(lambda e, selb=selb: e.memset(selb, 0.0), [], [s_])
                for blk in range(n + 1):
                    j = blk // 2
                    c0 = (blk % 2) * 256
                    pst = self.ps[j]
                    if blk < n:
                        self.dve(lambda e, z=Z.ap[:, blk * 256:(blk + 1) * 256], p=pst.ap[:, c0:c0 + 256], b=selb[:, blk:blk + 1]:
                                 e.tensor_scalar(z, p, SCALE, b, ALU.mult, ALU.add), [pst, s_], [Z])
                    else:
                        wd = L - n * 256
                        self.dve(lambda e, z=Z.ap[:, blk * 256:blk * 256 + wd], p=pst.ap[:, c0:c0 + wd]:
                                 e.tensor_scalar(z, p, SCALE, None, ALU.mult), [pst], [Z])

        def stage_a2(it, kvh, hq, tt):
            L = (tt + 1) * 128
            E, Z, Pb, s_ = WE[it % NS], WZ[it % NS], WP[it % NS], sm[it % NS]
            if kind == 'sb':
                C = WC[it % 2]
                self.dve(lambda e, c=C.ap[:, 0:L], o1=ones2k.ap[:, 0:L], sp=E.ap[:, 0:L]:
                         e.tensor_tensor_scan(c, o1, sp, 0.0, ALU.mult, ALU.subtract), [E, ones2k], [C])
                self.dve(lambda e, z=Z.ap[:, 0:L], sp=E.ap[:, 0:L]: e.tensor_tensor(z, z, sp, ALU.subtract), [Z, E], [Z])
                self.dve(lambda e, z=Z.ap[:, 0:L], c=C.ap[:, 0:L]: e.tensor_tensor(z, z, c, ALU.subtract), [Z, C], [Z])
                self.act(Pb, Pb.ap[:, 0:L], Z, Z.ap[:, 0:L], AF.Exp, bias=C.ap[:, L - 1:L], reads=[C])
            else:
                nm = s_.ap[:, 24:25]
                self.dve(lambda e, nm=nm, z=Z.ap[:, 0:L]: e.tensor_reduce(nm, z, AX.X, ALU.max, negate=True), [Z], [s_])
                self.act(E, E.ap[:, 0:L], Z, Z.ap[:, 0:L], AF.Exp, bias=nm, reads=[s_])
                dn = s_.ap[:, 25:26]
                self.dve(lambda e, dn=dn, ea=E.ap[:, 0:L]: e.tensor_reduce(dn, ea, AX.X, ALU.add), [E], [s_])
                self.dve(lambda e, dn=dn: e.reciprocal(dn, dn), [s_], [s_])
                self.dve(lambda e, dn=dn, pb=Pb.ap[:, 0:L], ea=E.ap[:, 0:L]: e.tensor_scalar(pb, ea, dn, None, ALU.mult), [E, s_], [Pb])

        def stage_b(it, kvh, hq, tt):
            h = kvh * 4 + hq
            V = Vs[kvh % 2]
            Pb = WP[it % NS]
            AOt = AOh[h % 2]
            aops = self.ps[6 + (it % 2)]
            for jb in range((tt + 4) // 4):
                nq = min(4, tt + 1 - jb * 4)
                pv_rr = st['pv_rr']
                pst = self.ps[4 + (pv_rr % 2)]
                psb = pst.ap.bitcast(BF16)
                pts = PTs[pv_rr % 3]
                for q in range(nq):
                    j = jb * 4 + q
                    self.tr(pst, psb[:, q * 128:(q + 1) * 128], Pb, Pb.ap[:, j * 128:(j + 1) * 128], identb.ap)
                if pv_rr % 2 == 0:
                    self.act(pts, pts.ap[:, 0:nq * 128], pst, psb[:, 0:nq * 128], AF.Copy)
                else:
                    self.dve(lambda e, o=pts.ap[:, 0:nq * 128], i=psb[:, 0:nq * 128]: e.tensor_copy(o, i), [pst], [pts])
                for q in range(nq):
                    j = jb * 4 + q
                    self.mm(aops, aops.ap[:, 0:128], V, V.ap[:, j, :], pts, pts.ap[:, q * 128:(q + 1) * 128], j == 0, j == tt)
                st['pv_rr'] += 1
            self.act(AOt, AOt.ap[:, tt * 128:(tt + 1) * 128], aops, aops.ap[:, 0:128], AF.Copy)
            if tt == 15:
                self.dma('sp', dr['AO'][h], AOt.ap, reads=[AOt], writes=[self.b_ao])

        iters = [(kvh, hq, tt) for kvh in range(4) for hq in range(4) for tt in range(16)]
        N = len(iters)
        for i in range(N + 2):
            if i < N:
                stage_a1(i, *iters[i])
            if 0 <= i - 1 < N:
                stage_a2(i - 1, *iters[i - 1])
            if 0 <= i - 2 < N:
                stage_b(i - 2, *iters[i - 2])


def tile_w(W):
    K_, N_ = W.shape
    return np.ascontiguousarray(W.reshape(K_ // 128, 128, N_ // 128, 128).transpose(2, 1, 0, 3))


def make_consts():
    c = np.zeros((128, 1024), np.float32)
    c[:, 0:128] = 1.0
    c[:, 128:256] = np.eye(128, dtype=np.float32)
    c[:, 256] = EPS
    c[:, 257] = np.arange(128)
    t = np.arange(128)[:, None]
    s = np.arange(128)[None, :]
    c[:, 384:512] = (s < t).astype(np.float32)
    c[:, 512:640] = np.where(s <= t, 0.0, NEG).astype(np.float32)
    c[:, 640:768] = (t <= s).astype(np.float32)
    c[:, 768:896] = (t > s).astype(np.float32)
    c[:, 896:1024] = np.where(s < t, 0.0, NEG).astype(np.float32)
    return c


def pk(v):
    return np.ascontiguousarray(np.asarray(v).reshape(-1, 128).T)


def make_consts2():
    c = np.zeros((128, 2048), np.float32)
    r = np.arange(128)
    g, h = r // 16, r % 16
    c[:, 0:128] = ((h[:, None] == h[None, :]) & (g[:, None] > g[None, :])).astype(np.float32)
    c[:, 128:256] = (h[:, None] == h[None, :]).astype(np.float32)
    c[0:16, 256:260] = (np.arange(16)[:, None] // 4 == np.arange(4)[None, :]).astype(np.float32)
    for g8 in range(8):
        for hh in range(16):
            c[hh, 512 + g8 * 128 + g8 * 16 + hh] = 1.0
    return c


def make_cvec(inp):
    c = np.zeros((128, 2048), np.float32)
    c[:, 0:32] = pk(inp['conf_b_pw1'][0])
    c[:, 32:48] = pk(inp['conf_b_dw'][0])
    c[:, 48:64] = pk(inp['conf_ln_g'][0])
    c[:, 64:80] = pk(inp['conf_ln_b'][0])
    c[:, 80:96] = pk(inp['conf_b_pw2'][0])
    wdw = inp['conf_w_dw'][0]
    w3 = wdw.reshape(31, 16, 128).transpose(2, 0, 1)
    c[:, 96:592] = w3.reshape(128, 496)
    c[:, 592:1088] = w3.transpose(0, 2, 1).reshape(128, 496)
    cw = inp['ssm_conv_w'][0].reshape(4, 48, 128).transpose(2, 0, 1)
    c[:, 1088:1280] = cw.reshape(128, 192)
    c[:, 1280:1328] = pk(inp['ssm_conv_b'][0])
    c[:, 1328:1360] = pk(inp['ssm_norm_g'][0])
    c[0:64, 1360] = inp['ssm_dt_bias'][0]
    c[0:64, 1361] = inp['ssm_a_log'][0]
    c[0:64, 1362] = inp['ssm_d'][0]
    c[:, 1363:1555] = cw.transpose(0, 2, 1).reshape(128, 192)
    c[:, 1555:1587] = pk(np.repeat(inp['ssm_d'][0], 64))
    return c


def shared_builders(inp):
    B = {}
    B['consts'] = make_consts
    B['lng'] = lambda: np.ascontiguousarray(inp['ln_g'].reshape(16, 16, 128).transpose(2, 0, 1).reshape(128, 256))
    B['lnb'] = lambda: np.ascontiguousarray(inp['ln_b'].reshape(16, 16, 128).transpose(2, 0, 1).reshape(128, 256))
    for nm, src in (('w1', 'ffn_w1'), ('w3', 'ffn_w3'), ('w2', 'ffn_w2')):
        B[nm] = lambda src=src: np.stack([np.stack([tile_w(inp[src][l, j]) for j in range(2)]) for l in range(NL)])
    B['wgate'] = lambda: np.stack([tile_w(inp['ple_w_gate'][l]) for l in range(NL)])
    B['wproj'] = lambda: np.stack([tile_w(inp['ple_w_proj'][l]) for l in range(NL)])
    B['sbqkv'] = lambda: tile_w(inp['sb_w_qkv'][0])
    B['sbo'] = lambda: tile_w(inp['sb_w_o'][0])
    B['mbqkv'] = lambda: tile_w(inp['moba_w_qkv'][0])
    B['mbo'] = lambda: tile_w(inp['moba_w_o'][0])
    B['cpw1'] = lambda: tile_w(inp['conf_w_pw1'][0])
    B['cpw2'] = lambda: tile_w(inp['conf_w_pw2'][0])
    B['cvec'] = lambda: make_cvec(inp)
    B['consts2'] = make_consts2
    B['sbkc'] = lambda: inp['cache_sb_k'][0].reshape(1280 * 128, 512)
    B['sbvc'] = lambda: inp['cache_sb_v'][0].reshape(1280 * 128, 512)
    B['mbkc'] = lambda: inp['cache_moba_k'][0].reshape(1280 * 128, 512)
    B['mbvc'] = lambda: inp['cache_moba_v'][0].reshape(1280 * 128, 512)
    B['ssmin'] = lambda: tile_w(np.concatenate([inp['ssm_w_in'][0], np.zeros((D, 64), np.float32)], axis=1))
    B['ssmout'] = lambda: tile_w(inp['ssm_w_out'][0])
    B['gexp'] = lambda: (np.arange(64)[:, None] == (np.arange(4096)[None, :] // 64)).astype(np.float32)
    return B


def prep_shared(inp):
    return {k: f() for k, f in shared_builders(inp).items()}


def prep_core(inp, c):
    b = c // 2
    m = {}
    m['xT'] = np.ascontiguousarray(inp['x_prompt'][b].T)
    m['xsT'] = np.ascontiguousarray(inp['x_sample'][c, 0].reshape(DC, 128).T)
    m['pT'] = np.ascontiguousarray(inp['p_prompt'][:, b].transpose(0, 2, 1))
    m['psT'] = np.ascontiguousarray(inp['p_sample'][:, c, 0].reshape(NL, 2, 128).transpose(0, 2, 1))
    m['ptab'] = np.ascontiguousarray(inp['page_table'][c:c + 1].astype(np.int32))
    m['sst'] = np.ascontiguousarray(inp['state_ssm'][0, c].reshape(4096, 128))
    m['scstT'] = np.ascontiguousarray(inp['state_ssm_conv'][0, c].T.reshape(48, 128, 3).transpose(1, 0, 2))
    m['cstT'] = np.ascontiguousarray(inp['state_conf_conv'][0, c].T.reshape(DC, 128, 30).transpose(1, 0, 2))
    return m


_CACHE = {}


def get_nc(cfg):
    key = tuple(sorted(cfg.items()))
    if key not in _CACHE:
        nc = bass.Bass("TRN2", target_bir_lowering=False)
        k = K(nc, cfg)
        k.build()
        _CACHE[key] = (nc, k)
    return _CACHE[key]


def run(inp, cfg, cores=8):
    nc, k = get_nc(cfg)
    sh = prep_shared(inp)
    names = [n for n, t in k.dram.items()]
    in_maps = []
    for c in range(cores):
        m = dict(sh)
        m.update(prep_core(inp, c))
        in_maps.append(m)
    res = run_bass_kernel_spmd(nc, in_maps, core_ids=list(range(cores)))
    return res.results


def kernel(**inp):
    inp = {k: np.asarray(v) for k, v in inp.items()}
    r = run(inp, {})
    f32 = np.float32
    yp = np.stack([r[2 * b]['yT'].T for b in range(4)]).astype(f32)
    ys = np.stack([r[c]['ysT'].T.reshape(1, D) for c in range(8)]).astype(f32)

    def kvp(nm):
        return np.stack([r[2 * b][nm].T.reshape(T_, 4, 128) for b in range(4)])[None].astype(f32)

    def kvs(nm):
        return np.stack([r[c][nm].T.reshape(1, 4, 128) for c in range(8)])[None].astype(f32)
    shp = np.stack([r[2 * b]['shp'].reshape(64, 64, 128) for b in range(4)])[None].astype(f32)
    shs = np.stack([r[c]['shs'].reshape(64, 64, 128) for c in range(8)])[None].astype(f32)
    scp = np.stack([r[2 * b]['scT'].T for b in range(4)])[None].astype(f32)
    scs = np.stack([r[c]['scsT'].transpose(1, 0, 2).reshape(6144, 3).T for c in range(8)])[None].astype(f32)
    ccp = np.stack([r[2 * b]['ccT'].T for b in range(4)])[None].astype(f32)
    ccs = np.stack([r[c]['ccsT'].transpose(1, 0, 2).reshape(D, 30).T for c in range(8)])[None].astype(f32)
    return (yp, ys, kvp('sbkT'), kvp('sbvT'), kvs('sbksT'), kvs('sbvsT'), shp, shs, scp, scs, ccp, ccs,
            kvp('mbkT'), kvp('mbvT'), kvs('mbksT'), kvs('mbvsT'))
```

```python
import numpy as np
import concourse.bass as bass
import concourse.mybir as mybir

F32 = mybir.dt.float32
BF16 = mybir.dt.bfloat16
I32 = mybir.dt.int32
U32 = mybir.dt.uint32
AF = mybir.ActivationFunctionType
ALU = mybir.AluOpType
AX = mybir.AxisListType

ENGS = ['pe', 'act', 'dve', 'pool', 'sp']
N_DMA_SEMS = 12


class Buf:
    __slots__ = ('name', 'w', 'rs')

    def __init__(self, name=''):
        self.name = name
        self.w = None
        self.rs = []


class T:
    __slots__ = ('ap', 'buf')

    def __init__(self, ap, name=''):
        self.ap = ap
        self.buf = Buf(name)


class Op:
    __slots__ = ('eng', 'fn', 'deps', 'dma', 'sem', 'val', 'marked', 'idx')

    def __init__(self, eng, fn, dma):
        self.eng = eng
        self.fn = fn
        self.deps = set()
        self.dma = dma
        self.sem = None
        self.val = None
        self.marked = False
        self.idx = -1


class Prog:
    def __init__(self, nc):
        self.nc = nc
        self.streams = {e: [] for e in ENGS}
        self.dma_count = {e: 0 for e in ENGS}
        self.dma_last = {}
        self.last_real = {e: None for e in ENGS}
        self.pending_dmas = []
        self.out_dmas = []
        self.arena_off = 0

    def op(self, eng, fn, reads=(), writes=(), dma=False, out=False):
        o = Op(eng, fn, dma)
        for t in reads:
            b = t.buf if isinstance(t, T) else t
            if b.w is not None:
                o.deps.add(b.w)
        for t in writes:
            b = t.buf if isinstance(t, T) else t
            if b.w is not None:
                o.deps.add(b.w)
            for r in b.rs:
                o.deps.add(r)
        if dma:
            slot = self.dma_count[eng] % N_DMA_SEMS
            self.dma_count[eng] += 1
            prev = self.dma_last.get((eng, slot))
            if prev is not None:
                o.deps.add(prev)
            self.dma_last[(eng, slot)] = o
            o.sem = ('dma', eng, slot)
            self.pending_dmas.append(o)
            if out:
                self.out_dmas.append(o)
        else:
            if eng == 'pe':
                o.deps = {d for d in o.deps if not (d.eng == 'pe' and not d.dma)}
            if fn is not None:
                self.last_real[eng] = o
        o.deps.discard(o)
        for t in reads:
            b = t.buf if isinstance(t, T) else t
            b.rs.append(o)
        for t in writes:
            b = t.buf if isinstance(t, T) else t
            b.w = o
            b.rs = []
        o.idx = len(self.streams[eng])
        self.streams[eng].append(o)
        return o

    def barrier(self):
        lasts = [self.last_real[e] for e in ENGS if self.last_real[e] is not None]
        pend = list(self.pending_dmas)
        self.pending_dmas = []
        for e in ENGS:
            o = Op(e, None, False)
            o.deps = set(lasts) | set(pend)
            o.idx = len(self.streams[e])
            self.streams[e].append(o)

    def finish(self):
        o = Op('sp', None, False)
        o.deps = set(self.out_dmas) | set(self.pending_dmas)
        self.streams['sp'].append(o)

    def emit(self, block, sems):
        for e in ENGS:
            for o in self.streams[e]:
                for d in o.deps:
                    d.marked = True
        for e in ENGS:
            cnt = 0
            dcnt = {}
            for o in self.streams[e]:
                if o.dma:
                    dcnt[o.sem] = dcnt.get(o.sem, 0) + 16
                    o.val = dcnt[o.sem]
                elif o.fn is not None and o.marked:
                    cnt += 1
                    o.sem = ('e', e)
                    o.val = cnt
        prog = self

        def run(e, eng):
            seen = {}
            n_wait = 0
            for o in prog.streams[e]:
                need = {}
                for d in o.deps:
                    if d.val is None:
                        continue
                    if need.get(d.sem, 0) < d.val:
                        need[d.sem] = d.val
                for s, v in need.items():
                    if seen.get(s, 0) < v:
                        eng.wait_ge(sems[s], v)
                        seen[s] = v
                        n_wait += 1
                if o.fn is None:
                    continue
                ins = o.fn(eng)
                if o.dma:
                    ins.then_inc(sems[o.sem], 16)
                elif o.marked:
                    ins.then_inc(sems[o.sem], 1)

        @block.tensor
        def _(eng):
            run('pe', eng)

        @block.scalar
        def _(eng):
            run('act', eng)

        @block.vector
        def _(eng):
            run('dve', eng)

        @block.gpsimd
        def _(eng):
            run('pool', eng)

        @block.sync
        def _(eng):
            run('sp', eng)

    def sem_names(self):
        names = [('e', e) for e in ENGS]
        for e in ('sp', 'act', 'pool'):
            for s in range(N_DMA_SEMS):
                names.append(('dma', e, s))
        return names


import math
from contextlib import ExitStack
from concourse.bass_utils import run_bass_kernel_spmd

D = 2048
DC = 16
FF = 5632
FC = 44
T_ = 2048
TG = 512
NG = 4
NL = 4
ALPHA = (2 * NL) ** 0.25
EPS = 1e-5
SCALE = 128 ** -0.5
NEG = -30000.0
N_WSLOT = 8
ARENA_WORDS = 51500


def _prod(s):
    r = 1
    for v in s:
        r *= v
    return r


class Arena:
    def __init__(self, sb, nwords):
        self.sb = sb
        self.n = nwords
        self.off = 0

    def alloc(self, shape, dtype, name=''):
        nfree = _prod(shape[1:])
        words = nfree if dtype in (F32, I32, U32) else (nfree + 1) // 2
        a = self.sb[0:shape[0], self.off:self.off + words]
        self.off += words
        assert self.off <= self.n, f"arena overflow at {name}: {self.off}"
        if dtype == BF16:
            a = a.bitcast(BF16)[:, 0:nfree]
        elif dtype != F32:
            a = a.bitcast(dtype)
        if len(shape) == 3:
            a = a.rearrange("p (k w) -> p k w", k=shape[1])
        elif len(shape) == 4:
            a = a.rearrange("p (k j w) -> p k j w", k=shape[1], j=shape[2])
        return T(a, name)


class Seg:
    def __init__(self, A, W, name):
        self.W = W
        self.xf = A.alloc([128, DC, W], F32, name + '_xf')
        self.xb = A.alloc([128, DC, W], BF16, name + '_xb')
        off = A.off
        self.h = A.alloc([128, FC, W], BF16, name + '_h')
        if W == TG:
            self.hf = T(A.sb[:, off:off + DC * W].rearrange("p (k w) -> p k w", k=DC), name + '_hf')
            self.hf.buf = self.h.buf


class K:
    def __init__(self, nc, cfg):
        self.nc = nc
        self.cfg = cfg
        self.P = Prog(nc)
        self.dram = {}

    def din(self, name, shape, dtype=F32):
        t = self.nc.dram_tensor(name, list(shape), dtype, kind="ExternalInput").ap()
        self.dram[name] = t
        return t

    def dout(self, name, shape, dtype=F32):
        t = self.nc.dram_tensor(name, list(shape), dtype, kind="ExternalOutput").ap()
        self.dram[name] = t
        return t

    def dscr(self, name, shape, dtype=F32):
        t = self.nc.dram_tensor(name, list(shape), dtype, kind="Internal").ap()
        self.dram[name] = t
        return t

    def dma(self, q, out, in_, reads=(), writes=(), is_out=False):
        return self.P.op(q, lambda e: e.dma_start(out=out, in_=in_), reads=reads, writes=writes, dma=True, out=is_out)

    def mm(self, out_t, out_ap, l_t, l_ap, r_t, r_ap, start, stop):
        return self.P.op('pe', lambda e: e.matmul(out_ap, lhsT=l_ap, rhs=r_ap, start=start, stop=stop),
                         reads=[l_t, r_t], writes=[out_t])

    def tr(self, out_t, out_ap, in_t, in_ap, ident_ap):
        return self.P.op('pe', lambda e: e.transpose(out_ap, in_ap, ident_ap), reads=[in_t], writes=[out_t])

    def act(self, out_t, out_ap, in_t, in_ap, func, bias=None, scale=None, reads=(), accum=None, extra_w=()):
        kw = {}
        if bias is not None:
            kw['bias'] = bias
        if scale is not None:
            kw['scale'] = scale
        if accum is not None:
            kw['accum_out'] = accum
        return self.P.op('act', lambda e: e.activation(out_ap, in_ap, func, **kw),
                         reads=[in_t] + list(reads), writes=[out_t] + list(extra_w))

    def dve(self, fn, reads, writes):
        return self.P.op('dve', fn, reads=reads, writes=writes)

    def psum(self, kind='p'):
        if kind == 'p':
            i = self.ps_rr % self.n_ps_p
            self.ps_rr += 1
            return self.ps[i]
        i = self.n_ps_p + (self.pss_rr % (8 - self.n_ps_p))
        self.pss_rr += 1
        return self.ps[i]

    def tmp(self, kind='p'):
        if kind == 'p':
            i = self.tmp_rr % len(self.tmps)
            self.tmp_rr += 1
            return self.tmps[i]
        i = self.tmps_rr % len(self.tmps_s)
        self.tmps_rr += 1
        return self.tmps_s[i]

    def wslot(self):
        i = self.ws_rr % N_WSLOT
        self.ws_rr += 1
        return self.wslots[i]

    def linear(self, segs, srcs, wt, KC, o_list, epi):
        for o in o_list:
            pss = [None] * len(segs)
            for kg in range(0, KC, 16):
                kc = min(16, KC - kg)
                sl = self.wslot()
                sl3 = sl.ap[:, 0:kc * 128].rearrange("p (k c) -> p k c", k=kc)
                self.dma('pool', sl3, wt[o, :, kg:kg + kc, :], writes=[sl])
                for si, sg in enumerate(segs):
                    if kg == 0:
                        pss[si] = self.psum('p' if sg.W > 1 else 's')
                    src_t, src_ap = srcs[si]
                    for k in range(kc):
                        self.mm(pss[si], pss[si].ap[:, 0:sg.W], sl, sl3[:, k, :], src_t, src_ap[:, kg + k, 0:sg.W],
                                start=(kg + k == 0), stop=(kg + k == KC - 1))
            for si, sg in enumerate(segs):
                epi(o, si, pss[si], pss[si].ap[:, 0:sg.W])

    def layernorm(self, sg, gcol):
        self.ln_generic(sg.W, sg.xf, sg.xf.ap, lambda k: (self.lng.ap[:, gcol, k:k + 1], self.lnb.ap[:, gcol, k:k + 1]),
                        (sg.xf, sg.xf.ap), (sg.xb, sg.xb.ap), AF.Identity)

    def ln_generic(self, W, x_t, x_ap, gb, out_f, out_b, func):
        kd = 'p' if W > 1 else 's'
        ps1 = self.psum(kd)
        ps2 = self.psum(kd)
        ones = self.c_ones
        for k in range(DC):
            sq = self.tmp(kd)
            xk = x_ap[:, k, 0:W]
            sqa = sq.ap[:, 0:W]
            self.dve(lambda e, sqa=sqa, xk=xk: e.tensor_tensor(sqa, xk, xk, ALU.mult), [x_t], [sq])
            self.mm(ps1, ps1.ap[:, 0:W], ones, ones.ap, x_t, xk, k == 0, k == DC - 1)
            self.mm(ps2, ps2.ap[:, 0:W], ones, ones.ap, sq, sqa, k == 0, k == DC - 1)
        mean, msq, rstd = self.ln_tiles[kd]
        ma, qa, ra = mean.ap[:, 0:W], msq.ap[:, 0:W], rstd.ap[:, 0:W]
        p1, p2 = ps1.ap[:, 0:W], ps2.ap[:, 0:W]
        self.dve(lambda e: e.tensor_scalar(ma, p1, 1.0 / D, None, ALU.mult), [ps1], [mean])
        self.dve(lambda e: e.tensor_tensor(qa, ma, ma, ALU.mult), [mean], [msq])
        self.dve(lambda e: e.scalar_tensor_tensor(qa, p2, 1.0 / D, qa, ALU.mult, ALU.subtract), [ps2, msq], [msq])
        self.act(rstd, ra, msq, qa, AF.Sqrt, bias=self.c_eps.ap[:, 0:1])
        self.dve(lambda e: e.reciprocal(ra, ra), [rstd], [rstd])
        t1s = {}

        def sub(k):
            t1 = self.tmp(kd)
            t1s[k] = t1
            self.dve(lambda e, ta=t1.ap[:, 0:W], xk=x_ap[:, k, 0:W]: e.tensor_tensor(ta, xk, ma, ALU.subtract), [x_t, mean], [t1])

        def mul(k):
            t1 = t1s[k]
            ta = t1.ap[:, 0:W]
            self.dve(lambda e, ta=ta: e.tensor_tensor(ta, ta, ra, ALU.mult), [t1, rstd], [t1])
            g, b = gb(k)
            if out_f is not None:
                self.act(out_f[0], out_f[1][:, k, 0:W], t1, ta, func, bias=b, scale=g, reads=[self.cvec])
            if out_b is not None:
                self.act(out_b[0], out_b[1][:, k, 0:W], t1, ta, func, bias=b, scale=g, reads=[self.cvec])
        sub(0)
        sub(1)
        for k in range(DC):
            if k + 2 < DC:
                sub(k + 2)
            mul(k)

    def resid_epi(self, segs):
        def epi(o, si, ps_t, ps_ap):
            sg = segs[si]
            xk = sg.xf.ap[:, o, 0:sg.W]
            self.dve(lambda e: e.scalar_tensor_tensor(xk, xk, ALPHA, ps_ap, ALU.mult, ALU.add), [sg.xf, ps_t], [sg.xf])
        return epi

    def ffn(self, segs, l, j):
        w1, w3, w2 = self.dram['w1'], self.dram['w3'], self.dram['w2']
        srcs = [(sg.xb, sg.xb.ap) for sg in segs]
        for o in range(FC):
            s1 = self.wslot()
            s3 = self.wslot()
            a1 = s1.ap.rearrange("p (k c) -> p k c", k=16)
            a3 = s3.ap.rearrange("p (k c) -> p k c", k=16)
            self.dma('pool', a1, w1[l, j, o], writes=[s1])
            self.dma('pool', a3, w3[l, j, o], writes=[s3])
            for si, sg in enumerate(segs):
                kd = 'p' if sg.W > 1 else 's'
                pa = self.psum(kd)
                pb = self.psum(kd)
                W = sg.W
                for k in range(DC):
                    self.mm(pa, pa.ap[:, 0:W], s1, a1[:, k, :], sg.xb, sg.xb.ap[:, k, 0:W], k == 0, k == DC - 1)
                for k in range(DC):
                    self.mm(pb, pb.ap[:, 0:W], s3, a3[:, k, :], sg.xb, sg.xb.ap[:, k, 0:W], k == 0, k == DC - 1)
                sa = self.tmp(kd)
                saa = sa.ap[:, 0:W]
                self.act(sa, saa, pa, pa.ap[:, 0:W], AF.Silu)
                ho = sg.h.ap[:, o, 0:W]
                pba = pb.ap[:, 0:W]
                self.dve(lambda e, ho=ho, pba=pba, saa=saa: e.scalar_tensor_tensor(ho, pba, 0.5, saa, ALU.mult, ALU.mult),
                         [pb, sa], [sg.h])
        hs = [(sg.h, sg.h.ap) for sg in segs]
        self.linear(segs, hs, w2[l, j], FC, range(DC), self.resid_epi(segs))
        for sg in segs:
            self.layernorm(sg, l * 4 + (0 if j == 0 else 2))

    def ple(self, segs, psrcs, l):
        wg, wp = self.dram['wgate'], self.dram['wproj']
        for o in range(DC):
            sl = self.wslot()
            sl3 = sl.ap.rearrange("p (k c) -> p k c", k=16)
            self.dma('pool', sl3, wg[l, o], writes=[sl])
            sp_ = self.wslot()
            sp3 = sp_.ap[:, 0:256].rearrange("p (k c) -> p k c", k=2)
            self.dma('pool', sp3, wp[l, o], writes=[sp_])
            for si, sg in enumerate(segs):
                W = sg.W
                kd = 'p' if W > 1 else 's'
                pg = self.psum(kd)
                pp = self.psum(kd)
                for k in range(DC):
                    self.mm(pg, pg.ap[:, 0:W], sl, sl3[:, k, :], sg.xb, sg.xb.ap[:, k, 0:W], k == 0, k == DC - 1)
                pt_t, pt_ap = psrcs[si]
                for k in range(2):
                    self.mm(pp, pp.ap[:, 0:W], sp_, sp3[:, k, :], pt_t, pt_ap[:, k, 0:W], k == 0, k == 1)
                sgm = self.tmp(kd)
                sa = sgm.ap[:, 0:W]
                self.act(sgm, sa, pg, pg.ap[:, 0:W], AF.Sigmoid)
                ppa = pp.ap[:, 0:W]
                self.dve(lambda e, sa=sa, ppa=ppa: e.tensor_tensor(sa, sa, ppa, ALU.mult), [sgm, pp], [sgm])
                xk = sg.xf.ap[:, o, 0:W]
                self.dve(lambda e, sa=sa, xk=xk: e.scalar_tensor_tensor(xk, xk, ALPHA, sa, ALU.mult, ALU.add),
                         [sg.xf, sgm], [sg.xf])
        for sg in segs:
            self.layernorm(sg, l * 4 + 3)

    def setup(self, es):
        nc = self.nc
        self.sb = es.enter_context(nc.sbuf_tensor("arena", [128, ARENA_WORDS], F32))
        self.A = Arena(self.sb, ARENA_WORDS)
        self.ps = []
        for i in range(8):
            p = es.enter_context(nc.psum_tensor(f"ps{i}", [128, 512], F32))
            self.ps.append(T(p[:, :], f"ps{i}"))
        self.n_ps_p = 6
        self.ps_rr = 0
        self.pss_rr = 0
        self.tmp_rr = 0
        self.tmps_rr = 0
        self.ws_rr = 0
        A = self.A
        cst = self.din('consts', [128, 1024])
        self.c_all = A.alloc([128, 1024], F32, 'consts')
        self.dma('sp', self.c_all.ap, cst, writes=[self.c_all])
        ca = self.c_all
        self.c_ones = T(ca.ap[:, 0:128], 'ones')
        self.c_ones.buf = ca.buf
        self.c_ident = T(ca.ap[:, 128:256], 'ident')
        self.c_ident.buf = ca.buf
        self.c_eps = T(ca.ap[:, 256:257], 'eps')
        self.c_eps.buf = ca.buf
        self.c_mask_lt = T(ca.ap[:, 384:512], 'mask_lt')
        self.c_mask_lt.buf = ca.buf
        self.c_bias_le = T(ca.ap[:, 512:640], 'bias_le')
        self.c_bias_le.buf = ca.buf
        self.c_tri_le = T(ca.ap[:, 640:768], 'tri_le')
        self.c_tri_le.buf = ca.buf
        self.c_tri_gt = T(ca.ap[:, 768:896], 'tri_gt')
        self.c_tri_gt.buf = ca.buf
        self.c_bias_lt = T(ca.ap[:, 896:1024], 'bias_lt')
        self.c_bias_lt.buf = ca.buf
        self.c_identb = A.alloc([128, 128], BF16, 'identb')
        self.dve(lambda e: e.tensor_copy(self.c_identb.ap, self.c_ident.ap), [ca], [self.c_identb])
        self.c_blt_b = A.alloc([128, 128], BF16, 'blt_b')
        self.c_ble_b = A.alloc([128, 128], BF16, 'ble_b')
        self.dve(lambda e: e.tensor_scalar(self.c_blt_b.ap, self.c_bias_lt.ap, 1.0 / SCALE, None, ALU.mult), [ca], [self.c_blt_b])
        self.dve(lambda e: e.tensor_scalar(self.c_ble_b.ap, self.c_bias_le.ap, 1.0 / SCALE, None, ALU.mult), [ca], [self.c_ble_b])
        lg = self.din('lng', [128, 16 * 16])
        lb = self.din('lnb', [128, 16 * 16])
        self.lng = A.alloc([128, 16, 16], F32, 'lng')
        self.lnb = A.alloc([128, 16, 16], F32, 'lnb')
        self.dma('sp', self.lng.ap, lg.rearrange("p (a b) -> p a b", a=16), writes=[self.lng])
        self.dma('sp', self.lnb.ap, lb.rearrange("p (a b) -> p a b", a=16), writes=[self.lnb])
        cv = self.din('cvec', [128, 2048])
        self.cvec = A.alloc([128, 2048], F32, 'cvec')
        self.dma('sp', self.cvec.ap, cv, writes=[self.cvec])
        self.wslots = [A.alloc([128, 2048], BF16, f'ws{i}') for i in range(N_WSLOT)]
        self.tmps = [A.alloc([128, 512], F32, f'tmp{i}') for i in range(6)]
        self.tmps_s = [A.alloc([128, 1], F32, f'tmps{i}') for i in range(6)]
        self.ln_tiles = {'p': [A.alloc([128, 512], F32, f'ln{i}') for i in range(3)],
                         's': [A.alloc([128, 1], F32, f'lns{i}') for i in range(3)]}
        self.sseg = Seg(A, 1, 'ss')
        self.base_off = A.off

    def declare(self):
        d = self.din
        d('xT', [D, T_]); d('xsT', [128, DC])
        d('pT', [NL, 256, T_]); d('psT', [NL, 128, 2])
        d('w1', [NL, 2, FC, 128, DC, 128]); d('w3', [NL, 2, FC, 128, DC, 128]); d('w2', [NL, 2, DC, 128, FC, 128])
        d('wgate', [NL, DC, 128, DC, 128]); d('wproj', [NL, DC, 128, 2, 128])
        d('sbqkv', [24, 128, DC, 128]); d('sbo', [DC, 128, DC, 128])
        d('mbqkv', [24, 128, DC, 128]); d('mbo', [DC, 128, DC, 128])
        d('ssmin', [81, 128, DC, 128]); d('ssmout', [DC, 128, 32, 128]); d('gexp', [64, 4096])
        d('sst', [4096, 128]); d('scstT', [128, 48, 3])
        for nm in ('sbkc', 'sbvc', 'mbkc', 'mbvc'):
            d(nm, [1280 * 128, 512])
        d('ptab', [1, 128], I32); d('consts2', [128, 2048])
        d('cpw1', [32, 128, DC, 128]); d('cpw2', [DC, 128, DC, 128]); d('cstT', [128, DC, 30])
        o = self.dout
        o('ccT', [D, 30]); o('ccsT', [128, DC, 30])
        o('shp', [4096, 128]); o('shs', [4096, 128]); o('scT', [6144, 3]); o('scsT', [128, 48, 3])
        o('yT', [D, T_]); o('ysT', [128, DC])
        o('sbkT', [512, T_]); o('sbvT', [512, T_]); o('sbksT', [128, 4]); o('sbvsT', [128, 4])
        o('mbkT', [512, T_]); o('mbvT', [512, T_]); o('mbksT', [128, 4]); o('mbvsT', [128, 4])
        s = self.dscr
        s('XS', [NG, 128, DC, TG])
        s('QT', [16, 128, T_], BF16)
        s('AO', [32, 128, T_], BF16)
        s('UT', [D, T_]); s('CT', [D, T_])
        s('ZT', [4096, T_]); s('XBC', [6144, T_]); s('DTr', [64, T_]); s('XC', [6144, T_]); s('YT', [4096, T_])

    def fm(self, ap2, g):
        return ap2.rearrange("(k p) t -> p k t", p=128)[:, :, g * TG:(g + 1) * TG]

    def fms(self, ap2):
        return ap2

    def stage(self, dtype):
        if dtype == BF16:
            i = self.stb_rr % len(self.stb)
            self.stb_rr += 1
            return self.stb[i]
        i = self.stf_rr % len(self.stf)
        self.stf_rr += 1
        return self.stf[i]

    def qkv_proj(self, segs, g, wname, kname, vname):
        dr = self.dram

        def epi(o, si, ps_t, ps_ap):
            if si == 0:
                if o < 16:
                    st = self.stage(BF16)
                    self.act(st, st.ap, ps_t, ps_ap, AF.Copy)
                    self.dma('sp', dr['QT'][o, :, g * TG:(g + 1) * TG], st.ap, reads=[st], writes=[self.b_qt])
                else:
                    st = self.stage(F32)
                    self.act(st, st.ap, ps_t, ps_ap, AF.Copy)
                    nm = kname if o < 20 else vname
                    c = (o - 16) % 4
                    self.dma('sp', dr[nm + 'T'][c * 128:(c + 1) * 128, g * TG:(g + 1) * TG], st.ap,
                             reads=[st], writes=[self.b_kv], is_out=True)
            else:
                if o < 16:
                    self.act(self.qs, self.qs.ap[:, o, :], ps_t, ps_ap, AF.Copy)
                else:
                    self.act(self.kvs, self.kvs.ap[:, o - 16, :], ps_t, ps_ap, AF.Copy)
        srcs = [(sg.xb, sg.xb.ap) for sg in segs]
        self.linear(segs, srcs, dr[wname], DC, range(24), epi)
        if len(segs) > 1:
            self.dma('sp', dr[kname + 'sT'], self.kvs.ap[:, 0:4, 0], reads=[self.kvs], is_out=True)
            self.dma('sp', dr[vname + 'sT'], self.kvs.ap[:, 4:8, 0], reads=[self.kvs], is_out=True)

    def build(self):
        cfg = self.cfg
        with ExitStack() as es:
            self.setup(es)
            self.declare()
            A = self.A
            dr = self.dram
            self.b_qt = Buf('QT')
            self.b_kv = Buf('KV')
            self.b_xs = [Buf(f'XS{g}') for g in range(NG)]
            self.b_ao = Buf('AO')
            self.qs = A.alloc([128, 16, 1], F32, 'qs')
            self.kvs = A.alloc([128, 8, 1], F32, 'kvs')
            self.aos = A.alloc([128, 32, 1], BF16, 'aos')
            self.pst = A.alloc([128, 2, 1], BF16, 'pst')
            self.stb = [A.alloc([128, 512], BF16, f'stb{i}') for i in range(2)]
            self.stf = [A.alloc([128, 512], F32, f'stf{i}') for i in range(2)]
            self.stb_rr = 0
            self.stf_rr = 0
            self.base_off = A.off
            n_layers = cfg.get('n_layers', NL)
            l0 = cfg.get('l0', 0)
            n_layers = l0 + n_layers
            self.b_ut = Buf('UT')
            self.b_ct = Buf('CT')
            self.b_zt = Buf('ZT'); self.b_xbc = Buf('XBC'); self.b_dt = Buf('DT'); self.b_xc = Buf('XC'); self.b_yt = Buf('YT')
            self.zs = A.alloc([128, 32, 1], F32, 'zs')
            self.xbcs = A.alloc([128, 48, 1], F32, 'xbcs')
            self.dts = A.alloc([128, 1], F32, 'dts')
            self.ys = A.alloc([128, 32, 1], F32, 'ys')
            self.us = A.alloc([128, DC, 1], F32, 'us')
            self.cs = A.alloc([128, DC, 1], F32, 'cs')
            self.base_off = A.off
            for r in range(l0, n_layers + 1):
                A.off = self.base_off
                seg = Seg(A, TG, f'pseg')
                pt = A.alloc([128, 2, TG], BF16, 'pt')
                ao = A.alloc([128, 32, TG], BF16, 'ao')
                for g in range(NG):
                    segs = [seg] + ([self.sseg] if g == 0 else [])
                    if r == l0:
                        self.dma('sp', seg.xf.ap, self.fm(dr['xT'], g), writes=[seg.xf])
                        self.dma('pool', seg.xb.ap, self.fm(dr['xT'], g), writes=[seg.xb])
                        if g == 0:
                            self.dma('sp', self.sseg.xf.ap[:, :, 0], dr['xsT'], writes=[self.sseg.xf])
                            self.dma('pool', self.sseg.xb.ap[:, :, 0], dr['xsT'], writes=[self.sseg.xb])
                    else:
                        l = r - 1
                        self.dma('sp', seg.xf.ap, dr['XS'][g], reads=[self.b_xs[g]], writes=[seg.xf])
                        self.mixer_out(l, segs, g, ao)
                        for sg in segs:
                            self.layernorm(sg, l * 4 + 1)
                        if cfg.get('stop') == f'pn{l}':
                            self.write_y(segs, g)
                            continue
                        self.ffn(segs, l, 1)
                        self.dma('pool', pt.ap, self.fm(dr['pT'][l], g), writes=[pt])
                        psrcs = [(pt, pt.ap)]
                        if g == 0:
                            self.dma('pool', self.pst.ap[:, :, 0], dr['psT'][l], writes=[self.pst])
                            psrcs.append((self.pst, self.pst.ap))
                        self.ple(segs, psrcs, l)
                    if r < n_layers:
                        self.ffn(segs, r, 0)
                        if cfg.get('stop') == f'ffn1_{r}':
                            self.write_y(segs, g)
                            continue
                        self.mixer_in(r, segs, g)
                        self.dma('sp', dr['XS'][g], seg.xf.ap, reads=[seg.xf], writes=[self.b_xs[g]])
                    else:
                        self.write_y(segs, g)
                if cfg.get('stop') in (f'ffn1_{r}', f'pn{r - 1}'):
                    break
                self.P.barrier()
                if r < n_layers:
                    A.off = self.base_off
                    self.mixer_core(r)
                    self.P.barrier()
            self.P.finish()
            sems = {}
            for nm in self.P.sem_names():
                sems[nm] = es.enter_context(self.nc.semaphore("s_" + "_".join(str(x) for x in nm)))
            block = es.enter_context(self.nc.Block())
            self.P.emit(block, sems)

    def write_y(self, segs, g):
        dr = self.dram
        self.dma('sp', self.fm(dr['yT'], g), segs[0].xf.ap, reads=[segs[0].xf], is_out=True)
        if len(segs) > 1:
            self.dma('sp', dr['ysT'], segs[1].xf.ap[:, :, 0], reads=[segs[1].xf], is_out=True)

    def mixer_in(self, l, segs, g):
        if l == 0:
            self.qkv_proj(segs, g, 'sbqkv', 'sbk', 'sbv')
        elif l == 3:
            self.qkv_proj(segs, g, 'mbqkv', 'mbk', 'mbv')
        elif l == 2:
            self.conf_in(segs, g)
        elif l == 1:
            self.ssm_in(segs, g)

    def mixer_out(self, l, segs, g, ao):
        dr = self.dram
        if l in (0, 3):
            ao16 = ao.ap[:, 0:16, :]
            self.dma('sp', ao16, dr['AO'][0:16].rearrange("k p t -> p k t")[:, :, g * TG:(g + 1) * TG],
                     reads=[self.b_ao], writes=[ao])
            srcs = [(ao, ao16)]
            if len(segs) > 1:
                srcs.append((self.aos, self.aos.ap[:, 0:16, :]))
            self.linear(segs, srcs, dr['sbo' if l == 0 else 'mbo'], DC, range(DC), self.resid_epi(segs))
        elif l == 1:
            self.ssm_out(segs, g, ao)
        elif l == 2:
            seg = segs[0]
            cv = self.cvec.ap
            self.dma('sp', seg.hf.ap, self.fm(dr['CT'], g), reads=[self.b_ct], writes=[seg.hf])
            gb = lambda k: (cv[:, 48 + k:49 + k], cv[:, 64 + k:65 + k])
            ao16 = ao.ap[:, 0:16, :]
            self.ln_generic(TG, seg.hf, seg.hf.ap, gb, None, (ao, ao16), AF.Silu)
            srcs = [(ao, ao16)]
            if len(segs) > 1:
                self.ln_generic(1, self.cs, self.cs.ap, gb, None, (self.aos, self.aos.ap[:, 0:16, :]), AF.Silu)
                srcs.append((self.aos, self.aos.ap[:, 0:16, :]))

            def epi(o, si, ps_t, ps_ap):
                sg = segs[si]
                tp = self.tmp('p' if sg.W > 1 else 's')
                ta = tp.ap[:, 0:sg.W]
                self.act(tp, ta, ps_t, ps_ap, AF.Identity, bias=cv[:, 80 + o:81 + o], reads=[self.cvec])
                xk = sg.xf.ap[:, o, 0:sg.W]
                self.dve(lambda e: e.scalar_tensor_tensor(xk, xk, ALPHA, ta, ALU.mult, ALU.add), [sg.xf, tp], [sg.xf])
            self.linear(segs, srcs, dr['cpw2'], DC, range(DC), epi)

    def ssm_in(self, segs, g):
        dr = self.dram
        gs = slice(g * TG, (g + 1) * TG)

        def epi(o, si, ps_t, ps_ap):
            if si == 0:
                st = self.stage(F32)
                self.act(st, st.ap, ps_t, ps_ap, AF.Copy)
                if o < 32:
                    self.dma('sp', dr['ZT'][o * 128:(o + 1) * 128, gs], st.ap, reads=[st], writes=[self.b_zt])
                elif o < 80:
                    self.dma('sp', dr['XBC'][(o - 32) * 128:(o - 31) * 128, gs], st.ap, reads=[st], writes=[self.b_xbc])
                else:
                    self.dma('sp', dr['DTr'][:, gs], st.ap[0:64, :], reads=[st], writes=[self.b_dt])
            else:
                if o < 32:
                    self.act(self.zs, self.zs.ap[:, o, :], ps_t, ps_ap, AF.Copy)
                elif o < 80:
                    self.act(self.xbcs, self.xbcs.ap[:, o - 32, :], ps_t, ps_ap, AF.Copy)
                else:
                    self.act(self.dts, self.dts.ap, ps_t, ps_ap, AF.Copy)
        srcs = [(sg.xb, sg.xb.ap) for sg in segs]
        self.linear(segs, srcs, dr['ssmin'], DC, range(81), epi)

    def ssm_out(self, segs, g, ao):
        dr = self.dram
        seg = segs[0]
        cv = self.cvec.ap
        gs = slice(g * TG, (g + 1) * TG)
        quarters = []
        for i in range(4):
            t = T(seg.hf.ap[:, i * 4:(i + 1) * 4, :], f'hfq{i}')
            quarters.append(t)
        def load(gq):
            y4 = quarters[(gq % 2) * 2]
            z4 = quarters[(gq % 2) * 2 + 1]
            self.dma('sp', y4.ap, dr['YT'][gq * 512:(gq + 1) * 512, gs].rearrange("(k p) t -> p k t", p=128), reads=[self.b_yt], writes=[y4, seg.h])
            self.dma('sp', z4.ap, dr['ZT'][gq * 512:(gq + 1) * 512, gs].rearrange("(k p) t -> p k t", p=128), reads=[self.b_zt], writes=[z4, seg.h])
        load(0)
        for gq in range(8):
            if gq + 1 < 8:
                load(gq + 1)
            y4 = quarters[(gq % 2) * 2]
            z4 = quarters[(gq % 2) * 2 + 1]
            items = [(TG, 'p', y4, y4.ap, z4, z4.ap, ao, ao.ap[:, gq * 4:(gq + 1) * 4, :], gq)]
            if len(segs) > 1:
                items.append((1, 's', self.ys, self.ys.ap[:, gq * 4:(gq + 1) * 4, :], self.zs, self.zs.ap[:, gq * 4:(gq + 1) * 4, :],
                              self.aos, self.aos.ap[:, gq * 4:(gq + 1) * 4, :], gq))
            self._ssm_norm_items(items, cv)
        srcs = [(ao, ao.ap)]
        if len(segs) > 1:
            srcs.append((self.aos, self.aos.ap))
        self.linear(segs, srcs, dr['ssmout'], 32, range(DC), self.resid_epi(segs))

    def _ssm_norm_items(self, items, cv):
        for (W, kd, y_t, y_ap, z_t, z_ap, o_t, o_ap, gq) in items:
            ps = self.psum(kd)
            szs = [self.tmp(kd) for _ in range(4)]
            for k in range(4):
                self.act(szs[k], szs[k].ap[:, 0:W], z_t, z_ap[:, k, :], AF.Silu)
            for k in range(4):
                yk = y_ap[:, k, :]
                self.dve(lambda e, yk=yk, sza=szs[k].ap[:, 0:W]: e.tensor_tensor(yk, yk, sza, ALU.mult), [y_t, szs[k]], [y_t])
            for k in range(4):
                yk = y_ap[:, k, :]
                sqa = szs[k].ap[:, 0:W]
                self.dve(lambda e, yk=yk, sqa=sqa: e.tensor_tensor(sqa, yk, yk, ALU.mult), [y_t], [szs[k]])
                self.mm(ps, ps.ap[:, 0:W], self.c_ones, self.c_ones.ap, szs[k], sqa, k == 0, k == 3)
            rs = self.ln_tiles[kd][2]
            ra = rs.ap[:, 0:W]
            self.dve(lambda e, ra=ra, p=ps.ap[:, 0:W]: e.tensor_scalar(ra, p, 1.0 / 512, None, ALU.mult), [ps], [rs])
            self.act(rs, ra, rs, ra, AF.Sqrt, bias=self.c_eps.ap[:, 0:1])
            self.dve(lambda e, ra=ra: e.reciprocal(ra, ra), [rs], [rs])
            for k in range(4):
                yk = y_ap[:, k, :]
                self.dve(lambda e, ta=szs[k].ap[:, 0:W], yk=yk, ra=ra: e.tensor_tensor(ta, yk, ra, ALU.mult), [y_t, rs], [szs[k]])
            for k in range(4):
                gcol = cv[:, 1328 + gq * 4 + k:1329 + gq * 4 + k]
                self.dve(lambda e, o=o_ap[:, k, :], ta=szs[k].ap[:, 0:W], gcol=gcol: e.tensor_scalar(o, ta, gcol, None, ALU.mult), [szs[k], self.cvec], [o_t])

    def ssm_core(self):
        A = self.A
        dr = self.dram
        cv = self.cvec.ap
        sv = A.alloc([128, 8], F32, 'sv')
        mark = A.off

        def bcl(ap, n):
            return bass.AP(ap.tensor, ap.offset, [list(x) for x in ap.ap] + [[0, n]])

        def bcm(ap, m):
            l = [list(x) for x in ap.ap]
            return bass.AP(ap.tensor, ap.offset, [l[0], [0, m]] + l[1:])
        xpad = [A.alloc([128, 3 + T_], F32, f'xpad{i}') for i in range(2)]
        acc = [A.alloc([128, T_], F32, f'sacc{i}') for i in range(2)]
        for k in range(48):
            up, ac = xpad[k % 2], acc[k % 2]
            self.dve(lambda e, a=up.ap[:, 0:3]: e.memset(a, 0.0), [], [up])
            self.dma('sp', up.ap[:, 3:3 + T_], dr['XBC'][k * 128:(k + 1) * 128, :], reads=[self.b_xbc], writes=[up])
            wcol = lambda w: cv[:, 1088 + w * 48 + k:1089 + w * 48 + k]
            self.dve(lambda e, o=ac.ap, i=up.ap[:, 0:T_], w0=wcol(0), b=cv[:, 1280 + k:1281 + k]: e.tensor_scalar(o, i, w0, b, ALU.mult, ALU.add),
                     [up, self.cvec], [ac])
            for w in range(1, 4):
                self.dve(lambda e, o=ac.ap, i=up.ap[:, w:w + T_], ww=wcol(w): e.scalar_tensor_tensor(o, i, ww, o, ALU.mult, ALU.add),
                         [up, ac, self.cvec], [ac])
            self.act(ac, ac.ap, ac, ac.ap, AF.Silu)
            self.dma('sp', dr['XC'][k * 128:(k + 1) * 128, :], ac.ap, reads=[ac], writes=[self.b_xc])
        self.dma('sp', dr['scT'], dr['XBC'][:, T_ - 3:T_], reads=[self.b_xbc], is_out=True)
        self.P.barrier()
        A.off = mark
        ident = self.c_ident
        ones = self.c_ones
        tri_le = self.c_tri_le
        tri_gt = self.c_tri_gt
        self.act(sv, sv.ap[0:64, 0:1], self.cvec, cv[0:64, 1361:1362], AF.Exp)
        self.dve(lambda e: e.tensor_scalar(sv.ap[0:64, 0:1], sv.ap[0:64, 0:1], -1.0, None, ALU.mult), [sv], [sv])
        a_col = sv.ap[0:64, 0:1]
        hT = A.alloc([128, 4096], F32, 'hT')
        hTb = A.alloc([128, 4096], BF16, 'hTb')
        self.dve(lambda e: e.memset(hT.ap, 0.0), [], [hT])
        self.dve(lambda e: e.memset(hTb.ap, 0.0), [], [hTb])
        xcs = A.alloc([128, 48, 128], F32, 'xcs')
        xcb = A.alloc([128, 16, 128], BF16, 'xcb')
        Xtm = A.alloc([128, 4096], BF16, 'Xtm')
        Xw = A.alloc([128, 4096], BF16, 'Xw')
        yTo = A.alloc([128, 32, 128], F32, 'yTo')
        Btm = A.alloc([128, 8, 128], BF16, 'Btm')
        CBm = A.alloc([128, 8, 128], F32, 'CBm')
        dtf = A.alloc([128, 128], F32, 'dtf')
        daf = A.alloc([128, 128], F32, 'daf')
        dtda = A.alloc([128, 128], F32, 'dtda')
        ecum = A.alloc([128, 64], F32, 'ecum')
        edec = A.alloc([128, 64], F32, 'edec')
        totb = A.alloc([128, 64], F32, 'totb')
        wtm = A.alloc([128, 64], F32, 'wtm')
        Ut = [A.alloc([128, 128], F32, f'Ut{i}') for i in range(16)]
        Et = [A.alloc([128, 512], F32, f'Et{i}') for i in range(4)]
        Mt = [A.alloc([128, 128], BF16, f'Mt{i}') for i in range(8)]
        ytmp = [A.alloc([128, 512], F32, f'ytmp{i}') for i in range(2)]
        ytm = [A.alloc([128, 512], F32, f'ytm{i}') for i in range(2)]
        dt_tm = dtda.ap[:, 0:64]
        da_tm = dtda.ap[:, 64:128]
        id64 = ident.ap[0:64, 0:64]
        rr = 0
        for c in range(16):
            cs = slice(c * 128, (c + 1) * 128)
            self.dma('sp', xcs.ap, dr['XC'].rearrange("(k p) t -> p k t", p=128)[:, :, cs], reads=[self.b_xc], writes=[xcs])
            self.dma('sp', dtf.ap[0:64, :], dr['DTr'][:, cs], reads=[self.b_dt], writes=[dtf])
            self.act(dtf, dtf.ap[0:64, :], dtf, dtf.ap[0:64, :], AF.Exp, bias=cv[0:64, 1360:1361], reads=[self.cvec])
            self.act(dtf, dtf.ap[0:64, :], dtf, dtf.ap[0:64, :], AF.Ln, bias=ones.ap[0:64, 0:1])
            self.dve(lambda e: e.tensor_scalar(daf.ap[0:64, :], dtf.ap[0:64, :], a_col, None, ALU.mult), [dtf, sv], [daf])
            pA = self.psum()
            self.tr(pA, pA.ap[:, 0:64], dtf, dtf.ap[0:64, :], id64)
            self.tr(pA, pA.ap[:, 64:128], daf, daf.ap[0:64, :], id64)
            self.act(dtda, dtda.ap, pA, pA.ap[:, 0:128], AF.Copy)
            pB = self.psum()
            self.mm(pB, pB.ap[:, 0:64], tri_le, tri_le.ap, dtda, da_tm, True, True)
            self.mm(pB, pB.ap[:, 64:128], ones, ones.ap, dtda, da_tm, True, True)
            self.act(ecum, ecum.ap, pB, pB.ap[:, 0:64], AF.Exp)
            self.act(edec, edec.ap, pB, pB.ap[:, 64:128], AF.Exp)
            self.act(totb, totb.ap, pB, pB.ap[:, 64:128], AF.Copy)
            self.dve(lambda e, p=pB.ap[:, 0:64]: e.tensor_tensor(wtm.ap, totb.ap, p, ALU.subtract), [totb, pB], [wtm])
            self.act(wtm, wtm.ap, wtm, wtm.ap, AF.Exp)
            self.dve(lambda e: e.tensor_tensor(wtm.ap, wtm.ap, dt_tm, ALU.mult), [wtm, dtda], [wtm])
            self.act(xcb, xcb.ap, xcs, xcs.ap[:, 32:48, :], AF.Copy)
            for kb in range(8):
                pt_ = self.psum()
                for q in range(4):
                    self.tr(pt_, pt_.ap[:, q * 128:(q + 1) * 128], xcs, xcs.ap[:, kb * 4 + q, :], ident.ap)
                xo = Xtm.ap[:, kb * 512:(kb + 1) * 512]
                if kb % 2 == 0:
                    self.act(Xtm, xo, pt_, pt_.ap, AF.Copy)
                else:
                    self.dve(lambda e, xo=xo, p=pt_.ap: e.tensor_copy(xo, p), [pt_], [Xtm])
            for kb in range(2):
                pt_ = self.psum()
                for q in range(4):
                    self.tr(pt_, pt_.ap[:, q * 128:(q + 1) * 128], xcs, xcs.ap[:, 32 + kb * 4 + q, :], ident.ap)
                self.act(Btm, Btm.ap[:, kb * 4:(kb + 1) * 4, :], pt_, pt_.ap.rearrange("p (k c) -> p k c", k=4), AF.Copy)
            self.dve(lambda e: e.tensor_tensor(Xw.ap.rearrange("p (h q) -> p h q", h=64), Xtm.ap.rearrange("p (h q) -> p h q", h=64),
                                               bcl(wtm.ap, 64), ALU.mult), [Xtm, wtm], [Xw])
            for g4 in range(2):
                pC = self.psum()
                for q in range(4):
                    g = g4 * 4 + q
                    self.mm(pC, pC.ap[:, q * 128:(q + 1) * 128], xcb, xcb.ap[:, g, :], xcb, xcb.ap[:, 8 + g, :], True, True)
                self.dve(lambda e, o=CBm.ap[:, g4 * 4:(g4 + 1) * 4, :], p=pC.ap.rearrange("p (k c) -> p k c", k=4):
                         e.tensor_tensor(o, p, bcm(tri_le.ap, 4), ALU.mult), [pC, self.c_all], [CBm])
            def s1(g):
                gsl = slice(g * 512, (g + 1) * 512)
                pY = self.ps[g % 2]
                self.mm(pY, pY.ap, xcb, xcb.ap[:, 8 + g, :], hTb, hTb.ap[:, gsl], True, True)
                yt_ = ytmp[g % 2]
                self.dve(lambda e, o=yt_.ap.rearrange("p (h q) -> p h q", h=8), p=pY.ap.rearrange("p (h q) -> p h q", h=8),
                         b=bcl(ecum.ap[:, g * 8:(g + 1) * 8], 64): e.tensor_tensor(o, p, b, ALU.mult), [pY, ecum], [yt_])
                for hb in range(2):
                    pD = self.ps[2 + hb]
                    for q in range(4):
                        h = g * 8 + hb * 4 + q
                        U = Ut[(g % 2) * 8 + hb * 4 + q]
                        self.dve(lambda e, u=U.ap, sc=da_tm[:, h:h + 1]: e.tensor_scalar(u, tri_gt.ap, sc, None, ALU.mult),
                                 [self.c_all, dtda], [U])
                        self.mm(pD, pD.ap[:, q * 128:(q + 1) * 128], U, U.ap, tri_le, tri_le.ap, True, True)
                    E = Et[(g % 2) * 2 + hb]
                    self.act(E, E.ap, pD, pD.ap, AF.Exp)

            def s2(g):
                gsl = slice(g * 512, (g + 1) * 512)
                yt_ = ytmp[g % 2]
                pI = self.ps[4]
                for hb in range(2):
                    E = Et[(g % 2) * 2 + hb]
                    for q in range(4):
                        h = g * 8 + hb * 4 + q
                        M = Mt[hb * 4 + q]
                        self.dve(lambda e, m=M.ap, ea=E.ap[:, q * 128:(q + 1) * 128], sc=dt_tm[:, h:h + 1], cb=CBm.ap[:, g, :]:
                                 e.scalar_tensor_tensor(m, ea, sc, cb, ALU.mult, ALU.mult), [E, dtda, CBm], [M])
                        co = (hb * 4 + q) * 64
                        self.mm(pI, pI.ap[:, co:co + 64], M, M.ap, Xtm, Xtm.ap[:, h * 64:(h + 1) * 64], True, True)
                ym = ytm[g % 2]
                self.dve(lambda e, o=ym.ap, a=yt_.ap, p=pI.ap: e.tensor_tensor(o, a, p, ALU.add), [yt_, pI], [ym])
                pT = self.ps[5]
                for q in range(4):
                    self.tr(pT, pT.ap[:, q * 128:(q + 1) * 128], ym, ym.ap[:, q * 128:(q + 1) * 128], ident.ap)
                for q in range(4):
                    k = g * 4 + q
                    self.dve(lambda e, o=yTo.ap[:, k, :], x=xcs.ap[:, k, :], dcol=cv[:, 1555 + k:1556 + k], p=pT.ap[:, q * 128:(q + 1) * 128]:
                             e.scalar_tensor_tensor(o, x, dcol, p, ALU.mult, ALU.add), [xcs, self.cvec, pT], [yTo])
                pS = self.ps[6]
                self.mm(pS, pS.ap, Btm, Btm.ap[:, g, :], Xw, Xw.ap[:, gsl], True, True)
                hv = hT.ap[:, gsl].rearrange("p (h q) -> p h q", h=8)
                self.dve(lambda e, hv=hv, b=bcl(edec.ap[:, g * 8:(g + 1) * 8], 64): e.tensor_tensor(hv, hv, b, ALU.mult), [hT, edec], [hT])
                self.dve(lambda e, hg=hT.ap[:, gsl], p=pS.ap: e.tensor_tensor(hg, hg, p, ALU.add), [hT, pS], [hT])
                self.act(hTb, hTb.ap[:, gsl], hT, hT.ap[:, gsl], AF.Copy)
            s1(0)
            for g in range(8):
                if g + 1 < 8:
                    s1(g + 1)
                s2(g)
            self.dma('sp', dr['YT'].rearrange("(k p) t -> p k t", p=128)[:, :, cs], yTo.ap, reads=[yTo], writes=[self.b_yt])
        for kb in range(8):
            pt_ = self.psum()
            for q in range(4):
                k = kb * 4 + q
                self.tr(pt_, pt_.ap[:, q * 128:(q + 1) * 128], hT, hT.ap[:, k * 128:(k + 1) * 128], ident.ap)
            self.act(yTo, yTo.ap[:, kb * 4:(kb + 1) * 4, :], pt_, pt_.ap.rearrange("p (k c) -> p k c", k=4), AF.Copy)
        self.dma('sp', dr['shp'].rearrange("(k p) n -> p k n", p=128), yTo.ap, reads=[yTo], is_out=True)
        self.P.barrier()
        A.off = mark
        ups = A.alloc([128, 48, 4], F32, 'sups')
        tp3 = A.alloc([128, 48, 4], F32, 'stp3')
        xcv = A.alloc([128, 48], F32, 'xcv')
        self.dma('sp', ups.ap[:, :, 0:3], dr['scstT'], writes=[ups])
        self.dve(lambda e: e.tensor_copy(ups.ap[:, :, 3:4], self.xbcs.ap), [self.xbcs], [ups])
        w2 = cv[:, 1363:1555].rearrange("p (k w) -> p k w", k=48)
        self.dve(lambda e: e.tensor_tensor(tp3.ap, ups.ap, w2, ALU.mult), [ups, self.cvec], [tp3])
        self.dve(lambda e: e.tensor_reduce(xcv.ap, tp3.ap, AX.X, ALU.add), [tp3], [xcv])
        self.dve(lambda e: e.tensor_tensor(xcv.ap, xcv.ap, cv[:, 1280:1328], ALU.add), [xcv, self.cvec], [xcv])
        self.act(xcv, xcv.ap, xcv, xcv.ap, AF.Silu)
        self.dma('sp', dr['scsT'], ups.ap[:, :, 1:4], reads=[ups], is_out=True)
        dd = A.alloc([128, 2], F32, 'dd')
        self.act(dd, dd.ap[0:64, 0:1], self.dts, self.dts.ap[0:64, :], AF.Exp, bias=cv[0:64, 1360:1361], reads=[self.cvec])
        self.act(dd, dd.ap[0:64, 0:1], dd, dd.ap[0:64, 0:1], AF.Ln, bias=ones.ap[0:64, 0:1])
        self.dve(lambda e: e.tensor_scalar(dd.ap[0:64, 1:2], dd.ap[0:64, 0:1], a_col, None, ALU.mult), [dd, sv], [dd])
        Gx = A.alloc([64, 4096], F32, 'Gx')
        self.dma('sp', Gx.ap, dr['gexp'], writes=[Gx])
        pE = self.psum()
        for k in range(32):
            self.mm(pE, pE.ap[:, k * 2:(k + 1) * 2], Gx, Gx.ap[:, k * 128:(k + 1) * 128], dd, dd.ap[0:64, 0:2], True, True)
        ex = A.alloc([128, 32, 2], F32, 'ex')
        self.act(ex, ex.ap, pE, pE.ap[:, 0:64].rearrange("p (k c) -> p k c", k=32), AF.Copy)
        edA = A.alloc([128, 32], F32, 'edA')
        dtx = A.alloc([128, 32], F32, 'dtx')
        self.act(edA, edA.ap, ex, ex.ap[:, :, 1], AF.Exp)
        self.dve(lambda e: e.tensor_tensor(dtx.ap, ex.ap[:, :, 0], xcv.ap[:, 0:32], ALU.mult), [ex, xcv], [dtx])
        BC = A.alloc([128, 2, 8, 128], F32, 'BC')
        dg = [A.alloc([128, 128], F32, f'dg{i}') for i in range(2)]
        i_ = 0
        for g in range(8):
            for wh, off in ((0, 32), (1, 40)):
                d_ = dg[i_ % 2]
                i_ += 1
                self.dve(lambda e, d=d_.ap, sc=xcv.ap[:, off + g:off + g + 1]: e.tensor_scalar(d, ident.ap, sc, None, ALU.mult),
                         [self.c_all, xcv], [d_])
                pb = self.psum()
                self.mm(pb, pb.ap[:, 0:128], ones, ones.ap, d_, d_.ap, True, True)
                self.act(BC, BC.ap[:, wh, g, :], pb, pb.ap[:, 0:128], AF.Copy)
        h0 = A.alloc([128, 32, 128], F32, 'h0')
        self.dma('sp', h0.ap, dr['sst'].rearrange("(k p) n -> p k n", p=128), writes=[h0])
        tn = [A.alloc([128, 128], F32, f'tn{i}') for i in range(2)]
        for k in range(32):
            g = k // 4
            hk = h0.ap[:, k, :]
            self.dve(lambda e, hk=hk, sc=edA.ap[:, k:k + 1]: e.tensor_scalar(hk, hk, sc, None, ALU.mult), [h0, edA], [h0])
            self.dve(lambda e, hk=hk, b=BC.ap[:, 0, g, :], sc=dtx.ap[:, k:k + 1]: e.scalar_tensor_tensor(hk, b, sc, hk, ALU.mult, ALU.add),
                     [h0, BC, dtx], [h0])
            t_ = tn[k % 2]
            self.dve(lambda e, t=t_.ap, hk=hk, cc=BC.ap[:, 1, g, :]: e.tensor_tensor(t, hk, cc, ALU.mult), [h0, BC], [t_])
            self.dve(lambda e, o=self.ys.ap[:, k, :], t=t_.ap: e.tensor_reduce(o, t, AX.X, ALU.add), [t_], [self.ys])
        ysa = self.ys.ap[:, :, 0]
        self.dve(lambda e: e.tensor_tensor(dtx.ap, xcv.ap[:, 0:32], cv[:, 1555:1587], ALU.mult), [xcv, self.cvec], [dtx])
        self.dve(lambda e: e.tensor_tensor(ysa, ysa, dtx.ap, ALU.add), [self.ys, dtx], [self.ys])
        self.dma('sp', dr['shs'].rearrange("(k p) n -> p k n", p=128), h0.ap, reads=[h0], is_out=True)

    def conf_in(self, segs, g):
        dr = self.dram
        w = dr['cpw1']
        cv = self.cvec.ap
        for oc in range(DC):
            s1 = self.wslot()
            s2 = self.wslot()
            a1 = s1.ap.rearrange("p (k c) -> p k c", k=16)
            a2 = s2.ap.rearrange("p (k c) -> p k c", k=16)
            self.dma('pool', a1, w[oc], writes=[s1])
            self.dma('pool', a2, w[oc + 16], writes=[s2])
            for si, sg in enumerate(segs):
                W = sg.W
                kd = 'p' if W > 1 else 's'
                pa = self.psum(kd)
                pg = self.psum(kd)
                for k in range(DC):
                    self.mm(pa, pa.ap[:, 0:W], s1, a1[:, k, :], sg.xb, sg.xb.ap[:, k, 0:W], k == 0, k == DC - 1)
                for k in range(DC):
                    self.mm(pg, pg.ap[:, 0:W], s2, a2[:, k, :], sg.xb, sg.xb.ap[:, k, 0:W], k == 0, k == DC - 1)
                sgm = self.tmp(kd)
                sa = sgm.ap[:, 0:W]
                self.act(sgm, sa, pg, pg.ap[:, 0:W], AF.Sigmoid, bias=cv[:, 16 + oc:17 + oc], reads=[self.cvec])
                paa = pa.ap[:, 0:W]
                ba = cv[:, oc:oc + 1]
                if si == 0:
                    st = self.stage(F32)
                    self.dve(lambda e, o=st.ap, paa=paa, ba=ba, sa=sa: e.scalar_tensor_tensor(o, paa, ba, sa, ALU.add, ALU.mult),
                             [pa, sgm, self.cvec], [st])
                    self.dma('sp', dr['UT'][oc * 128:(oc + 1) * 128, g * TG:(g + 1) * TG], st.ap, reads=[st], writes=[self.b_ut])
                else:
                    self.dve(lambda e, o=self.us.ap[:, oc, :], paa=paa, ba=ba, sa=sa: e.scalar_tensor_tensor(o, paa, ba, sa, ALU.add, ALU.mult),
                             [pa, sgm, self.cvec], [self.us])

    def conf_core(self):
        A = self.A
        dr = self.dram
        cv = self.cvec.ap
        upad = [A.alloc([128, 30 + T_], F32, f'upad{i}') for i in range(2)]
        acc = [A.alloc([128, T_], F32, f'acc{i}') for i in range(2)]
        for k in range(DC):
            up, ac = upad[k % 2], acc[k % 2]
            self.dve(lambda e, a=up.ap[:, 0:30]: e.memset(a, 0.0), [], [up])
            self.dma('sp', up.ap[:, 30:30 + T_], dr['UT'][k * 128:(k + 1) * 128, :], reads=[self.b_ut], writes=[up])
            wcol = lambda w: cv[:, 96 + w * 16 + k:97 + w * 16 + k]
            self.dve(lambda e, o=ac.ap, i=up.ap[:, 0:T_], w0=wcol(0), b=cv[:, 32 + k:33 + k]: e.tensor_scalar(o, i, w0, b, ALU.mult, ALU.add),
                     [up, self.cvec], [ac])
            for w in range(1, 31):
                self.dve(lambda e, o=ac.ap, i=up.ap[:, w:w + T_], ww=wcol(w): e.scalar_tensor_tensor(o, i, ww, o, ALU.mult, ALU.add),
                         [up, ac, self.cvec], [ac])
            self.dma('sp', dr['CT'][k * 128:(k + 1) * 128, :], ac.ap, reads=[ac], writes=[self.b_ct])
        self.dma('sp', dr['ccT'], dr['UT'][:, T_ - 30:T_], reads=[self.b_ut], is_out=True)
        ups = A.alloc([128, DC, 31], F32, 'ups')
        tp3 = A.alloc([128, DC, 31], F32, 'tp3')
        self.dma('sp', ups.ap[:, :, 0:30], dr['cstT'], writes=[ups])
        self.dve(lambda e: e.tensor_copy(ups.ap[:, :, 30:31], self.us.ap), [self.us], [ups])
        w2 = cv[:, 592:592 + 496].rearrange("p (k w) -> p k w", k=DC)
        self.dve(lambda e: e.tensor_tensor(tp3.ap, ups.ap, w2, ALU.mult), [ups, self.cvec], [tp3])
        csa = self.cs.ap[:, :, 0]
        self.dve(lambda e: e.tensor_reduce(csa, tp3.ap, AX.X, ALU.add), [tp3], [self.cs])
        self.dve(lambda e: e.tensor_tensor(csa, csa, cv[:, 32:48], ALU.add), [self.cs, self.cvec], [self.cs])
        self.dma('sp', dr['ccsT'], ups.ap[:, :, 1:31], reads=[ups], is_out=True)

    def mixer_core(self, l):
        if l == 1:
            self.ssm_core()
        if l == 2:
            self.conf_core()
        if l in (0, 3):
            kind = 'sb' if l == 0 else 'mb'
            mark = self.A.off
            self.attn_prompt(kind)
            self.P.barrier()
            self.A.off = mark
            if not self.cfg.get('no_sample_attn'):
                self.attn_sample(kind)

    def attn_sample(self, kind):
        A = self.A
        dr = self.dram
        ident = self.c_ident
        ones = self.c_ones
        c2d = dr['consts2']
        c2 = A.alloc([128, 2048], F32, 'c2')
        self.dma('sp', c2.ap, c2d, writes=[c2])
        Mx = c2.ap[:, 0:128]
        Hsame = c2.ap[:, 128:256]
        bmask = c2.ap[0:16, 256:260]
        Lall = c2.ap[0:16, 512:1536]
        kc_d = dr[kind + 'kc']
        vc_d = dr[kind + 'vc']
        ptd = dr['ptab']
        ptb = A.alloc([128, 128], I32, 'ptb')
        idx = A.alloc([128, 128], I32, 'idx')
        self.dma('sp', ptb.ap, bass.AP(ptd.tensor, 0, [[0, 128], [1, 128]]), writes=[ptb])
        siota = self.c_all.ap[:, 257:258]
        self.dve(lambda e: e.tensor_scalar(idx.ap, ptb.ap, 128.0, siota, ALU.mult, ALU.add), [ptb, self.c_all], [idx])
        qbc = A.alloc([128, 16, 128], F32, 'qbc')
        dg = [A.alloc([128, 128], F32, f'adg{i}') for i in range(2)]
        for hb in range(4):
            pb = self.psum()
            for q in range(4):
                h = hb * 4 + q
                d_ = dg[h % 2]
                self.dve(lambda e, d=d_.ap, sc=self.qs.ap[:, h, :]: e.tensor_scalar(d, ident.ap, sc, None, ALU.mult), [self.c_all, self.qs], [d_])
                self.mm(pb, pb.ap[:, q * 128:(q + 1) * 128], ones, ones.ap, d_, d_.ap, True, True)
            self.act(qbc, qbc.ap[:, hb * 4:(hb + 1) * 4, :], pb, pb.ap.rearrange("p (k c) -> p k c", k=4), AF.Copy)
        KP = [A.alloc([128, 512], F32, f'KP{i}') for i in range(4)]
        prod = [A.alloc([128, 16, 128], F32, f'prod{i}') for i in range(2)]
        zT2 = A.alloc([128, 16, 8, 16], F32, 'zT2')
        q4 = qbc.ap.rearrange("p (a b) d -> p a b d", a=4)

        def gather(dst, src_d, i):
            ia = idx.ap[:, i:i + 1]
            return self.P.op('pool', lambda e: e.indirect_dma_start(out=dst.ap, out_offset=None, in_=src_d,
                                                                    in_offset=bass.IndirectOffsetOnAxis(ap=ia, axis=0)),
                             reads=[idx], writes=[dst], dma=True)
        def mulstep(i):
            kp = KP[i % 4]
            gather(kp, kc_d, i)
            pr = prod[i % 2]
            ka = kp.ap
            kl = [list(x) for x in ka.ap]
            k4 = bass.AP(ka.tensor, ka.offset, [kl[0], [128, 4], [0, 4], [1, 128]])
            self.dve(lambda e, o=pr.ap.rearrange("p (a b) d -> p a b d", a=4), k4=k4: e.tensor_tensor(o, k4, q4, ALU.mult), [kp, qbc], [pr])

        def redstep(i):
            g8, pl = divmod(i, 16)
            pr = prod[i % 2]
            self.dve(lambda e, o=zT2.ap[:, pl, g8, :], p=pr.ap: e.tensor_reduce(o, p, AX.X, ALU.add), [pr], [zT2])
        mulstep(0)
        for i in range(128):
            if i + 1 < 128:
                mulstep(i + 1)
            redstep(i)
        Z = A.alloc([128, 2048], F32, 'sZ')
        E = A.alloc([128, 2048], F32, 'sE')
        ZS = A.alloc([128, 2048], F32, 'sZS')
        for pb4 in range(4):
            pt_ = self.psum()
            for q in range(4):
                pl = pb4 * 4 + q
                self.tr(pt_, pt_.ap[:, q * 128:(q + 1) * 128], zT2, zT2.ap[:, pl].rearrange("p a b -> p (a b)"), ident.ap)
            self.act(Z, Z.ap[:, pb4 * 512:(pb4 + 1) * 512], pt_, pt_.ap, AF.Copy)
        sm = A.alloc([128, 64], F32, 'ssm')
        if kind == 'sb':
            C = A.alloc([128, 2048], F32, 'sC')
            o2k = A.alloc([128, 2048], F32, 'so2k')
            self.dve(lambda e: e.memset(o2k.ap, 1.0), [], [o2k])
            self.act(E, E.ap, Z, Z.ap, AF.Exp, scale=SCALE)
            self.act(E, E.ap, E, E.ap, AF.Ln, bias=ones.ap[:, 0:1])
            self.dve(lambda e: e.scalar_tensor_tensor(ZS.ap, Z.ap, SCALE, E.ap, ALU.mult, ALU.subtract), [Z, E], [ZS])
            self.dve(lambda e: e.tensor_tensor_scan(C.ap, o2k.ap, E.ap, 0.0, ALU.mult, ALU.add), [E, o2k], [C])
            tot = sm.ap[:, 0:1]
            self.dve(lambda e: e.tensor_copy(tot, C.ap[:, 2047:2048]), [C], [sm])
            pc = self.psum()
            self.mm(pc, pc.ap[:, 0:1], c2, Mx, sm, tot, True, True)
            nb = sm.ap[:, 1:2]
            self.dve(lambda e: e.scalar_tensor_tensor(nb, pc.ap[:, 0:1], -1.0, tot, ALU.mult, ALU.subtract), [pc, sm], [sm])
            self.dve(lambda e: e.tensor_tensor(ZS.ap, ZS.ap, C.ap, ALU.add), [ZS, C], [ZS])
            Pm = Z
            self.act(Pm, Pm.ap, ZS, ZS.ap, AF.Exp, bias=nb, reads=[sm])
        else:
            grow = sm.ap[:, 0:8]
            self.dve(lambda e: e.tensor_reduce(grow, Z.ap.rearrange("p (b s) -> p b s", b=8), AX.X, ALU.add), [Z], [sm])
            pg = self.psum()
            for g8 in range(8):
                self.mm(pg, pg.ap[0:16, g8 * 8:(g8 + 1) * 8], self.c_all, ident.ap[:, g8 * 16:(g8 + 1) * 16], sm, grow, True, True)
            s16 = A.alloc([16, 160], F32, 's16')
            g16 = s16.ap[:, 0:64]
            self.act(s16, g16, pg, pg.ap[0:16, 0:64], AF.Copy)
            t8 = s16.ap[:, 64:72]
            self.dve(lambda e: e.max(t8, g16), [s16], [s16])
            sb16 = s16.ap[:, 80:144]
            self.dve(lambda e: e.tensor_scalar(sb16, g16, t8[:, 2:3], -NEG, ALU.is_ge, ALU.mult), [s16], [s16])
            self.dve(lambda e: e.tensor_scalar(sb16, sb16, NEG, None, ALU.add), [s16], [s16])
            psb = self.psum()
            for g8 in range(8):
                self.mm(psb, psb.ap[:, 0:8], c2, Lall[:, g8 * 128:(g8 + 1) * 128], s16, sb16[:, g8 * 8:(g8 + 1) * 8], g8 == 0, g8 == 7)
            sbr = sm.ap[:, 8:16]
            self.act(sm, sbr, psb, psb.ap[:, 0:8], AF.Copy)
            for bl in range(8):
                self.dve(lambda e, o=E.ap[:, bl * 256:(bl + 1) * 256], z=Z.ap[:, bl * 256:(bl + 1) * 256], b=sbr[:, bl:bl + 1]:
                         e.tensor_scalar(o, z, SCALE, b, ALU.mult, ALU.add), [Z, sm], [E])
            p16 = A.alloc([128, 16], F32, 'p16')
            ksa = self.kvs.ap[:, 0:4, 0]
            kl2 = [list(x) for x in ksa.ap]
            k44 = bass.AP(ksa.tensor, ksa.offset, [kl2[0], kl2[1], [0, 4]])
            self.dve(lambda e: e.tensor_tensor(p16.ap.rearrange("p (a b) -> p a b", a=4), self.qs.ap[:, :, 0].rearrange("p (a b) -> p a b", a=4), k44, ALU.mult),
                     [self.qs, self.kvs], [p16])
            prep = A.alloc([128, 8, 16], F32, 'prep')
            pl_ = [list(x) for x in p16.ap.ap]
            p16b = bass.AP(p16.ap.tensor, p16.ap.offset, [pl_[0], [0, 8], pl_[1]])
            self.dve(lambda e: e.tensor_copy(prep.ap, p16b), [p16], [prep])
            pz = self.psum()
            self.mm(pz, pz.ap[:, 0:1], prep, prep.ap.rearrange("p a b -> p (a b)"), self.c_all, ones.ap[:, 0:1], True, True)
            negm = sm.ap[:, 16:17]
            self.dve(lambda e: e.tensor_scalar(negm, pz.ap[:, 0:1], -SCALE, None, ALU.mult), [pz], [sm])
            self.act(ZS, ZS.ap, E, E.ap, AF.Exp, bias=negm, reads=[sm])
            rs = sm.ap[:, 17:18]
            self.dve(lambda e: e.tensor_reduce(rs, ZS.ap, AX.X, ALU.add), [ZS], [sm])
            pd = self.psum()
            self.mm(pd, pd.ap[:, 0:1], c2, Hsame, sm, rs, True, True)
            rden = sm.ap[:, 18:19]
            self.dve(lambda e: e.tensor_scalar(rden, pd.ap[:, 0:1], 1.0, None, ALU.add), [pd], [sm])
            self.dve(lambda e: e.reciprocal(rden, rden), [sm], [sm])
            Pm = Z
            self.dve(lambda e: e.tensor_scalar(Pm.ap, ZS.ap, rden, None, ALU.mult), [ZS, sm], [Pm])
            Rm = dg[0]
            self.dve(lambda e: e.tensor_scalar(Rm.ap, ones.ap, rden, None, ALU.mult), [self.c_all, sm], [Rm])
            pbc = self.psum()
            self.mm(pbc, pbc.ap[:, 0:16], Rm, Rm.ap, self.c_all, ident.ap[:, 0:16], True, True)
            pown = A.alloc([128, 16], F32, 'pown')
            self.act(pown, pown.ap, pbc, pbc.ap[:, 0:16], AF.Copy)
            diagV = A.alloc([128, 4, 128], F32, 'diagV')
            for kvh in range(4):
                self.dve(lambda e, o=diagV.ap[:, kvh, :], sc=self.kvs.ap[:, 4 + kvh, :]: e.tensor_scalar(o, ident.ap, sc, None, ALU.mult),
                         [self.c_all, self.kvs], [diagV])
        PT = A.alloc([128, 16, 128], F32, 'sPT')
        for pb4 in range(4):
            pt_ = self.psum()
            for q in range(4):
                pl = pb4 * 4 + q
                self.tr(pt_, pt_.ap[:, q * 128:(q + 1) * 128], Pm, Pm.ap[:, pl * 128:(pl + 1) * 128], ident.ap)
            self.act(PT, PT.ap[:, pb4 * 4:(pb4 + 1) * 4, :], pt_, pt_.ap.rearrange("p (k c) -> p k c", k=4), AF.Copy)
        po = self.ps[7]
        for i in range(128):
            g8, pl = divmod(i, 16)
            vp = KP[i % 4]
            gather(vp, vc_d, i)
            self.mm(po, po.ap[0:16, :], PT, PT.ap[:, pl, g8 * 16:(g8 + 1) * 16], vp, vp.ap, i == 0, (i == 127 and kind == 'sb'))
        if kind == 'mb':
            self.mm(po, po.ap[0:16, :], pown, pown.ap, diagV, diagV.ap.rearrange("p a b -> p (a b)"), False, True)
        t16 = A.alloc([16, 4, 128], F32, 't16')
        r16 = A.alloc([16, 128], F32, 'r16')
        bl_ = [list(x) for x in bmask.ap]
        bm3 = bass.AP(bmask.tensor, bmask.offset, [bl_[0], bl_[1], [0, 128]])
        self.dve(lambda e: e.tensor_tensor(t16.ap, po.ap[0:16, :].rearrange("p (k d) -> p k d", k=4), bm3, ALU.mult), [po, c2], [t16])
        self.dve(lambda e: e.tensor_reduce(r16.ap, t16.ap.rearrange("p k d -> p d k"), AX.X, ALU.add), [t16], [r16])
        pf = self.psum()
        self.tr(pf, pf.ap[:, 0:16], r16, r16.ap, ident.ap[0:16, 0:16])
        self.act(self.aos, self.aos.ap[:, 0:16, 0], pf, pf.ap[:, 0:16], AF.Copy)

    def attn_prompt(self, kind):
        A = self.A
        dr = self.dram
        pfx = kind
        KTs = [A.alloc([128, T_], BF16, f'KT{i}') for i in range(2)]
        VTs = [A.alloc([128, T_], BF16, f'VT{i}') for i in range(2)]
        Vs = [A.alloc([128, 16, 128], BF16, f'V{i}') for i in range(2)]
        QTh = [A.alloc([128, T_], BF16, f'QTh{i}') for i in range(2)]
        AOh = [A.alloc([128, T_], BF16, f'AOh{i}') for i in range(2)]
        NS = 3
        WE = [A.alloc([128, T_], F32, f'WE{i}') for i in range(NS)]
        WZ = [A.alloc([128, T_], F32, f'WZ{i}') for i in range(NS)]
        WP = [A.alloc([128, T_], BF16, f'WP{i}') for i in range(NS)]
        PTs = [A.alloc([128, 512], BF16, f'PTs{i}') for i in range(3)]
        sm = [A.alloc([128, 32], F32, f'sm{i}') for i in range(NS)]
        if kind == 'sb':
            WC = [A.alloc([128, T_], F32, f'WC{i}') for i in range(2)]
            ones2k = A.alloc([128, T_], F32, 'ones2k')
            self.dve(lambda e: e.memset(ones2k.ap, 1.0), [], [ones2k])
        else:
            KMs = [A.alloc([128, 8], F32, f'KM{i}') for i in range(2)]
            KMbs = [A.alloc([128, 8], BF16, f'KMb{i}') for i in range(2)]
        identb = self.c_identb
        st = {'pv_rr': 0}

        def kv_prologue(kvh):
            KT, VT, V = KTs[kvh % 2], VTs[kvh % 2], Vs[kvh % 2]
            self.dma('pool', KT.ap, dr[pfx + 'kT'][kvh * 128:(kvh + 1) * 128, :], reads=[self.b_kv], writes=[KT])
            self.dma('pool', VT.ap, dr[pfx + 'vT'][kvh * 128:(kvh + 1) * 128, :], reads=[self.b_kv], writes=[VT])
            for jb in range(4):
                pst = self.ps[4 + (jb % 2)]
                psb = pst.ap.bitcast(BF16)
                for q in range(4):
                    j = jb * 4 + q
                    self.tr(pst, psb[:, q * 128:(q + 1) * 128], VT, VT.ap[:, j * 128:(j + 1) * 128], identb.ap)
                va = V.ap[:, jb * 4:(jb + 1) * 4, :]
                self.act(V, va, pst, psb[:, 0:512].rearrange("p (k c) -> p k c", k=4), AF.Copy)
            if kind == 'mb':
                KM, KMb = KMs[kvh % 2], KMbs[kvh % 2]
                self.dve(lambda e: e.tensor_reduce(KM.ap, KT.ap.rearrange("p (b s) -> p b s", b=8), AX.X, ALU.add), [KT], [KM])
                self.dve(lambda e: e.tensor_scalar(KMb.ap, KM.ap, 1.0 / 256, None, ALU.mult), [KM], [KMb])

        def stage_a1(it, kvh, hq, tt):
            h = kvh * 4 + hq
            KT = KTs[kvh % 2]
            Q = QTh[h % 2]
            if tt == 0:
                if hq == 0:
                    kv_prologue(kvh)
                self.dma('sp', Q.ap, dr['QT'][h], reads=[self.b_qt], writes=[Q])
            L = (tt + 1) * 128
            nb = (L + 511) // 512
            E, Z, s_ = WE[it % NS], WZ[it % NS], sm[it % NS]
            qa = Q.ap[:, tt * 128:(tt + 1) * 128]
            bmat = self.c_blt_b if kind == 'sb' else self.c_ble_b
            for j in range(nb):
                cols = min(512, L - j * 512)
                last = (j == nb - 1)
                self.mm(self.ps[j], self.ps[j].ap[:, 0:cols], Q, qa, KT, KT.ap[:, j * 512:j * 512 + cols], True, not last)
                if last:
                    self.mm(self.ps[j], self.ps[j].ap[:, cols - 128:cols], identb, identb.ap, bmat, bmat.ap, False, True)
            if kind == 'sb':
                for j in range(nb):
                    cols = min(512, L - j * 512)
                    sl = slice(j * 512, j * 512 + cols)
                    self.act(E, E.ap[:, sl], self.ps[j], self.ps[j].ap[:, 0:cols], AF.Exp, scale=SCALE)
                for j in range(nb):
                    cols = min(512, L - j * 512)
                    sl = slice(j * 512, j * 512 + cols)
                    self.act(Z, Z.ap[:, sl], self.ps[j], self.ps[j].ap[:, 0:cols], AF.Copy, scale=SCALE)
                self.act(E, E.ap[:, 0:L], E, E.ap[:, 0:L], AF.Ln, bias=self.c_ones.ap[:, 0:1])
            else:
                n = tt // 2
                selb = s_.ap[:, 8:16]
                if n >= 4:
                    KMb = KMbs[kvh % 2]
                    gp = self.ps[5]
                    self.mm(gp, gp.ap[:, 0:8], Q, qa, KMb, KMb.ap, True, True)
                    gt = s_.ap[:, 0:8]
                    self.dve(lambda e, gt=gt: e.memset(gt, -1e30), [], [s_])
                    self.dve(lambda e, gt=gt, n=n, gp=gp: e.tensor_copy(gt[:, 0:n], gp.ap[:, 0:n]), [gp], [s_])
                    t8 = s_.ap[:, 16:24]
                    self.dve(lambda e, t8=t8, gt=gt: e.max(t8, gt), [s_], [s_])
                    self.dve(lambda e, selb=selb, gt=gt, t8=t8: e.tensor_scalar(selb, gt, t8[:, 2:3], -NEG, ALU.is_ge, ALU.mult), [s_], [s_])
                    self.dve(lambda e, selb=selb: e.tensor_scalar(selb, selb, NEG, None, ALU.add), [s_], [s_])
                else:
                    self.dve(lambda e, selb=selb: e.memset(selb, 0.0), [], [s_])
                for blk in range(n + 1):
                    j = blk // 2
                    c0 = (blk % 2) * 256
                    pst = self.ps[j]
                    if blk < n:
                        self.dve(lambda e, z=Z.ap[:, blk * 256:(blk + 1) * 256], p=pst.ap[:, c0:c0 + 256], b=selb[:, blk:blk + 1]:
                                 e.tensor_scalar(z, p, SCALE, b, ALU.mult, ALU.add), [pst, s_], [Z])
                    else:
                        wd = L - n * 256
                        self.dve(lambda e, z=Z.ap[:, blk * 256:blk * 256 + wd], p=pst.ap[:, c0:c0 + wd]:
                                 e.tensor_scalar(z, p, SCALE, None, ALU.mult), [pst], [Z])

        def stage_a2(it, kvh, hq, tt):
            L = (tt + 1) * 128
            E, Z, Pb, s_ = WE[it % NS], WZ[it % NS], WP[it % NS], sm[it % NS]
            if kind == 'sb':
                C = WC[it % 2]
                self.dve(lambda e, c=C.ap[:, 0:L], o1=ones2k.ap[:, 0:L], sp=E.ap[:, 0:L]:
                         e.tensor_tensor_scan(c, o1, sp, 0.0, ALU.mult, ALU.subtract), [E, ones2k], [C])
                self.dve(lambda e, z=Z.ap[:, 0:L], sp=E.ap[:, 0:L]: e.tensor_tensor(z, z, sp, ALU.subtract), [Z, E], [Z])
                self.dve(lambda e, z=Z.ap[:, 0:L], c=C.ap[:, 0:L]: e.tensor_tensor(z, z, c, ALU.subtract), [Z, C], [Z])
                self.act(Pb, Pb.ap[:, 0:L], Z, Z.ap[:, 0:L], AF.Exp, bias=C.ap[:, L - 1:L], reads=[C])
            else:
                nm = s_.ap[:, 24:25]
                self.dve(lambda e, nm=nm, z=Z.ap[:, 0:L]: e.tensor_reduce(nm, z, AX.X, ALU.max, negate=True), [Z], [s_])
                self.act(E, E.ap[:, 0:L], Z, Z.ap[:, 0:L], AF.Exp, bias=nm, reads=[s_])
                dn = s_.ap[:, 25:26]
                self.dve(lambda e, dn=dn, ea=E.ap[:, 0:L]: e.tensor_reduce(dn, ea, AX.X, ALU.add), [E], [s_])
                self.dve(lambda e, dn=dn: e.reciprocal(dn, dn), [s_], [s_])
                self.dve(lambda e, dn=dn, pb=Pb.ap[:, 0:L], ea=E.ap[:, 0:L]: e.tensor_scalar(pb, ea, dn, None, ALU.mult), [E, s_], [Pb])

        def stage_b(it, kvh, hq, tt):
            h = kvh * 4 + hq
            V = Vs[kvh % 2]
            Pb = WP[it % NS]
            AOt = AOh[h % 2]
            aops = self.ps[6 + (it % 2)]
            for jb in range((tt + 4) // 4):
                nq = min(4, tt + 1 - jb * 4)
                pv_rr = st['pv_rr']
                pst = self.ps[4 + (pv_rr % 2)]
                psb = pst.ap.bitcast(BF16)
                pts = PTs[pv_rr % 3]
                for q in range(nq):
                    j = jb * 4 + q
                    self.tr(pst, psb[:, q * 128:(q + 1) * 128], Pb, Pb.ap[:, j * 128:(j + 1) * 128], identb.ap)
                if pv_rr % 2 == 0:
                    self.act(pts, pts.ap[:, 0:nq * 128], pst, psb[:, 0:nq * 128], AF.Copy)
                else:
                    self.dve(lambda e, o=pts.ap[:, 0:nq * 128], i=psb[:, 0:nq * 128]: e.tensor_copy(o, i), [pst], [pts])
                for q in range(nq):
                    j = jb * 4 + q
                    self.mm(aops, aops.ap[:, 0:128], V, V.ap[:, j, :], pts, pts.ap[:, q * 128:(q + 1) * 128], j == 0, j == tt)
                st['pv_rr'] += 1
            self.act(AOt, AOt.ap[:, tt * 128:(tt + 1) * 128], aops, aops.ap[:, 0:128], AF.Copy)
            if tt == 15:
                self.dma('sp', dr['AO'][h], AOt.ap, reads=[AOt], writes=[self.b_ao])

        iters = [(kvh, hq, tt) for kvh in range(4) for hq in range(4) for tt in range(16)]
        N = len(iters)
        for i in range(N + 2):
            if i < N:
                stage_a1(i, *iters[i])
            if 0 <= i - 1 < N:
                stage_a2(i - 1, *iters[i - 1])
            if 0 <= i - 2 < N:
                stage_b(i - 2, *iters[i - 2])


def tile_w(W):
    K_, N_ = W.shape
    return np.ascontiguousarray(W.reshape(K_ // 128, 128, N_ // 128, 128).transpose(2, 1, 0, 3))


def make_consts():
    c = np.zeros((128, 1024), np.float32)
    c[:, 0:128] = 1.0
    c[:, 128:256] = np.eye(128, dtype=np.float32)
    c[:, 256] = EPS
    c[:, 257] = np.arange(128)
    t = np.arange(128)[:, None]
    s = np.arange(128)[None, :]
    c[:, 384:512] = (s < t).astype(np.float32)
    c[:, 512:640] = np.where(s <= t, 0.0, NEG).astype(np.float32)
    c[:, 640:768] = (t <= s).astype(np.float32)
    c[:, 768:896] = (t > s).astype(np.float32)
    c[:, 896:1024] = np.where(s < t, 0.0, NEG).astype(np.float32)
    return c


def pk(v):
    return np.ascontiguousarray(np.asarray(v).reshape(-1, 128).T)


def make_consts2():
    c = np.zeros((128, 2048), np.float32)
    r = np.arange(128)
    g, h = r // 16, r % 16
    c[:, 0:128] = ((h[:, None] == h[None, :]) & (g[:, None] > g[None, :])).astype(np.float32)
    c[:, 128:256] = (h[:, None] == h[None, :]).astype(np.float32)
    c[0:16, 256:260] = (np.arange(16)[:, None] // 4 == np.arange(4)[None, :]).astype(np.float32)
    for g8 in range(8):
        for hh in range(16):
            c[hh, 512 + g8 * 128 + g8 * 16 + hh] = 1.0
    return c


def make_cvec(inp):
    c = np.zeros((128, 2048), np.float32)
    c[:, 0:32] = pk(inp['conf_b_pw1'][0])
    c[:, 32:48] = pk(inp['conf_b_dw'][0])
    c[:, 48:64] = pk(inp['conf_ln_g'][0])
    c[:, 64:80] = pk(inp['conf_ln_b'][0])
    c[:, 80:96] = pk(inp['conf_b_pw2'][0])
    wdw = inp['conf_w_dw'][0]
    w3 = wdw.reshape(31, 16, 128).transpose(2, 0, 1)
    c[:, 96:592] = w3.reshape(128, 496)
    c[:, 592:1088] = w3.transpose(0, 2, 1).reshape(128, 496)
    cw = inp['ssm_conv_w'][0].reshape(4, 48, 128).transpose(2, 0, 1)
    c[:, 1088:1280] = cw.reshape(128, 192)
    c[:, 1280:1328] = pk(inp['ssm_conv_b'][0])
    c[:, 1328:1360] = pk(inp['ssm_norm_g'][0])
    c[0:64, 1360] = inp['ssm_dt_bias'][0]
    c[0:64, 1361] = inp['ssm_a_log'][0]
    c[0:64, 1362] = inp['ssm_d'][0]
    c[:, 1363:1555] = cw.transpose(0, 2, 1).reshape(128, 192)
    c[:, 1555:1587] = pk(np.repeat(inp['ssm_d'][0], 64))
    return c


def shared_builders(inp):
    B = {}
    B['consts'] = make_consts
    B['lng'] = lambda: np.ascontiguousarray(inp['ln_g'].reshape(16, 16, 128).transpose(2, 0, 1).reshape(128, 256))
    B['lnb'] = lambda: np.ascontiguousarray(inp['ln_b'].reshape(16, 16, 128).transpose(2, 0, 1).reshape(128, 256))
    for nm, src in (('w1', 'ffn_w1'), ('w3', 'ffn_w3'), ('w2', 'ffn_w2')):
        B[nm] = lambda src=src: np.stack([np.stack([tile_w(inp[src][l, j]) for j in range(2)]) for l in range(NL)])
    B['wgate'] = lambda: np.stack([tile_w(inp['ple_w_gate'][l]) for l in range(NL)])
    B['wproj'] = lambda: np.stack([tile_w(inp['ple_w_proj'][l]) for l in range(NL)])
    B['sbqkv'] = lambda: tile_w(inp['sb_w_qkv'][0])
    B['sbo'] = lambda: tile_w(inp['sb_w_o'][0])
    B['mbqkv'] = lambda: tile_w(inp['moba_w_qkv'][0])
    B['mbo'] = lambda: tile_w(inp['moba_w_o'][0])
    B['cpw1'] = lambda: tile_w(inp['conf_w_pw1'][0])
    B['cpw2'] = lambda: tile_w(inp['conf_w_pw2'][0])
    B['cvec'] = lambda: make_cvec(inp)
    B['consts2'] = make_consts2
    B['sbkc'] = lambda: inp['cache_sb_k'][0].reshape(1280 * 128, 512)
    B['sbvc'] = lambda: inp['cache_sb_v'][0].reshape(1280 * 128, 512)
    B['mbkc'] = lambda: inp['cache_moba_k'][0].reshape(1280 * 128, 512)
    B['mbvc'] = lambda: inp['cache_moba_v'][0].reshape(1280 * 128, 512)
    B['ssmin'] = lambda: tile_w(np.concatenate([inp['ssm_w_in'][0], np.zeros((D, 64), np.float32)], axis=1))
    B['ssmout'] = lambda: tile_w(inp['ssm_w_out'][0])
    B['gexp'] = lambda: (np.arange(64)[:, None] == (np.arange(4096)[None, :] // 64)).astype(np.float32)
    return B


def prep_shared(inp):
    return {k: f() for k, f in shared_builders(inp).items()}


def prep_core(inp, c):
    b = c // 2
    m = {}
    m['xT'] = np.ascontiguousarray(inp['x_prompt'][b].T)
    m['xsT'] = np.ascontiguousarray(inp['x_sample'][c, 0].reshape(DC, 128).T)
    m['pT'] = np.ascontiguousarray(inp['p_prompt'][:, b].transpose(0, 2, 1))
    m['psT'] = np.ascontiguousarray(inp['p_sample'][:, c, 0].reshape(NL, 2, 128).transpose(0, 2, 1))
    m['ptab'] = np.ascontiguousarray(inp['page_table'][c:c + 1].astype(np.int32))
    m['sst'] = np.ascontiguousarray(inp['state_ssm'][0, c].reshape(4096, 128))
    m['scstT'] = np.ascontiguousarray(inp['state_ssm_conv'][0, c].T.reshape(48, 128, 3).transpose(1, 0, 2))
    m['cstT'] = np.ascontiguousarray(inp['state_conf_conv'][0, c].T.reshape(DC, 128, 30).transpose(1, 0, 2))
    return m


_CACHE = {}


def get_nc(cfg):
    key = tuple(sorted(cfg.items()))
    if key not in _CACHE:
        nc = bass.Bass("TRN2", target_bir_lowering=False)
        k = K(nc, cfg)
        k.build()
        _CACHE[key] = (nc, k)
    return _CACHE[key]


def run(inp, cfg, cores=8):
    nc, k = get_nc(cfg)
    sh = prep_shared(inp)
    names = [n for n, t in k.dram.items()]
    in_maps = []
    for c in range(cores):
        m = dict(sh)
        m.update(prep_core(inp, c))
        in_maps.append(m)
    res = run_bass_kernel_spmd(nc, in_maps, core_ids=list(range(cores)))
    return res.results


def kernel(**inp):
    inp = {k: np.asarray(v) for k, v in inp.items()}
    r = run(inp, {})
    f32 = np.float32
    yp = np.stack([r[2 * b]['yT'].T for b in range(4)]).astype(f32)
    ys = np.stack([r[c]['ysT'].T.reshape(1, D) for c in range(8)]).astype(f32)

    def kvp(nm):
        return np.stack([r[2 * b][nm].T.reshape(T_, 4, 128) for b in range(4)])[None].astype(f32)

    def kvs(nm):
        return np.stack([r[c][nm].T.reshape(1, 4, 128) for c in range(8)])[None].astype(f32)
    shp = np.stack([r[2 * b]['shp'].reshape(64, 64, 128) for b in range(4)])[None].astype(f32)
    shs = np.stack([r[c]['shs'].reshape(64, 64, 128) for c in range(8)])[None].astype(f32)
    scp = np.stack([r[2 * b]['scT'].T for b in range(4)])[None].astype(f32)
    scs = np.stack([r[c]['scsT'].transpose(1, 0, 2).reshape(6144, 3).T for c in range(8)])[None].astype(f32)
    ccp = np.stack([r[2 * b]['ccT'].T for b in range(4)])[None].astype(f32)
    ccs = np.stack([r[c]['ccsT'].transpose(1, 0, 2).reshape(D, 30).T for c in range(8)])[None].astype(f32)
    return (yp, ys, kvp('sbkT'), kvp('sbvT'), kvs('sbksT'), kvs('sbvsT'), shp, shs, scp, scs, ccp, ccs,
            kvp('mbkT'), kvp('mbvT'), kvs('mbksT'), kvs('mbvsT'))
```
